# Optimizing a Trainium2 kernel written in Bass

```python
import math
import jax, jax.numpy as jnp
from jax import lax
import numpy as np

D_MODEL = 2048
BATCH = 32
SEQ = 256
DEPTH = 2
DEC_BATCH = 2
DEC_SEQ = 2048
PAST_LEN = 512

GRID_W = 64
N_EVEN = (DEPTH + 1) // 2
N_ODD = DEPTH // 2
CONV_CH = D_MODEL // 2
CONV_W = 31
SSM_CH = D_MODEL // 2
SSM_P = 16
SSM_G = SSM_CH // SSM_P
SSM_N = 64
ATT_H = 16
ATT_DK = 64
ATT_DV = 2 * ATT_DK
D_FF = 4 * D_MODEL
Q_BLOCK = 128
ALPHA = (2 * DEPTH) ** 0.25
BETA = (8 * DEPTH) ** -0.25
LN_EPS = 1e-5
ROPE_BASE = 10000.0

kernel_name = "hybrid_diffusion_conv_s5_diffattn_step"


def layer_norm(x, g, b):
    x32 = x.astype(jnp.float32)
    mu = jnp.mean(x32, axis=-1, keepdims=True)
    var = jnp.mean(jnp.square(x32 - mu), axis=-1, keepdims=True)
    y = (x32 - mu) * lax.rsqrt(var + LN_EPS) * g.astype(jnp.float32) + b.astype(jnp.float32)
    return y.astype(x.dtype)


def modulate(x, shift, scale):
    return x * (1 + scale) + shift


def post_residual(x, y, gate, g, b):
    return layer_norm(ALPHA * x + gate * y, g, b)


def sq_relu_mlp(h, w1, w2):
    return jnp.square(jax.nn.relu(h @ w1)) @ w2


def conformer_conv(a_val, a_gate, w_dw, b_dw, ln_g, ln_b):
    u = a_val * jax.nn.sigmoid(a_gate)
    y = lax.conv_general_dilated(
        u, w_dw[:, None, :].astype(u.dtype), window_strides=(1,),
        padding=[(CONV_W // 2, CONV_W // 2)],
        dimension_numbers=("NWC", "WIO", "NWC"),
        feature_group_count=u.shape[-1]) + b_dw
    return jax.nn.silu(layer_norm(y, ln_g, ln_b))


def s5_discretize(lam_re, lam_im, log_dt, b_re, b_im):
    f32 = jnp.float32
    lam_re, lam_im = lam_re.astype(f32), lam_im.astype(f32)
    dt = jnp.exp(log_dt.astype(f32))[:, None]
    mag = jnp.exp(lam_re * dt)
    lb_re, lb_im = mag * jnp.cos(lam_im * dt), mag * jnp.sin(lam_im * dt)
    den = jnp.square(lam_re) + jnp.square(lam_im)
    coef_re = ((lb_re - 1) * lam_re + lb_im * lam_im) / den
    coef_im = (lb_im * lam_re - (lb_re - 1) * lam_im) / den
    b_re, b_im = b_re.astype(f32), b_im.astype(f32)
    bb_re = coef_re[..., None] * b_re - coef_im[..., None] * b_im
    bb_im = coef_re[..., None] * b_im + coef_im[..., None] * b_re
    return lb_re, lb_im, bb_re, bb_im


def complex_linear_combine(e1, e2):
    a1r, a1i, b1r, b1i = e1
    a2r, a2i, b2r, b2i = e2
    return (a1r * a2r - a1i * a2i, a1r * a2i + a1i * a2r,
            a2r * b1r - a2i * b1i + b2r, a2r * b1i + a2i * b1r + b2i)


def s5_scan(u, h0_re, h0_im, lam_re, lam_im, log_dt, b_re, b_im, c_re, c_im):
    f32 = jnp.float32
    lb_re, lb_im, bb_re, bb_im = s5_discretize(lam_re, lam_im, log_dt, b_re, b_im)
    bu_re = jnp.einsum("blgp,gnp->blgn", u, bb_re)
    bu_im = jnp.einsum("blgp,gnp->blgn", u, bb_im)
    h0_re, h0_im = h0_re.astype(f32), h0_im.astype(f32)
    bu_re = bu_re.at[:, 0].add(lb_re * h0_re - lb_im * h0_im)
    bu_im = bu_im.at[:, 0].add(lb_re * h0_im + lb_im * h0_re)
    a_re = jnp.broadcast_to(lb_re, bu_re.shape)
    a_im = jnp.broadcast_to(lb_im, bu_im.shape)
    _, _, h_re, h_im = lax.associative_scan(
        complex_linear_combine, (a_re, a_im, bu_re, bu_im), axis=1)
    c_re, c_im = c_re.astype(f32), c_im.astype(f32)
    y = jnp.einsum("gpn,blgn->blgp", c_re, h_re) - jnp.einsum("gpn,blgn->blgp", c_im, h_im)
    return y, h_re[:, -1], h_im[:, -1]


def conv_ssm_mixer(h, h0_re, h0_im, w_in, w_dw, b_dw, cln_g, cln_b, lam_re, lam_im,
                   log_dt, b_re, b_im, c_re, c_im, d_skip, w_glu, w_out):
    bsz, L, _ = h.shape
    a_val, a_gate, u = jnp.split(h @ w_in, [CONV_CH, 2 * CONV_CH], axis=-1)
    y_conv = conformer_conv(a_val, a_gate, w_dw, b_dw, cln_g, cln_b)
    u32 = u.astype(jnp.float32)
    ug = u32.reshape(bsz, L, SSM_G, SSM_P)
    y_f, hf_re, hf_im = s5_scan(ug, h0_re[:, 0], h0_im[:, 0], lam_re[0], lam_im[0], log_dt[0],
                                b_re[0], b_im[0], c_re[0], c_im[0])
    y_r, hb_re, hb_im = s5_scan(ug[:, ::-1], h0_re[:, 1], h0_im[:, 1], lam_re[1], lam_im[1],
                                log_dt[1], b_re[1], b_im[1], c_re[1], c_im[1])
    y_s = (y_f + y_r[:, ::-1]).reshape(bsz, L, SSM_CH) + d_skip.astype(jnp.float32) * u32
    y_s = jax.nn.gelu(y_s).astype(h.dtype)
    y_ssm = y_s * jax.nn.sigmoid(y_s @ w_glu)
    out = jnp.concatenate([y_conv, y_ssm], axis=-1) @ w_out
    return out, jnp.stack([hf_re, hb_re], axis=1), jnp.stack([hf_im, hb_im], axis=1)


def rope_axis(seg, pos):
    half = seg.shape[-1] // 2
    freqs = ROPE_BASE ** (-jnp.arange(half, dtype=jnp.float32) / half)
    ang = pos[:, None] * freqs
    cos, sin = jnp.cos(ang), jnp.sin(ang)
    s = seg.astype(jnp.float32)
    s1, s2 = s[..., :half], s[..., half:]
    return jnp.concatenate([s1 * cos - s2 * sin, s1 * sin + s2 * cos], axis=-1).astype(seg.dtype)


def rope_2d(x, rows):
    row = jnp.repeat(jnp.arange(rows, dtype=jnp.float32), GRID_W)
    col = jnp.tile(jnp.arange(GRID_W, dtype=jnp.float32), rows)
    ax = x.shape[-1] // 2
    return jnp.concatenate([rope_axis(x[..., :ax], row), rope_axis(x[..., ax:], col)], axis=-1)


def diff_attn_project(h, w_qkv):
    bsz, L, _ = h.shape
    qk = 2 * ATT_H * ATT_DK
    q, k, v = jnp.split(h @ w_qkv, [qk, 2 * qk], axis=-1)
    q = q.reshape(bsz, L, 2, ATT_H, ATT_DK).transpose(2, 0, 3, 1, 4)
    k = k.reshape(bsz, L, 2, ATT_H, ATT_DK).transpose(2, 0, 3, 1, 4)
    v = v.reshape(bsz, L, ATT_H, ATT_DV).transpose(0, 2, 1, 3)
    return q, k, v


def diff_lambda(lq1, lk1, lq2, lk2, lam_init):
    f32 = jnp.float32
    return (jnp.exp(jnp.sum(lq1.astype(f32) * lk1.astype(f32)))
            - jnp.exp(jnp.sum(lq2.astype(f32) * lk2.astype(f32))) + lam_init)


def diff_attend(q, k, v, lam):
    _, bsz, nh, lq, dk = q.shape
    nb = lq // Q_BLOCK
    qb = q.reshape(2, bsz, nh, nb, Q_BLOCK, dk).transpose(3, 0, 1, 2, 4, 5)
    scale = dk ** -0.5

    def one_block(qblk):
        s = jnp.einsum("mbhqd,mbhkd->mbhqk", qblk, k).astype(jnp.float32) * scale
        p = jax.nn.softmax(s, axis=-1)
        w = (p[0] - lam * p[1]).astype(v.dtype)
        return jnp.einsum("bhqk,bhkd->bhqd", w, v)

    o = lax.map(one_block, qb)
    return o.transpose(1, 2, 0, 3, 4).reshape(bsz, nh, lq, v.shape[-1])


def diff_attn_output(o, subln_g, lam_init, w_out):
    bsz, nh, L, dv = o.shape
    dt = o.dtype
    o32 = o.astype(jnp.float32)
    o32 = o32 * lax.rsqrt(jnp.mean(jnp.square(o32), axis=-1, keepdims=True) + LN_EPS)
    o32 = o32 * subln_g.astype(jnp.float32) * (1.0 - lam_init)
    o = o32.astype(dt).transpose(0, 2, 1, 3).reshape(bsz, L, nh * dv)
    return o @ w_out


def setup_inputs(seed: int = 0) -> dict:
    key = jax.random.key(seed)
    ks = jax.random.split(key, 40)
    f32 = jnp.float32

    def nrm(i, shape, s=1.0):
        return s * jax.random.normal(ks[i], shape, f32)

    lam_im_base = math.pi * jnp.arange(SSM_N, dtype=f32)
    return {
        "x_prompt": nrm(0, (BATCH, SEQ, D_MODEL)),
        "x_sample": nrm(1, (DEC_BATCH, DEC_SEQ, D_MODEL)),
        "state_s5_re": nrm(2, (DEC_BATCH, N_EVEN, 2, SSM_G, SSM_N), 0.3),
        "state_s5_im": nrm(3, (DEC_BATCH, N_EVEN, 2, SSM_G, SSM_N), 0.3),
        "cache_k": nrm(4, (DEC_BATCH, N_ODD, 2, ATT_H, PAST_LEN, ATT_DK)),
        "cache_v": nrm(5, (DEC_BATCH, N_ODD, ATT_H, PAST_LEN, ATT_DV)),
        "c": nrm(6, (DEC_BATCH, D_MODEL)),
        "c_ctx": nrm(7, (D_MODEL,)),
        "w_mod": nrm(8, (DEPTH, D_MODEL, 6 * D_MODEL), D_MODEL ** -0.5),
        "b_mod": nrm(9, (DEPTH, 6 * D_MODEL), 0.01),
        "ln_g": 1.0 + nrm(10, (DEPTH, 2, D_MODEL), 0.01),
        "ln_b": nrm(11, (DEPTH, 2, D_MODEL), 0.01),
        "w_in_ab": nrm(12, (N_EVEN, D_MODEL, 2 * CONV_CH + SSM_CH), D_MODEL ** -0.5),
        "w_dw": nrm(13, (N_EVEN, CONV_W, CONV_CH), CONV_W ** -0.5),
        "b_dw": nrm(14, (N_EVEN, CONV_CH), 0.01),
        "conv_ln_g": 1.0 + nrm(15, (N_EVEN, CONV_CH), 0.01),
        "conv_ln_b": nrm(16, (N_EVEN, CONV_CH), 0.01),
        "s5_lambda_re": -0.5 + nrm(17, (N_EVEN, 2, SSM_G, SSM_N), 0.01),
        "s5_lambda_im": lam_im_base + nrm(18, (N_EVEN, 2, SSM_G, SSM_N), 0.01),
        "s5_log_dt": jax.random.uniform(ks[19], (N_EVEN, 2, SSM_G), f32,
                                        minval=math.log(1e-3), maxval=math.log(1e-1)),
        "s5_b_re": nrm(20, (N_EVEN, 2, SSM_G, SSM_N, SSM_P), (2 * SSM_P) ** -0.5),
        "s5_b_im": nrm(21, (N_EVEN, 2, SSM_G, SSM_N, SSM_P), (2 * SSM_P) ** -0.5),
        "s5_c_re": nrm(22, (N_EVEN, 2, SSM_G, SSM_P, SSM_N), (2 * SSM_N) ** -0.5),
        "s5_c_im": nrm(23, (N_EVEN, 2, SSM_G, SSM_P, SSM_N), (2 * SSM_N) ** -0.5),
        "s5_d": 1.0 + nrm(24, (N_EVEN, SSM_CH), 0.1),
        "w_glu": nrm(25, (N_EVEN, SSM_CH, SSM_CH), SSM_CH ** -0.5),
        "w_out_ab": nrm(26, (N_EVEN, CONV_CH + SSM_CH, D_MODEL), BETA * (CONV_CH + SSM_CH) ** -0.5),
        "w_qkv": nrm(27, (N_ODD, D_MODEL, 4 * ATT_H * ATT_DK + ATT_H * ATT_DV), D_MODEL ** -0.5),
        "lam_q1": nrm(28, (N_ODD, ATT_DK), 0.1),
        "lam_k1": nrm(29, (N_ODD, ATT_DK), 0.1),
        "lam_q2": nrm(30, (N_ODD, ATT_DK), 0.1),
        "lam_k2": nrm(31, (N_ODD, ATT_DK), 0.1),
        "subln_g": 1.0 + nrm(32, (N_ODD, ATT_DV), 0.01),
        "w_out_c": nrm(33, (N_ODD, ATT_H * ATT_DV, D_MODEL), BETA * (ATT_H * ATT_DV) ** -0.5),
        "w_ff1": nrm(34, (DEPTH, D_MODEL, D_FF), D_MODEL ** -0.5),
        "w_ff2": nrm(35, (DEPTH, D_FF, D_MODEL), BETA * D_FF ** -0.5),
    }


def reference(x_prompt, x_sample, state_s5_re, state_s5_im, cache_k, cache_v, c, c_ctx,
              w_mod, b_mod, ln_g, ln_b, w_in_ab, w_dw, b_dw, conv_ln_g, conv_ln_b,
              s5_lambda_re, s5_lambda_im, s5_log_dt, s5_b_re, s5_b_im, s5_c_re, s5_c_im,
              s5_d, w_glu, w_out_ab, w_qkv, lam_q1, lam_k1, lam_q2, lam_k2, subln_g,
              w_out_c, w_ff1, w_ff2):
    silu_ctx = jax.nn.silu(c_ctx)[None, :]
    silu_c = jax.nn.silu(c)
    yp, ys = x_prompt, x_sample
    zero_state = jnp.zeros((x_prompt.shape[0], 2, SSM_G, SSM_N), jnp.float32)
    s_re_list, s_im_list, k_list, v_list = [], [], [], []
    for l in range(DEPTH):
        mod_p = jnp.split((silu_ctx @ w_mod[l] + b_mod[l])[:, None, :], 6, axis=-1)
        mod_s = jnp.split((silu_c @ w_mod[l] + b_mod[l])[:, None, :], 6, axis=-1)
        hp = modulate(yp, mod_p[0], mod_p[1])
        hs = modulate(ys, mod_s[0], mod_s[1])
        if l % 2 == 0:
            e = l // 2
            prm = (w_in_ab[e], w_dw[e], b_dw[e], conv_ln_g[e], conv_ln_b[e],
                   s5_lambda_re[e], s5_lambda_im[e], s5_log_dt[e], s5_b_re[e], s5_b_im[e],
                   s5_c_re[e], s5_c_im[e], s5_d[e], w_glu[e], w_out_ab[e])
            mp, st_re, st_im = conv_ssm_mixer(hp, zero_state, zero_state, *prm)
            ms, _, _ = conv_ssm_mixer(hs, state_s5_re[:, e], state_s5_im[:, e], *prm)
            s_re_list.append(st_re)
            s_im_list.append(st_im)
        else:
            o_i = l // 2
            lam_init = 0.8 - 0.6 * math.exp(-0.3 * l)
            lam = diff_lambda(lam_q1[o_i], lam_k1[o_i], lam_q2[o_i], lam_k2[o_i], lam_init)
            qp, kp, vp = diff_attn_project(hp, w_qkv[o_i])
            mp = diff_attn_output(diff_attend(qp, kp, vp, lam), subln_g[o_i], lam_init, w_out_c[o_i])
            k_list.append(jnp.swapaxes(kp, 0, 1))
            v_list.append(vp)
            qs, ks_, vs = diff_attn_project(hs, w_qkv[o_i])
            rows = hs.shape[1] // GRID_W
            qs = rope_2d(qs, rows)
            ks_ = rope_2d(ks_, rows)
            ck = jnp.swapaxes(cache_k[:, o_i], 0, 1).astype(ks_.dtype)
            ks_ = jnp.concatenate([ks_, ck], axis=3)
            vs = jnp.concatenate([vs, cache_v[:, o_i].astype(vs.dtype)], axis=2)
            ms = diff_attn_output(diff_attend(qs, ks_, vs, lam), subln_g[o_i], lam_init, w_out_c[o_i])
        yp = post_residual(yp, mp, mod_p[2], ln_g[l, 0], ln_b[l, 0])
        ys = post_residual(ys, ms, mod_s[2], ln_g[l, 0], ln_b[l, 0])
        fp = sq_relu_mlp(modulate(yp, mod_p[3], mod_p[4]), w_ff1[l], w_ff2[l])
        fs = sq_relu_mlp(modulate(ys, mod_s[3], mod_s[4]), w_ff1[l], w_ff2[l])
        yp = post_residual(yp, fp, mod_p[5], ln_g[l, 1], ln_b[l, 1])
        ys = post_residual(ys, fs, mod_s[5], ln_g[l, 1], ln_b[l, 1])
    new_state_s5_re = jnp.stack(s_re_list, axis=1)
    new_state_s5_im = jnp.stack(s_im_list, axis=1)
    new_cache_k = jnp.stack(k_list, axis=1)
    new_cache_v = jnp.stack(v_list, axis=1)
    return (yp, ys, new_state_s5_re, new_state_s5_im, new_cache_k, new_cache_v)
```

```python
import math
import numpy as np
import concourse.bass as bass
import concourse.mybir as mybir
from concourse.bass_utils import run_bass_kernel_spmd
from contextlib import ExitStack

F32 = mybir.dt.float32
BF16 = mybir.dt.bfloat16
AF = mybir.ActivationFunctionType
ALU = mybir.AluOpType
AX = mybir.AxisListType

D = 2048; KC = 16; DFF = 8192
ALPHA = 4 ** 0.25
LN_EPS = 1e-5
LAM_INIT = 0.8 - 0.6 * math.exp(-0.3 * 1)
ENGS = ('pe', 'act', 'dve', 'pool', 'sp')
NDS = 24
PAGE = 512
SB_BASE = 16640
SB_END = 229376


class Prog:
    def __init__(self, nc):
        self.nc = nc; self.ops = []; self.lastw = {}; self.readers = {}

    def op(self, eng, fn, reads=(), writes=(), dma=False):
        if eng != 'pe':
            pk = [k for k in reads if k[0] == 'ps']
            if pk: writes = list(writes) + pk
        i = len(self.ops); deps = set()
        lw = self.lastw; rd = self.readers
        for k in reads:
            j = lw.get(k)
            if j is not None: deps.add(j)
        for k in writes:
            j = lw.get(k)
            if j is not None: deps.add(j)
            r = rd.get(k)
            if r: deps.update(r)
        for k in reads:
            r = rd.get(k)
            if r is None: rd[k] = [i]
            else: r.append(i)
        for k in writes:
            lw[k] = i; rd[k] = []
        deps.discard(i)
        self.ops.append((eng, fn, deps, dma))
        return i

    def finalize(self, stack):
        nc = self.nc; ops = self.ops
        esem = {e: stack.enter_context(nc.semaphore('s_' + e)) for e in ENGS}
        dsem = [stack.enter_context(nc.semaphore('d%d' % i)) for i in range(NDS)]
        red = []
        needed = set()
        for (e, fn, deps, dma) in ops:
            best = {}; keep = []
            for d in deps:
                pe_, _, _, pdma = ops[d]
                if pdma:
                    keep.append(d); continue
                if pe_ == 'pe' and e == 'pe' and not dma: continue
                if best.get(pe_, -1) < d: best[pe_] = d
            keep += list(best.values())
            red.append(keep)
            needed.update(keep)
        ecount = {e: 0 for e in ENGS}; dcount = [0] * NDS
        dnx = {True: 0, False: 0}; half = NDS // 2
        ev = {}; dprev = {}
        for i, (e, fn, deps, dma) in enumerate(ops):
            if dma:
                sw = (e == 'pool')
                j = (0 if sw else half) + dnx[sw]; dnx[sw] = (dnx[sw] + 1) % half
                dprev[i] = (dsem[j], dcount[j]); dcount[j] += 16; ev[i] = (dsem[j], dcount[j])
            elif i in needed:
                ecount[e] += 1; ev[i] = (esem[e], ecount[e])
        self.ecount = ecount
        waited = {e: {} for e in ENGS}
        plan = {e: [] for e in ENGS}
        lastdma = {e: {} for e in ENGS}
        for i, (e, fn, deps, dma) in enumerate(ops):
            ws = {}
            if dma:
                s, v = dprev[i]
                if v > 0: ws[s.name] = (s, v)
            for d in red[i]:
                s, v = ev[d]
                if s.name not in ws or ws[s.name][1] < v: ws[s.name] = (s, v)
            wl = []
            for nm, (s, v) in ws.items():
                if waited[e].get(nm, 0) >= v: continue
                waited[e][nm] = v; wl.append((s, v))
            inc = ev.get(i)
            if dma: lastdma[e][inc[0].name] = inc
            plan[e].append((fn, wl, inc, 16 if dma else 1))
        with nc.Block() as block:
            def runner(e):
                def run(eng):
                    for fn, wl, inc, by in plan[e]:
                        for s, v in wl: eng.wait_ge(s, v)
                        ins = fn(eng)
                        if inc is not None: ins.then_inc(inc[0], by)
                    for nm, (s, v) in lastdma[e].items():
                        if waited[e].get(nm, 0) < v: eng.wait_ge(s, v)
                return run
            block.tensor(runner('pe')); block.scalar(runner('act')); block.vector(runner('dve'))
            block.gpsimd(runner('pool')); block.sync(runner('sp'))


class V:
    __slots__ = ('ap', 'pg')
    def __init__(self, ap, pg): self.ap = ap; self.pg = pg


class T:
    def __init__(self, nc, name, shape, dtype, off):
        self.es = 2 if dtype == BF16 else 4
        fsize = 1
        for s in shape[1:]: fsize *= s
        assert off % 4 == 0 and off >= SB_BASE and off + fsize * self.es <= SB_END, (name, off, fsize * self.es)
        self.t = nc.alloc_sbuf_tensor_at(name, list(shape), dtype, offset=off)
        self.off = off; self.shape = list(shape); self.fsize = fsize; self.nbytes = fsize * self.es
        st = []; acc = 1
        for s in reversed(shape[1:]):
            st.append(acc); acc *= s
        self.strides = list(reversed(st))

    def _pages(self, lo, hi):
        a = (self.off + lo * self.es) // PAGE; b = (self.off + hi * self.es - 1) // PAGE
        return [('sb', i) for i in range(a, b + 1)]

    def v(self, *idx, p=slice(None)):
        idx = tuple(idx) + (slice(None),) * (len(self.shape) - 1 - len(idx))
        lo = 0; hi = 0
        for i, s, n in zip(idx, self.strides, self.shape[1:]):
            if isinstance(i, int):
                lo += i * s; hi += i * s
            else:
                a, b, step = i.indices(n)
                if step > 0:
                    cnt = max(0, (b - a + step - 1) // step)
                    lo += a * s; hi += (a + (cnt - 1) * step) * s
                else:
                    cnt = max(0, (a - b - step - 1) // (-step))
                    hi += a * s; lo += (a + (cnt - 1) * step) * s
        return V(self.t[(p,) + idx], self._pages(lo, hi + 1))


class PSB:
    def __init__(self, t, i): self.t = t; self.i = i
    def v(self, *idx, p=slice(None)):
        return V(self.t[(p,) + tuple(idx)] if idx else self.t[p], [('ps', self.i)])


def DR(ap, name):
    return V(ap, [('dram', name)])


def _pgs(*vs):
    out = []
    for x in vs:
        if isinstance(x, V): out += x.pg
    return out


def _ap(x):
    return x.ap if isinstance(x, V) else x


class Kern:
    def __init__(self, nc):
        self.nc = nc; self.P = Prog(nc); self.uid = 0

    def name(self, s):
        self.uid += 1
        return '%s%d' % (s, self.uid)

    def mm(self, out, lhsT, rhs, start=True, stop=True, tp=None):
        o, l, r = out.ap, lhsT.ap, rhs.ap
        if tp is None:
            fn = lambda e: e.matmul(o, lhsT=l, rhs=r, start=start, stop=stop)
        else:
            fn = lambda e: e.matmul(o, lhsT=l, rhs=r, start=start, stop=stop, tile_position=tp)
        self.P.op('pe', fn, reads=lhsT.pg + rhs.pg, writes=out.pg)

    def act(self, out, in_, func, scale=1.0, bias=0.0, accum=None, eng='act'):
        o, i, s, b = out.ap, in_.ap, _ap(scale), _ap(bias)
        if accum is None:
            fn = lambda e: e.activation(out=o, in_=i, func=func, bias=b, scale=s)
        else:
            a = accum.ap
            fn = lambda e: e.activation(out=o, in_=i, func=func, bias=b, scale=s, accum_out=a)
        self.P.op('act', fn, reads=in_.pg + _pgs(scale, bias), writes=out.pg + _pgs(accum))

    def tt(self, out, in0, in1, op, eng='dve'):
        o, a, b = out.ap, in0.ap, in1.ap
        self.P.op(eng, lambda e: e.tensor_tensor(out=o, in0=a, in1=b, op=op), reads=in0.pg + in1.pg, writes=out.pg)

    def ts(self, out, in0, s1, op0, s2=None, op1=None, eng='dve'):
        o, a, x1, x2 = out.ap, in0.ap, _ap(s1), _ap(s2)
        if op1 is None:
            fn = lambda e: e.tensor_scalar(out=o, in0=a, scalar1=x1, scalar2=None, op0=op0)
        else:
            fn = lambda e: e.tensor_scalar(out=o, in0=a, scalar1=x1, scalar2=x2, op0=op0, op1=op1)
        self.P.op(eng, fn, reads=in0.pg + _pgs(s1, s2), writes=out.pg)

    def stt(self, out, in0, scalar, in1, op0, op1, eng='dve'):
        o, a, s, b = out.ap, in0.ap, _ap(scalar), in1.ap
        self.P.op(eng, lambda e: e.scalar_tensor_tensor(out=o, in0=a, scalar=s, in1=b, op0=op0, op1=op1),
                  reads=in0.pg + in1.pg + _pgs(scalar), writes=out.pg)

    def copy(self, out, in_, eng='dve'):
        o, i = out.ap, in_.ap
        self.P.op(eng, lambda e: e.tensor_copy(out=o, in_=i), reads=in_.pg, writes=out.pg)

    def memset(self, out, val, eng='dve'):
        o = out.ap
        self.P.op(eng, lambda e: e.memset(o, val), writes=out.pg)

    def recip(self, out, in_):
        o, i = out.ap, in_.ap
        self.P.op('dve', lambda e: e.reciprocal(out=o, in_=i), reads=in_.pg, writes=out.pg)

    def scan(self, out, d0, d1, init=0.0):
        o, a, b = out.ap, d0.ap, d1.ap
        self.P.op('dve', lambda e: e.tensor_tensor_scan(out=o, data0=a, data1=b, initial=init, op0=ALU.mult, op1=ALU.add),
                  reads=d0.pg + d1.pg, writes=out.pg)

    def reduce(self, out, in_, op=ALU.add, axis=AX.X):
        o, i = out.ap, in_.ap
        self.P.op('dve', lambda e: e.tensor_reduce(out=o, in_=i, axis=axis, op=op), reads=in_.pg, writes=out.pg)

    def dma(self, out, in_, eng='sp'):
        o, i = out.ap, in_.ap
        self.P.op(eng, lambda e: e.dma_start(out=o, in_=i), reads=in_.pg, writes=out.pg, dma=True)


CONST_OFF = SB_BASE
X_OFF = CONST_OFF + 9984
H_OFF = X_OFF + 65536
BIG_OFF = H_OFF + 32768
WR_OFF = BIG_OFF + 65536
RING_BYTES = 24064
SCR_OFF = WR_OFF + RING_BYTES
WSLOT = 8192
NWSLOT = 3


class Bump:
    def __init__(self, off, end): self.o = off; self.end = end
    def take(self, n):
        n = (n + 31) // 32 * 32
        o = self.o; self.o += n
        assert self.o <= self.end, ('bump overflow', self.o, self.end)
        return o


class Builder:
    def __init__(self, dbg=()):
        self.nc = nc = bass.Bass("TRN2", target_bir_lowering=False)
        self.K = Kern(nc)
        self.dbg = set(dbg)
        self.cut = 99
        self.ins = {}; self.outs = {}
        self.cb = Bump(CONST_OFF, X_OFF)
        self.ps = [PSB(nc.alloc_psum_tensor('ps%d' % i, [128, 512], F32), i) for i in range(8)]
        self.psn = 0; self.pin = False
        self.scr = [T(nc, 'scr%d' % i, [128, 512], F32, SCR_OFF + 2048 * i) for i in range(7)]
        self.scrn = 0
        self.scrb = [T(nc, 'scrb%d' % i, [128, 512], BF16, SCR_OFF + 2048 * i) for i in range(3)]
        self.wq = []
        self.wissued = 0; self.wused = 0; self.wpos = 0; self.wlive = []; self.wtiles = {}

    def din(self, name, shape, dtype=F32):
        t = self.nc.dram_tensor(name, list(shape), dtype, kind="ExternalInput")
        self.ins[name] = t
        return t

    def dout(self, name, shape, dtype=F32):
        t = self.nc.dram_tensor(name, list(shape), dtype, kind="ExternalOutput")
        self.outs[name] = t
        return t

    def dscr(self, name, shape, dtype):
        return self.nc.dram_tensor(name, list(shape), dtype, kind="Internal")

    def nps(self):
        p = self.ps[self.psn % 8]; self.psn += 1
        return p

    def nscr(self):
        s = self.scr[self.scrn % 3]; self.scrn += 1
        return s

    def nscrb(self):
        s = self.scrb[self.scrn % 3]; self.scrn += 1
        return s

    def ct(self, name, shape, dtype=F32):
        n = (2 if dtype == BF16 else 4)
        for s in shape[1:]: n *= s
        return T(self.nc, name, shape, dtype, self.cb.take(n))

    def dump(self, key, t, shape=None):
        if key not in self.dbg: return
        K = self.K
        fs = t.fsize
        o = self.dout('dbg_' + key, [t.shape[0], fs], F32)
        if t.es == 4:
            K.dma(DR(o.ap(), 'dbg_' + key), V(t.t[:].rearrange(_flat(len(t.shape))) if len(t.shape) > 2 else t.t[:], t._pages(0, fs)))
        else:
            raise NotImplementedError

    def dump_bf(self, key, t, nch):
        if key not in self.dbg: return
        K = self.K
        nt = t.shape[2]
        o = self.dout('dbg_' + key, [128, nch, nt], F32)
        for kc in range(nch):
            for c0 in range(0, nt, 512):
                s = self.nscr()
                K.copy(s.v(slice(0, 512)), t.v(kc, slice(c0, c0 + 512)))
                K.dma(DR(o[:, kc, c0:c0 + 512], 'dbg_' + key), s.v(slice(0, 512)))

    def wplan(self, units):
        self.wq += list(units)

    def wnext(self):
        K = self.K; nc = self.nc
        RING = RING_BYTES
        cur = self.wused
        while self.wissued < len(self.wq):
            dv, n = self.wq[self.wissued]
            nb = 2 * n
            pos = self.wpos
            if pos + nb > RING: pos = 0
            ok = True
            for (idx, a, b_) in self.wlive:
                if idx >= cur - 1 and not (pos + nb <= a or b_ <= pos):
                    ok = False; break
            if not ok and self.wissued > cur: break
            assert ok, 'weight ring too small'
            t = T(nc, K.name('wu'), [128, n], BF16, WR_OFF + pos)
            K.dma(t.v(), dv, eng='pool')
            self.wlive.append((self.wissued, pos, pos + nb)); self.wtiles[self.wissued] = t
            self.wlive = [x for x in self.wlive if x[0] >= cur - 1]
            self.wpos = pos + nb
            self.wissued += 1
        t = self.wtiles.pop(cur)
        self.wused += 1
        return t


def _flat(nd):
    names = 'abcdefg'[:nd - 1]
    return 'p ' + ' '.join(names) + ' -> p (' + ' '.join(names) + ')'


def phase_consts(B):
    K = B.K
    c = B.din('consts', [128, 3, 128])
    B.cst = B.ct('cst', [128, 3, 128])
    K.dma(B.cst.v(), DR(c.ap(), 'consts'))
    B.ones_f = lambda p=slice(None): B.cst.v(0, p=p)
    B.ident_f = lambda p=slice(None), n=128: B.cst.v(1, slice(0, n), p=p)
    B.perm_f = lambda: B.cst.v(2)
    B.cstb = B.ct('cstb', [128, 2, 128], BF16)
    K.copy(B.cstb.v(), B.cst.v(slice(0, 2)))
    B.ones_b = lambda p=slice(None): B.cstb.v(0, p=p)
    B.ident_b = lambda: B.cstb.v(1)
    lnp = B.din('lnp', [128, 2 * 2 * 2 * 16])
    B.lnp = B.ct('lnp', [128, 2, 2, 2, 16])
    K.dma(V(B.lnp.t[:].rearrange('p a b c d -> p (a b c d)'), B.lnp._pages(0, B.lnp.fsize)), DR(lnp.ap(), 'lnp'))
    B.eps_ln = B.ct('eps_ln', [128, 1]); K.memset(B.eps_ln.v(), LN_EPS / (ALPHA * ALPHA))
    B.eps_raw = B.ct('eps_raw', [128, 1]); K.memset(B.eps_raw.v(), LN_EPS)


def phase_mod(B):
    for _ in phase_mod_gen(B): pass


def phase_mod_gen(B):
    K = B.K
    cvec = B.din('cvec', [128, 32])
    wmod = B.din('wmod', [2, 48, 128, 4096])
    bcol = B.din('bcol', [128, 2 * 6 * 16])
    B.modt = B.ct('modt', [128, 2, 6, 16, 2])
    NS = 4
    slabs = [T(B.nc, 'slab%d' % i, [128, 16, 256], BF16, WR_OFF + 8192 * i) for i in range(NS)]
    so = WR_OFF + 8192 * NS
    cv = T(B.nc, 'cv', [128, 16, 2], F32, so)
    cs = T(B.nc, 'cs', [128, 16, 2], BF16, so + 512)
    bc = T(B.nc, 'bc', [128, 2, 6, 16], F32, so + 1024)
    rows = T(B.nc, 'rows', [128, 256], F32, so + 2048)
    K.dma(V(cv.t[:].rearrange('p a b -> p (a b)'), cv._pages(0, 32)), DR(cvec.ap(), 'cvec'))
    K.dma(V(bc.t[:].rearrange('p a b c -> p (a b c)'), bc._pages(0, 192)), DR(bcol.ap(), 'bcol'))
    K.act(cs.v(), cv.v(), AF.Silu)
    units = [(l, j) for l in range(2) for j in range(48)]
    issued = 0
    for ui, (l, j) in enumerate(units):
        while issued < len(units) and issued < ui + NS - 1:
            ll, jj = units[issued]
            sl = slabs[issued % NS]
            K.dma(V(sl.t[:].rearrange('p a b -> p (a b)'), sl._pages(0, sl.fsize)), DR(wmod[ll, jj], 'wmod'), eng='pool')
            issued += 1
        sl = slabs[ui % NS]
        ps = B.nps()
        for kc in range(16):
            K.mm(ps.v(slice(0, 256), p=slice(0, 2)), cs.v(kc), sl.v(kc), start=(kc == 0), stop=(kc == 15))
        K.copy(rows.v(p=slice(0, 2)), ps.v(slice(0, 256), p=slice(0, 2)))
        ps2 = B.nps()
        for i in range(2):
            K.mm(ps2.v(slice(2 * i, 2 * i + 2)), rows.v(slice(128 * i, 128 * i + 128), p=slice(0, 2)),
                 B.ident_f(p=slice(0, 2), n=2))
        six = j // 8; ch = (j % 8) * 2
        K.tt(B.modt.v(l, six, slice(ch, ch + 2)),
             V(ps2.t[:, 0:4].rearrange('p (a b) -> p a b', b=2), ps2.v().pg),
             V(bc.t[:, l, six, ch:ch + 2].unsqueeze(2).to_broadcast([128, 2, 2]), bc.v().pg), ALU.add)
        yield
    for l in range(2):
        for six in (1, 4):
            K.ts(B.modt.v(l, six), B.modt.v(l, six), 1.0, ALU.add)
        for six in (2, 5):
            K.ts(B.modt.v(l, six), B.modt.v(l, six), 1.0 / ALPHA, ALU.mult)
    B.dump('modt', B.modt)


def host_common(inp):
    f = np.float32
    H = {}
    ones = np.ones((128, 128), f); ident = np.eye(128, dtype=f)
    perm = np.zeros((128, 128), f)
    for m in range(128):
        j = m % 32
        k = m + 16 if j < 16 else m - 16
        perm[k, m] = 1.0
    H['consts'] = np.ascontiguousarray(np.stack([ones, ident, perm], 1))
    wm = inp['w_mod'].reshape(2, 16, 128, 48, 256).transpose(0, 3, 2, 1, 4)
    H['wmod'] = np.ascontiguousarray(wm).reshape(2, 48, 128, 4096)
    bm = inp['b_mod'].reshape(2, 6, 16, 128).transpose(3, 0, 1, 2)
    H['bcol'] = np.ascontiguousarray(bm).reshape(128, 192)
    lg = inp['ln_g'].reshape(2, 2, 16, 128).transpose(3, 0, 1, 2); lb = inp['ln_b'].reshape(2, 2, 16, 128).transpose(3, 0, 1, 2)
    H['lnp'] = np.ascontiguousarray(np.stack([lg, lb], 1)).reshape(128, 128)
    return H


def host_all(inp):
    H = host_common(inp)
    host_ff(inp, H)
    host_s5(inp, H)
    host_l0(inp, H)
    host_l1(inp, H)
    return H


def host_core(inp, core):
    f = np.float32
    b = core // 4
    C = {}
    C['xp'] = fm_tokens(inp['x_prompt'][4 * core:4 * core + 4].reshape(1024, 2048))
    host_sample(inp, core, C)
    cv = np.stack([inp['c_ctx'], inp['c'][b]], -1).reshape(16, 128, 2).transpose(1, 0, 2)
    C['cvec'] = np.ascontiguousarray(cv).reshape(128, 32)
    return C


class Pass:
    def __init__(self, B, name, ntok, vec):
        self.B = B; self.name = name; self.ntok = ntok; self.vec = vec
        self.X = T(B.nc, B.K.name('X'), [128, 16, ntok], F32, X_OFF)
        self.H = T(B.nc, B.K.name('H'), [128, 16, ntok], BF16, H_OFF)
        self.tiles = [(c, min(512, ntok - c)) for c in range(0, ntok, 512)]


def mcol(B, l, six, kc, vec):
    return B.modt.v(l, six, kc, slice(vec, vec + 1))


def modulate(B, ps_, l, which):
    K = B.K; X = ps_.X; H = ps_.H
    for kc in range(16):
        sc = mcol(B, l, 3 * which + 1, kc, ps_.vec); sh = mcol(B, l, 3 * which, kc, ps_.vec)
        if kc % 2 == 0:
            K.act(H.v(kc), X.v(kc), AF.Identity, scale=sc, bias=sh)
        else:
            K.ts(H.v(kc), X.v(kc), sc, ALU.mult, sh, ALU.add)


def wv(sl, kc, m=128):
    return sl.v(slice(kc * m, (kc + 1) * m))


def layer_norm(B, X, nch, tiles, gcol, bcol, eps_v, act_func=AF.Identity, out=None):
    K = B.K
    Dn = nch * 128
    for (c0, n) in tiles:
        cs = slice(c0, c0 + n)
        ps1 = B.nps(); ps2 = B.nps()
        for kc in range(nch):
            K.mm(ps1.v(slice(0, n)), B.ones_f(), X.v(kc, cs), start=(kc == 0), stop=(kc == nch - 1))
        for kc in range(nch):
            sq = B.nscrb()
            K.act(sq.v(slice(0, n)), X.v(kc, cs), AF.Square)
            K.mm(ps2.v(slice(0, n)), B.ones_b(), sq.v(slice(0, n)), start=(kc == 0), stop=(kc == nch - 1))
        mean, msq, var, rstd = B.scr[3], B.scr[4], B.scr[5], B.scr[6]
        sn = slice(0, n)
        K.ts(mean.v(sn), ps1.v(sn), 1.0 / Dn, ALU.mult)
        K.tt(msq.v(sn), mean.v(sn), mean.v(sn), ALU.mult)
        K.stt(var.v(sn), ps2.v(sn), 1.0 / Dn, msq.v(sn), ALU.mult, ALU.subtract)
        K.act(var.v(sn), var.v(sn), AF.Ln, bias=eps_v)
        K.act(rstd.v(sn), var.v(sn), AF.Exp, scale=-0.5)
        for kc in range(nch):
            t = B.nscr()
            K.tt(t.v(sn), X.v(kc, cs), mean.v(sn), ALU.subtract)
            K.tt(t.v(sn), t.v(sn), rstd.v(sn), ALU.mult)
            o = (out if out is not None else X).v(kc, cs)
            K.act(o, t.v(sn), act_func, scale=gcol(kc), bias=bcol(kc))


def post_ln(B, ps_, l, which):
    lnp = B.lnp
    layer_norm(B, ps_.X, 16, ps_.tiles,
               lambda kc: lnp.v(0, l, which, slice(kc, kc + 1)), lambda kc: lnp.v(1, l, which, slice(kc, kc + 1)),
               B.eps_ln.v())


def ffn(B, ps_, l):
    K = B.K; X = ps_.X; H = ps_.H
    w1 = B.ins['wff1']; w2 = B.ins['wff2']
    HID = T(B.nc, K.name('hid'), [128, 32, ps_.ntok], BF16, BIG_OFF)
    units = []
    for hh in range(2):
        units += [(DR(w1[l, hh * 32 + m], 'wff1'), 2048) for m in range(32)]
        units += [(DR(w2[l, hh, o], 'wff2'), 4096) for o in range(16)]
    B.wplan(units)
    for hh in range(2):
        for m in range(32):
            sl = B.wnext()
            for (c0, n) in ps_.tiles:
                cs = slice(c0, c0 + n); ps = B.nps()
                for kc in range(16):
                    K.mm(ps.v(slice(0, n)), wv(sl, kc), H.v(kc, cs), start=(kc == 0), stop=(kc == 15))
                r = B.nscr()
                K.act(r.v(slice(0, n)), ps.v(slice(0, n)), AF.Relu)
                K.tt(HID.v(m, cs), r.v(slice(0, n)), r.v(slice(0, n)), ALU.mult)
        for o in range(16):
            sl = B.wnext()
            for (c0, n) in ps_.tiles:
                cs = slice(c0, c0 + n); ps = B.nps()
                for kc in range(32):
                    K.mm(ps.v(slice(0, n)), wv(sl, kc), HID.v(kc, cs), start=(kc == 0), stop=(kc == 31))
                K.stt(X.v(o, cs), ps.v(slice(0, n)), mcol(B, l, 5, o, ps_.vec), X.v(o, cs), ALU.mult, ALU.add)


def units_fm(W, mcols=128):
    Kd, M = W.shape
    return np.ascontiguousarray(W.reshape(Kd // 128, 128, M // mcols, mcols).transpose(2, 1, 0, 3)).reshape(M // mcols, 128, (Kd // 128) * mcols)


def host_ff(inp, H):
    H['wff1'] = np.stack([units_fm(inp['w_ff1'][l]) for l in range(2)])
    H['wff2'] = np.stack([np.stack([units_fm(inp['w_ff2'][l][hh * 4096:(hh + 1) * 4096]) for hh in range(2)]) for l in range(2)])


def fm_tokens(x):
    return np.ascontiguousarray(x.reshape(x.shape[0], 16, 128).transpose(2, 1, 0))


def load_x(B, ps_, dram, name, c0=0):
    K = B.K
    for kc in range(16):
        K.dma(ps_.X.v(kc), DR(dram[:, kc, c0:c0 + ps_.ntok], name))


def store_x(B, ps_, dram, name):
    K = B.K
    for kc in range(16):
        K.dma(DR(dram[:, kc, :], name), ps_.X.v(kc))


def build_stage(B, stage):
    if stage.startswith('full'):
        build_full(B, parts=stage.split(':')[1] if ':' in stage else 'ps')
        return
    phase_consts(B)
    phase_mod(B)
    if stage == 'mod': return
    if stage.startswith('full'):
        pass
    if stage == 's5prep':
        phase_s5prep(B)
        return
    if stage.startswith('l0p'):
        if ':' in stage: B.cut = int(stage.split(':')[1])
        phase_s5prep(B); phase_l0consts(B)
        B.din('wff1', [2, 64, 128, 2048]); B.din('wff2', [2, 2, 16, 128, 4096])
        xp = B.din('xp', [128, 16, 1024]); yo = B.dout('yp', [128, 16, 1024])
        ost = B.dout('ost', [2, 128, 8, 32])
        ps_ = Pass(B, 'P', 1024, 0)
        load_x(B, ps_, xp, 'xp')
        modulate(B, ps_, 0, 0)
        def s5out(j, d, hu, husw):
            s = B.nscr()
            B.K.copy(V(s.t[:, 0:32].rearrange('p (g s c) -> p g s c', g=8, c=1), s.v().pg), hu)
            B.K.dma(DR(ost[d, :, j, :], 'ost'), s.v(slice(0, 32)))
        mixer0(B, ps_, [(256 * i, 256) for i in range(4)], s5out=s5out)
        post_ln(B, ps_, 0, 0)
        store_x(B, ps_, yo, 'yp')
        return
    if stage == 'l1c':
        phase_l1consts(B)
        B.dump('neglam', B.neglam); B.dump('subg', B.subg)
        return
    if stage.startswith('l1p'):
        if ':' in stage: B.cut = int(stage.split(':')[1])
        phase_l1consts(B)
        xp = B.din('xp1', [128, 16, 1024]); yo = B.dout('yp', [128, 16, 1024])
        okT = B.dout('okT', [2, 16, 64, 1024]); ov = B.dout('ov', [1024, 16, 128])
        ps_ = Pass(B, 'P', 1024, 0)
        load_x(B, ps_, xp, 'xp1')
        modulate(B, ps_, 1, 0)
        attn_p(B, ps_, okT, ov)
        post_ln(B, ps_, 1, 0)
        store_x(B, ps_, yo, 'yp')
        return
    if stage == 'ff':
        B.din('wff1', [2, 64, 128, 2048]); B.din('wff2', [2, 2, 16, 128, 4096])
        xp = B.din('xp', [128, 16, 1024]); yo = B.dout('yp', [128, 16, 1024])
        ps_ = Pass(B, 'P', 1024, 0)
        load_x(B, ps_, xp, 'xp')
        modulate(B, ps_, 0, 1)
        ffn(B, ps_, 0)
        post_ln(B, ps_, 0, 1)
        store_x(B, ps_, yo, 'yp')
        return


TWO_PI = 2.0 * math.pi
MAGIC = 12582912.0
CW1 = 6.28125
CW2 = TWO_PI - 6.28125


def range_reduce(B, out, x, tmp, shift=0.0):
    K = B.K
    if shift != 0.0:
        K.ts(out, x, shift, ALU.add)
        x = out
    K.ts(tmp, x, 1.0 / TWO_PI, ALU.mult, MAGIC, ALU.add)
    K.ts(tmp, tmp, MAGIC, ALU.subtract)
    K.stt(out, tmp, -CW1, x, ALU.mult, ALU.add)
    K.stt(out, tmp, -CW2, out, ALU.mult, ALU.add)


def phase_s5prep(B):
    for _ in phase_s5prep_gen(B): pass


def phase_s5prep_gen(B):
    K = B.K; nc = B.nc
    sA = B.din('s5A', [128, 3, 2, 64])
    sBq = B.din('s5Bq', [128, 2, 2, 64, 16])
    sC = B.din('s5C', [128, 2, 2, 64, 16])
    sK = B.din('s5K', [128, 16 + 128 + 64 + 256])
    B.s5w1 = B.dscr('s5w1', [2, 8, 128, 8 * 128], BF16)
    B.s5w3 = B.dscr('s5w3', [2, 8, 128, 8 * 128], BF16)
    B.s5w1s = B.dscr('s5w1s', [2, 8, 128, 8 * 128], BF16)
    B.s5w2 = B.dscr('s5w2', [8, 128, 8 * 128], BF16)
    B.s5rot = B.dscr('s5rot', [2, 2, 128, 64 * 128], F32)
    B.s5R = B.ct('s5R', [128, 2, 64])
    o = [X_OFF]
    def tk(name, shape, dt=F32):
        n = 2 if dt == BF16 else 4
        for s in shape[1:]: n *= s
        if n >= 512: o[0] = (o[0] + 511) // 512 * 512
        t = T(nc, K.name(name), shape, dt, o[0]); o[0] += (n + 31) // 32 * 32
        assert o[0] <= WR_OFF, ('s5prep overflow', name, o[0])
        return t
    A = tk('sA', [128, 3, 2, 64]); Bq = tk('sBq', [128, 2, 2, 64, 16]); Cc = tk('sC', [128, 2, 2, 64, 16])
    Kc = tk('sK', [128, 464])
    K.dma(V(A.t[:].rearrange('p a b c -> p (a b c)'), A.v().pg), DR(sA.ap().rearrange('p a b c -> p (a b c)'), 's5A'))
    K.dma(V(Bq.t[:].rearrange('p a b c d -> p (a b c d)'), Bq.v().pg), DR(sBq.ap().rearrange('p a b c d -> p (a b c d)'), 's5Bq'))
    K.dma(V(Cc.t[:].rearrange('p a b c d -> p (a b c d)'), Cc.v().pg), DR(sC.ap().rearrange('p a b c d -> p (a b c d)'), 's5C'))
    K.dma(Kc.v(), DR(sK.ap(), 's5K'))
    o_mark = o[0]
    sgn = tk('sgn', [128, 1]); K.memset(sgn.v(), 1.0); K.memset(sgn.v(p=slice(64, 128)), -1.0)
    nsg = tk('nsg', [128, 1]); K.memset(nsg.v(), -1.0); K.memset(nsg.v(p=slice(64, 128)), 1.0)

    def basics(src, ng, pre):
        dt = tk(pre + 'dt', [128, 2, ng]); a = tk(pre + 'a', [128, 2, ng]); th = tk(pre + 'th', [128, 2, ng])
        K.act(dt.v(), src.v(2), AF.Exp)
        K.tt(a.v(), src.v(0), dt.v(), ALU.mult)
        K.tt(th.v(), src.v(1), dt.v(), ALU.mult)
        return dt, a, th
    dtA, aA, thA = basics(A, 64, 'A')
    o_mark2 = o[0]
    NG = 2 * 64
    flatA = lambda t: V(t.t[:].rearrange('p a b -> p (a b)'), t.v().pg)
    amk = tk('amk', [128, 2, 64, 16]); ank = tk('ank', [128, 2, 64, 16]); tmpk = tk('tmpk', [128, 2, 64, 16])
    pwr = tk('pwr', [128, 2, 64, 16]); pwi = tk('pwi', [128, 2, 64, 16])
    kv = V(Kc.t[:, 0:16].unsqueeze(1).unsqueeze(1).to_broadcast([128, 2, 64, 16]), Kc.v().pg)
    bc16 = lambda t: V(t.t[:].unsqueeze(3).to_broadcast([128, 2, 64, 16]), t.v().pg)
    K.tt(amk.v(), bc16(aA), kv, ALU.mult)
    K.tt(ank.v(), bc16(thA), kv, ALU.mult)
    K.act(amk.v(), amk.v(), AF.Exp)
    f4 = lambda t: V(t.t[:].rearrange('p a b c -> p (a b c)'), t.v().pg)
    range_reduce(B, f4(pwi), f4(ank), f4(tmpk))
    K.act(pwi.v(), pwi.v(), AF.Sin)
    range_reduce(B, f4(pwr), f4(ank), f4(tmpk), shift=math.pi / 2)
    K.act(pwr.v(), pwr.v(), AF.Sin)
    K.tt(pwr.v(), pwr.v(), amk.v(), ALU.mult)
    K.tt(pwi.v(), pwi.v(), amk.v(), ALU.mult)
    lbr = tk('lbr', [128, 2, 64]); lbi = tk('lbi', [128, 2, 64]); den = tk('den', [128, 2, 64]); t1 = tk('t1', [128, 2, 64])
    cre = tk('cre', [128, 2, 64]); cim = tk('cim', [128, 2, 64])
    K.ts(lbr.v(), pwr.v(slice(None), slice(None), 8), -1.0, ALU.add)
    K.copy(lbi.v(), pwi.v(slice(None), slice(None), 8))
    K.tt(den.v(), A.v(0), A.v(0), ALU.mult); K.tt(t1.v(), A.v(1), A.v(1), ALU.mult); K.tt(den.v(), den.v(), t1.v(), ALU.add)
    K.recip(den.v(), den.v())
    K.tt(cre.v(), lbr.v(), A.v(0), ALU.mult); K.tt(t1.v(), lbi.v(), A.v(1), ALU.mult); K.tt(cre.v(), cre.v(), t1.v(), ALU.add)
    K.tt(cre.v(), cre.v(), den.v(), ALU.mult)
    K.tt(cim.v(), lbi.v(), A.v(0), ALU.mult); K.tt(t1.v(), lbr.v(), A.v(1), ALU.mult); K.tt(cim.v(), cim.v(), t1.v(), ALU.subtract)
    K.tt(cim.v(), cim.v(), den.v(), ALU.mult)
    scim = tk('scim', [128, 2, 64]); K.ts(scim.v(), cim.v(), sgn.v(), ALU.mult)
    BbA = tk('BbA', [128, 2, 64, 16]); BbB = tk('BbB', [128, 2, 64, 16]); t16 = T(nc, K.name('t16'), [128, 2, 64, 16], F32, tmpk.off)
    bcp = lambda t: V(t.t[:].unsqueeze(3).to_broadcast([128, 2, 64, 16]), t.v().pg)
    K.tt(BbA.v(), Bq.v(0), bcp(cre), ALU.mult); K.tt(t16.v(), Bq.v(1), bcp(scim), ALU.mult); K.tt(BbA.v(), BbA.v(), t16.v(), ALU.subtract)
    K.tt(BbB.v(), Bq.v(1), bcp(cre), ALU.mult); K.tt(t16.v(), Bq.v(0), bcp(scim), ALU.mult); K.tt(BbB.v(), BbB.v(), t16.v(), ALU.add)
    nspwi = tk('nspwi', [128, 2, 64, 16]); spwr = tk('spwr', [128, 2, 64, 16]); npwi = tk('npwi', [128, 2, 64, 16])
    K.ts(nspwi.v(), pwi.v(), nsg.v(), ALU.mult)
    K.ts(spwr.v(), pwr.v(), sgn.v(), ALU.mult)
    K.ts(npwi.v(), pwi.v(), -1.0, ALU.mult)
    def ksl(k0, step):
        a0 = k0 + 7; a1 = a0 + 8 * step
        return slice(a0, a1 if a1 >= 0 else None, step)
    Z = [tk('Z%d' % i, [128, 8, 8, 16]) for i in range(4)]; m2 = tk('m2', [128, 8, 8, 16])
    Zb = tk('Zb', [128, 8, 8, 16], BF16)
    w1sb = tk('w1sb', [128, 8, 128], BF16); w1sw = tk('w1sw', [128, 8, 128], BF16); w2acc = tk('w2acc', [128, 8, 128]); w2sb = tk('w2sb', [128, 8, 128], BF16)

    def outer(dst, A1, A2, B1, B2, d, gs, ks):
        a1 = V(A1.t[:, d, gs, ks].unsqueeze(3).to_broadcast([128, 8, 8, 16]), A1.v().pg)
        a2 = V(A2.t[:, d, gs, ks].unsqueeze(3).to_broadcast([128, 8, 8, 16]), A2.v().pg)
        b1 = V(B1.t[:, d, gs, :].unsqueeze(2).to_broadcast([128, 8, 8, 16]), B1.v().pg)
        b2 = V(B2.t[:, d, gs, :].unsqueeze(2).to_broadcast([128, 8, 8, 16]), B2.v().pg)
        K.tt(dst.v(), a1, b1, ALU.mult); K.tt(m2.v(), a2, b2, ALU.mult); K.tt(dst.v(), dst.v(), m2.v(), ALU.add)

    C1 = lambda: None
    Cw = Cc
    class Sub:
        def __init__(s, t, i): s.t = t.t[:, i]; s._t = t
        def v(s): return s._t.v()
    for gch in range(8):
        gs = slice(8 * gch, 8 * gch + 8)
        for d in range(2):
            ks = ksl(7, -1) if d == 0 else ksl(0, 1)
            outer(Z[0], pwr, nspwi, BbA, BbB, d, gs, ks)
            ps = [B.nps(), B.nps()]
            for gl in range(8):
                K.mm(ps[gl // 4].v(slice(128 * (gl % 4), 128 * (gl % 4) + 128)),
                     V(Z[0].t[:, gl].rearrange('p a b -> p (a b)'), Z[0].v().pg), B.ident_f())
            for h in range(2):
                K.act(V(w1sb.t[:, 4 * h:4 * h + 4, :].rearrange('p a b -> p (a b)'), w1sb.v().pg), ps[h].v(), AF.Copy)
                pv = ps[h].t[:, :].rearrange('p (a b) -> p a b', b=128)
                K.copy(V(w1sw.t[:, 4 * h:4 * h + 4, 0:64], w1sw.v().pg), V(pv[:, :, 64:128], ps[h].v().pg))
                K.copy(V(w1sw.t[:, 4 * h:4 * h + 4, 64:128], w1sw.v().pg), V(pv[:, :, 0:64], ps[h].v().pg))
            K.dma(DR(B.s5w1[d, gch], 's5w1'), V(w1sb.t[:].rearrange('p a b -> p (a b)'), w1sb.v().pg))
            K.dma(DR(B.s5w1s[d, gch], 's5w1s'), V(w1sw.t[:].rearrange('p a b -> p (a b)'), w1sw.v().pg))
            ks = ksl(1, 1) if d == 0 else ksl(8, -1)
            outer(Z[1], spwr, npwi, Sub(Cw, 0), Sub(Cw, 1), d, gs, ks)
            K.act(Zb.v(), Z[1].v(), AF.Copy)
            K.dma(DR(B.s5w3[d, gch], 's5w3'), V(Zb.t[:].rearrange('p g a b -> p (g a b)'), Zb.v().pg))
            outer(Z[2], pwr, nspwi, BbA, BbB, d, gs, ksl(0, -1) if d == 0 else ksl(0, 1))
            outer(Z[3], spwr, npwi, Sub(Cw, 0), Sub(Cw, 1), d, gs, ksl(0, 1) if d == 0 else ksl(0, -1))
            ps = [B.nps(), B.nps()]
            for gl in range(8):
                K.mm(ps[gl // 4].v(slice(128 * (gl % 4), 128 * (gl % 4) + 128)),
                     V(Z[2].t[:, gl].rearrange('p a b -> p (a b)'), Z[2].v().pg),
                     V(Z[3].t[:, gl].rearrange('p a b -> p (a b)'), Z[3].v().pg))
            mk = V(Kc.t[:, 208 + 128 * d:208 + 128 * d + 128].unsqueeze(1).to_broadcast([128, 4, 128]), Kc.v().pg)
            for h in range(2):
                pv = V(ps[h].t[:, :].rearrange('p (a b) -> p a b', b=128), ps[h].v().pg)
                dv = V(w2acc.t[:, 4 * h:4 * h + 4, :], w2acc.v().pg)
                if d == 0:
                    K.tt(dv, pv, mk, ALU.mult)
                else:
                    t = V(m2.t[:, 0:4].rearrange('p a b c -> p a (b c)'), m2.v().pg)
                    K.tt(t, pv, mk, ALU.mult)
                    K.tt(dv, dv, t, ALU.add)
        for gl in range(8):
            g = 8 * gch + gl
            K.stt(w2sb.v(gl), B.ident_f(), Kc.v(slice(144 + g, 145 + g)), w2acc.v(gl), ALU.mult, ALU.add)
        K.dma(DR(B.s5w2[gch], 's5w2'), V(w2sb.t[:].rearrange('p a b -> p (a b)'), w2sb.v().pg))
        yield
    o[0] = o_mark2
    K.act(B.s5R.v(), aA.v(), AF.Exp, scale=8.0)
    ph = tk('ph', [128, 2, 64]); pt = tk('pt', [128, 2, 64])
    K.ts(ph.v(), thA.v(), 8.0, ALU.mult)
    range_reduce(B, flatA(ph), flatA(ph), flatA(pt))
    arg = tk('arg', [128, 64, 128]); red = tk('red', [128, 64, 128]); tmp = tk('rtmp', [128, 64, 128])
    f3 = lambda t: V(t.t[:].rearrange('p a b -> p (a b)'), t.v().pg)
    cvv = V(Kc.t[:, 16:144].unsqueeze(1).to_broadcast([128, 64, 128]), Kc.v().pg)
    halfpi = tk('halfpi', [128, 1]); K.memset(halfpi.v(), math.pi / 2)
    for d in range(2):
        K.tt(arg.v(), V(ph.t[:, d, :].unsqueeze(2).to_broadcast([128, 64, 128]), ph.v().pg), cvv, ALU.mult)
        range_reduce(B, f3(red), f3(arg), f3(tmp))
        K.act(tmp.v(), red.v(), AF.Sin, scale=sgn.v())
        K.dma(DR(B.s5rot[d, 1], 's5rot'), f3(tmp))
        K.act(red.v(), red.v(), AF.Abs)
        K.act(red.v(), red.v(), AF.Sin, scale=-1.0, bias=halfpi.v())
        K.dma(DR(B.s5rot[d, 0], 's5rot'), f3(red))
        yield
    if 's5prep' in B.dbg:
        for nm, src, shp, dt_ in (('w1', B.s5w1, [2 * 8 * 128, 1024], BF16), ('w3', B.s5w3, [2 * 8 * 128, 1024], BF16),
                                  ('w2', B.s5w2, [8 * 128, 1024], BF16), ('rot', B.s5rot, [4 * 128, 8192], F32)):
            o_ = B.dout('dbg_' + nm, shp, dt_)
            nd = len(src.shape)
            names = 'abcdefg'[:nd - 1]
            pat = ' '.join(names) + ' x -> (' + ' '.join(names) + ') x'
            K.dma(DR(o_.ap(), 'dbg_' + nm), DR(src.ap().rearrange(pat), {'w1': 's5w1', 'w3': 's5w3', 'w2': 's5w2', 'rot': 's5rot'}[nm]))
        B.dump('s5R', B.s5R)


def host_s5(inp, H):
    f = np.float32
    lre = inp['s5_lambda_re'][0]; lim = inp['s5_lambda_im'][0]; ldt = inp['s5_log_dt'][0]
    def layA(x):
        y = x.transpose(2, 0, 1)
        return np.concatenate([y, y], 0)
    ldtA = np.broadcast_to(ldt[None], (128, 2, 64))
    H['s5A'] = np.ascontiguousarray(np.stack([layA(lre), layA(lim), ldtA], 1)).astype(f)
    bre = inp['s5_b_re'][0].transpose(2, 0, 1, 3); bim = inp['s5_b_im'][0].transpose(2, 0, 1, 3)
    Q1 = np.concatenate([bre, bim], 0); Q2 = np.concatenate([bim, bre], 0)
    H['s5Bq'] = np.ascontiguousarray(np.stack([Q1, Q2], 1)).astype(f)
    cre = inp['s5_c_re'][0].transpose(3, 0, 1, 2); cim = inp['s5_c_im'][0].transpose(3, 0, 1, 2)
    C1 = np.concatenate([cre, cim], 0); C2 = np.concatenate([cim, cre], 0)
    H['s5C'] = np.ascontiguousarray(np.stack([C1, C2], 1)).astype(f)
    kv = np.arange(-7, 9, dtype=f); cv = np.arange(1, 129, dtype=f)
    dcol = np.tile(inp['s5_d'][0].reshape(64, 16).T, (8, 1))
    s_ = np.arange(128) // 16
    mF = (s_[:, None] <= s_[None, :]).astype(f); mB = (s_[:, None] >= s_[None, :]).astype(f)
    H['s5K'] = np.ascontiguousarray(np.concatenate([np.broadcast_to(kv, (128, 16)), np.broadcast_to(cv, (128, 128)), dcol, mF, mB], 1)).astype(f)


def load_small(B, t, dram, name):
    nd = len(t.shape)
    if nd > 2:
        pat = _flat(nd)
        B.K.dma(V(t.t[:].rearrange(pat), t.v().pg), DR(dram.ap(), name))
    else:
        B.K.dma(t.v(), DR(dram.ap(), name))


def phase_l0consts(B):
    cp = B.din('convp', [128, 8 * 31 + 8 * 4])
    B.convp = B.ct('convp', [128, 8 * 31 + 32])
    load_small(B, B.convp, cp, 'convp')
    B.wdw = lambda j: V(B.convp.t[:, 31 * j:31 * j + 31], B.convp.v().pg)
    B.cpar = lambda which, j: B.convp.v(slice(248 + 8 * which + j, 249 + 8 * which + j))
    B.din('wide', [128, 8 * 240])
    B.din('win', [24, 128, 2048]); B.din('wglu', [8, 128, 1024]); B.din('woab', [16, 128, 2048])


def mixer0(B, ps_, segs, halo=None, s5init=None, s5out=None, dirs=(0, 1), pre=False):
    K = B.K; nc = B.nc; X = ps_.X; H = ps_.H; NT = ps_.ntok
    win = B.ins['win']; wglu = B.ins['wglu']; woab = B.ins['woab']
    units = []
    if not pre:
        for j in range(8):
            units += [(DR(win[j], 'win'), 2048), (DR(win[8 + j], 'win'), 2048)]
    units += [(DR(win[16 + j], 'win'), 2048) for j in range(8)]
    if not pre:
        units += [(DR(wglu[m], 'wglu'), 1024) for m in range(8)]
        units += [(DR(woab[o], 'woab'), 2048) for o in range(16)]
    B.wplan(units)
    o = [BIG_OFF]
    def tk(name, shape, dt=F32):
        n = 2 if dt == BF16 else 4
        for s in shape[1:]: n *= s
        if n >= 512: o[0] = (o[0] + 511) // 512 * 512
        t = T(nc, K.name(name), shape, dt, o[0]); o[0] += (n + 31) // 32 * 32
        assert o[0] <= WR_OFF, (name, o[0] - BIG_OFF)
        return t
    US = tk('US', [128, 8, NT], BF16)
    o_s5 = o[0]
    segoff = []; tot = 0
    for (c0, L) in segs:
        segoff.append(tot); tot += L + 30
    UC = tk('UC', [128, 8, tot], BF16)
    YP = tk('YP', [128, 8, 512])
    Dg = tk('Dg', [128, 31, 128], BF16)
    Dgs = [Dg, T(nc, K.name('Dg2'), [128, 31, 128], BF16, H_OFF + 8 * NT * 2)]
    if not pre: K.memset(UC.v(), 0.0)

    def seg_parts(c0, n):
        out = []
        for si, (s0, L) in enumerate(segs):
            a = max(c0, s0); b = min(c0 + n, s0 + L)
            if a < b: out.append((si, a, b))
        return out
    for j in range(0 if pre else 8):
        slv = B.wnext()
        pvs = []
        for (c0, n) in ps_.tiles:
            cs = slice(c0, c0 + n); pv = B.nps(); pvs.append(pv)
            for kc in range(16):
                K.mm(pv.v(slice(0, n)), wv(slv, kc), H.v(kc, cs), start=(kc == 0), stop=(kc == 15))
        hps = []
        if halo is not None:
            for side in ('left', 'right'):
                hv = halo.get(side)
                if hv is None: continue
                pv = B.nps(); hps.append((side, hv, pv))
                for kc in range(16):
                    K.mm(pv.v(slice(0, 15)), wv(slv, kc), hv(kc), start=(kc == 0), stop=(kc == 15))
        slg = B.wnext()
        for pv, (c0, n) in zip(pvs, ps_.tiles):
            cs = slice(c0, c0 + n); pg = B.nps()
            for kc in range(16):
                K.mm(pg.v(slice(0, n)), wv(slg, kc), H.v(kc, cs), start=(kc == 0), stop=(kc == 15))
            sg = B.nscr()
            K.act(sg.v(slice(0, n)), pg.v(slice(0, n)), AF.Sigmoid)
            for (si, a, b) in seg_parts(c0, n):
                d0 = segoff[si] + 15 + (a - segs[si][0])
                K.tt(UC.v(j, slice(d0, d0 + (b - a))), pv.v(slice(a - c0, b - c0)), sg.v(slice(a - c0, b - c0)), ALU.mult)
        for (side, hv, pv) in hps:
            pg = B.nps()
            for kc in range(16):
                K.mm(pg.v(slice(0, 15)), wv(slg, kc), hv(kc), start=(kc == 0), stop=(kc == 15))
            sg = B.nscr()
            K.act(sg.v(slice(0, 15)), pg.v(slice(0, 15)), AF.Sigmoid)
            d0 = 0 if side == 'left' else tot - 15
            K.tt(UC.v(j, slice(d0, d0 + 15)), pv.v(slice(0, 15)), sg.v(slice(0, 15)), ALU.mult)
    if B.cut <= 1: return
    for j in range(8):
        sl = B.wnext()
        for ti, (c0, n) in enumerate(ps_.tiles):
            cs = slice(c0, c0 + n); ps = B.nps()
            for kc in range(16):
                K.mm(ps.v(slice(0, n)), wv(sl, kc), H.v(kc, cs), start=(kc == 0), stop=(kc == 15))
            if ti % 2 == 0: K.act(US.v(j, cs), ps.v(slice(0, n)), AF.Copy)
            else: K.copy(US.v(j, cs), ps.v(slice(0, n)))
    if B.cut <= 2: return
    for (c0, n) in ([] if pre else ps_.tiles):
        for j in range(8):
            Dg = Dgs[j % 2]
            if True:
                K.tt(Dg.v(), V(B.cstb.t[:, 1, :].unsqueeze(1).to_broadcast([128, 31, 128]), B.cstb.v().pg),
                     V(B.wdw(j).ap.unsqueeze(2).to_broadcast([128, 31, 128]), B.convp.v().pg), ALU.mult)
            for (si, a, b) in seg_parts(c0, n):
                ps = B.nps(); m = b - a
                base = segoff[si] + (a - segs[si][0])
                for tau in range(31):
                    K.mm(ps.v(slice(0, m)), Dg.v(tau), UC.v(j, slice(base + tau, base + tau + m)), start=(tau == 0), stop=(tau == 30))
                K.act(YP.v(j, slice(a - c0, b - c0)), ps.v(slice(0, m)), AF.Identity, bias=B.cpar(0, j))
        layer_norm(B, YP, 8, [(0, n)], lambda kc: B.cpar(1, kc), lambda kc: B.cpar(2, kc), B.eps_raw.v(),
                   act_func=AF.Silu, out=_ColShift(H, c0))
    B.dump_bf('yconv', H, 8)
    if B.cut <= 3: return
    o[0] = o_s5
    s5_core(B, ps_, US, segs, tk, s5init, s5out, dirs)
    B.dump_bf('ys', US, 8)
    if pre: return
    if B.cut <= 9: return
    for m in range(8):
        sl = B.wnext()
        for (c0, n) in ps_.tiles:
            cs = slice(c0, c0 + n); ps = B.nps()
            for kc in range(8):
                K.mm(ps.v(slice(0, n)), wv(sl, kc), US.v(kc, cs), start=(kc == 0), stop=(kc == 7))
            sg = B.nscr()
            K.act(sg.v(slice(0, n)), ps.v(slice(0, n)), AF.Sigmoid)
            K.tt(H.v(8 + m, cs), US.v(m, cs), sg.v(slice(0, n)), ALU.mult)
    for oc in range(16):
        sl = B.wnext()
        for (c0, n) in ps_.tiles:
            cs = slice(c0, c0 + n); ps = B.nps()
            for kc in range(16):
                K.mm(ps.v(slice(0, n)), wv(sl, kc), H.v(kc, cs), start=(kc == 0), stop=(kc == 15))
            K.stt(X.v(oc, cs), ps.v(slice(0, n)), mcol(B, 0, 2, oc, ps_.vec), X.v(oc, cs), ALU.mult, ALU.add)


class _ColShift:
    def __init__(self, t, c0): self.t = t; self.c0 = c0
    def v(self, kc, cs):
        return self.t.v(kc, slice(cs.start + self.c0, cs.stop + self.c0))


def s5_core(B, ps_, US, segs, tk, s5init, s5out, dirs):
    K = B.K; NT = ps_.ntok
    NS = len(segs); NCC = segs[0][1] // 8; NC = NT // 8
    assert NC == 128
    hoff = H_OFF + 8 * NT * 2
    ROTS = [T(B.nc, K.name('rot'), [128, 2, 8, 128], F32, hoff + 8192 * i) for i in range(2)]
    WIDE = tk('wide', [128, 8, 240], BF16)
    K.dma(V(WIDE.t[:].rearrange('p a b -> p (a b)'), WIDE.v().pg), DR(B.ins['wide'].ap(), 'wide'), eng='pool')
    W1S = [tk('W1s%d' % i, [128, 8, 128], BF16) for i in range(2)]; W1SS = [tk('W1ss%d' % i, [128, 8, 128], BF16) for i in range(2)]
    W3 = tk('W3s', [128, 2, 8, 128], BF16); W2 = tk('W2s', [128, 8, 128], BF16)
    Ush = tk('Ush', [128, 8, 128], BF16); Ysh = tk('Ysh', [128, 8, 128], BF16)
    Tt = tk('Tt', [128, 8, 128]); Ts = tk('Ts', [128, 8, 128]); Qt = tk('Qt', [128, 8, 128]); Qs = tk('Qs', [128, 8, 128])
    Rt = tk('Rt', [128, 8, 128]); Hbf = tk('Hbf', [128, 2, 8, 128], BF16)
    f3 = lambda t: V(t.t[:].rearrange('p a b -> p (a b)'), t.v().pg)
    def v4(t, gs=slice(None), rev=False, sl_c=slice(None)):
        a = t.t[:, gs].rearrange('p g (s c) -> p g s c', c=NCC)
        if rev: a = a[:, :, :, ::-1]
        return V(a[:, :, :, sl_c], t.v().pg)
    def p4(ps, rev=False):
        a = ps.t[:, :].rearrange('p (g s c) -> p g s c', g=4, c=NCC)
        if rev: a = a[:, :, :, ::-1]
        return V(a, ps.v().pg)
    def bc1(t, idx):
        a = t[idx]
        return a.unsqueeze(2).unsqueeze(3).to_broadcast([128, 8, NS, 1])
    its = [(j, d) for j in range(8) for d in dirs]
    def loads(k):
        j, d = its[k]
        K.dma(V(W1S[k % 2].t[:].rearrange('p a b -> p (a b)'), W1S[k % 2].v().pg), DR(B.s5w1[d, j], 's5w1'))
        K.dma(V(W1SS[k % 2].t[:].rearrange('p a b -> p (a b)'), W1SS[k % 2].v().pg), DR(B.s5w1s[d, j], 's5w1s'))
        K.dma(V(ROTS[k % 2].t[:].rearrange('p a b c -> p a (b c)'), ROTS[k % 2].v().pg),
              DR(B.s5rot[d, :, :, 1024 * j:1024 * j + 1024].rearrange('a p x -> p a x'), 's5rot'))
    loads(0)
    kk = -1
    for j in range(8):
        gsl = slice(8 * j, 8 * j + 8)
        K.dma(V(W2.t[:].rearrange('p a b -> p (a b)'), W2.v().pg), DR(B.s5w2[j], 's5w2'))
        if B.cut <= 4: return
        for half in range(2):
            ps = B.nps()
            for q in range(4):
                gl = 4 * half + q
                for s_ in range(8):
                    K.mm(ps.v(slice(128 * q, 128 * q + 128)), WIDE.v(gl, slice(112 - 16 * s_, 240 - 16 * s_)),
                         US.v(j, slice(s_, NT, 8)), start=(s_ == 0), stop=(s_ == 7))
            K.act(V(Ush.t[:, 4 * half:4 * half + 4, :].rearrange('p a b -> p (a b)'), Ush.v().pg), ps.v(), AF.Copy)
        if B.cut <= 5: return
        for d in dirs:
            kk += 1
            rev = (d == 1)
            W1 = W1S[kk % 2]; W1s = W1SS[kk % 2]; ROT = ROTS[kk % 2]
            def rot4(which, gs, sl_c=slice(0, NCC), ROT=ROT):
                a = ROT.t[:, which, gs, sl_c]
                return V(a.unsqueeze(2).to_broadcast([128, a.shape[1], NS, a.shape[-1]]), ROT.v().pg)
            K.dma(V(W3.t[:, d].rearrange('p a b -> p (a b)'), W3.v().pg), DR(B.s5w3[d, j], 's5w3'))
            pS = [B.nps(), B.nps()]; pW = [B.nps(), B.nps()]
            for gl in range(8):
                cs = slice(128 * (gl % 4), 128 * (gl % 4) + 128)
                K.mm(pS[gl // 4].v(cs), W1.v(gl), Ush.v(gl))
                K.mm(pW[gl // 4].v(cs), W1s.v(gl), Ush.v(gl))
            if kk + 1 < len(its): loads(kk + 1)
            if B.cut <= 6: return
            for h in range(2):
                gs = slice(4 * h, 4 * h + 4)
                cr = rot4(0, gs); ci = rot4(1, gs)
                K.tt(v4(Tt, gs), p4(pS[h], rev), cr, ALU.mult); K.tt(v4(Qt, gs), p4(pW[h], rev), ci, ALU.mult)
                K.tt(v4(Ts, gs), p4(pW[h], rev), cr, ALU.mult); K.tt(v4(Qs, gs), p4(pS[h], rev), ci, ALU.mult)
            K.tt(f3(Tt), f3(Tt), f3(Qt), ALU.add)
            K.tt(f3(Ts), f3(Ts), f3(Qs), ALU.subtract)
            Rv = V(B.s5R.t[:, d, gsl].unsqueeze(2).to_broadcast([128, 8, 128]), B.s5R.v().pg)
            K.copy(Rt.v(), Rv, eng='pool')
            K.memset(v4(Rt, sl_c=slice(0, 1)), 0.0, eng='pool')
            ini = s5init[d] if s5init is not None else None
            if ini is not None:
                for (tt_, it_) in ((Tt, ini[0]), (Ts, ini[1])):
                    iv = V(bc1(it_.t, (slice(None), gsl)), it_.v().pg)
                    rv = V(bc1(B.s5R.t, (slice(None), d, gsl)), B.s5R.v().pg)
                    K.tt(v4(Qt, sl_c=slice(0, 1)), iv, rv, ALU.mult)
                    K.tt(v4(tt_, sl_c=slice(0, 1)), v4(tt_, sl_c=slice(0, 1)), v4(Qt, sl_c=slice(0, 1)), ALU.add)
            K.scan(f3(Qt), f3(Rt), f3(Tt)); K.scan(f3(Qs), f3(Rt), f3(Ts))
            cr = rot4(0, slice(0, 8)); ci = rot4(1, slice(0, 8))
            K.tt(v4(Tt), v4(Qt), cr, ALU.mult); K.tt(v4(Ts), v4(Qs), ci, ALU.mult)
            K.tt(f3(Tt), f3(Tt), f3(Ts), ALU.subtract)
            if s5out is not None:
                lc = slice(NCC - 1, NCC)
                K.tt(v4(Ts, sl_c=lc), v4(Qs, sl_c=lc), rot4(0, slice(0, 8), lc), ALU.mult)
                K.tt(v4(Rt, sl_c=lc), v4(Qt, sl_c=lc), rot4(1, slice(0, 8), lc), ALU.mult)
                K.tt(v4(Ts, sl_c=lc), v4(Ts, sl_c=lc), v4(Rt, sl_c=lc), ALU.add)
                s5out(j, d, v4(Tt, sl_c=lc), v4(Ts, sl_c=lc))
            hb = Hbf.t[:, d].rearrange('p g (s c) -> p g s c', c=NCC)
            if rev: hb = hb[:, :, :, ::-1]
            K.act(V(hb[:, :, :, 1:NCC], Hbf.v().pg), v4(Tt, sl_c=slice(0, NCC - 1)), AF.Copy)
            if ini is not None:
                K.copy(V(hb[:, :, :, 0:1], Hbf.v().pg), V(bc1(ini[0].t, (slice(None), gsl)), ini[0].v().pg))
            else:
                K.memset(V(hb[:, :, :, 0:1], Hbf.v().pg), 0.0)
        if len(dirs) < 2:
            continue
        if B.cut <= 7: return
        for half in range(2):
            ps = B.nps()
            for q in range(4):
                gl = 4 * half + q
                cs = slice(128 * q, 128 * q + 128)
                K.mm(ps.v(cs), W3.v(0, gl), Hbf.v(0, gl), start=True, stop=False)
                K.mm(ps.v(cs), W3.v(1, gl), Hbf.v(1, gl), start=False, stop=False)
                K.mm(ps.v(cs), W2.v(gl), Ush.v(gl), start=False, stop=True)
            K.act(V(Ysh.t[:, 4 * half:4 * half + 4, :].rearrange('p a b -> p (a b)'), Ysh.v().pg), ps.v(), AF.Copy)
        if B.cut <= 8: return
        for hb_ in range(2):
            ps = B.nps()
            for q in range(4):
                s_ = 4 * hb_ + q
                for gl in range(8):
                    K.mm(ps.v(slice(128 * q, 128 * q + 128)), WIDE.v(s_, slice(112 - 16 * gl, 240 - 16 * gl)), Ysh.v(gl),
                         start=(gl == 0), stop=(gl == 7))
            dst = V(US.t[:, j, :].rearrange('p (c s) -> p c s', s=8)[:, :, 4 * hb_:4 * hb_ + 4], US.v(j).pg)
            src = V(ps.t[:, :].rearrange('p (s c) -> p c s', s=4), ps.v().pg)
            K.act(dst, src, AF.Gelu)


def host_l0(inp, H):
    f = np.float32
    wd = inp['w_dw'][0].reshape(31, 8, 128).transpose(2, 1, 0).reshape(128, 248)
    par = np.stack([inp['b_dw'][0], inp['conv_ln_g'][0], inp['conv_ln_b'][0], inp['s5_d'][0]], 0).reshape(4, 8, 128).transpose(2, 0, 1).reshape(128, 32)
    H['convp'] = np.ascontiguousarray(np.concatenate([wd, par], 1)).astype(f)
    wide = np.zeros((128, 8, 240), f)
    for a_ in range(8):
        for p in range(16):
            wide[16 * a_ + p, a_, 112 + p] = 1.0
    H['wide'] = np.ascontiguousarray(wide).reshape(128, 1920)
    H['win'] = units_fm(inp['w_in_ab'][0]); H['wglu'] = units_fm(inp['w_glu'][0]); H['woab'] = units_fm(inp['w_out_ab'][0])


def phase_l1consts(B):
    K = B.K
    ap_ = B.din('attp', [128, 257])
    B.attp = B.ct('attp', [128, 257])
    K.dma(B.attp.v(), DR(ap_.ap(), 'attp'))
    B.subg = B.ct('subg', [128, 1]); B.neglam = B.ct('neglam', [128, 1])
    K.ts(B.subg.v(), B.attp.v(slice(0, 1)), 1.0 - LAM_INIT, ALU.mult)
    pr = B.ct('lampr', [128, 2, 64]); sm = B.ct('lamsm', [128, 2])
    lv = B.attp.t[:, 1:257].rearrange('p (a b) -> p a b', b=64)
    K.tt(pr.v(), V(lv[:, 0::2, :], B.attp.v().pg), V(lv[:, 1::2, :], B.attp.v().pg), ALU.mult)
    K.reduce(sm.v(), pr.v())
    K.act(sm.v(), sm.v(), AF.Exp)
    K.tt(B.neglam.v(), sm.v(slice(1, 2)), sm.v(slice(0, 1)), ALU.subtract)
    K.ts(B.neglam.v(), B.neglam.v(), -LAM_INIT, ALU.add)
    B.din('wqk', [32, 128, 2048]); B.din('wv', [8, 128, 4096]); B.din('woc', [16, 128, 2048])


def attn_core1(B, n, O0, O1, S0, S1, sq, stg):
    K = B.K
    sn = slice(0, n)
    K.act(stg.v(0, sn), S0, AF.Ln)
    K.act(stg.v(2, sn), S1, AF.Ln)
    K.copy(stg.v(1, sn), O0); K.copy(stg.v(3, sn), O1)
    K.act(stg.v(0, sn), stg.v(0, sn), AF.Exp, scale=-1.0)
    K.act(stg.v(2, sn), stg.v(2, sn), AF.Exp, scale=-1.0)
    K.tt(stg.v(1, sn), stg.v(1, sn), stg.v(0, sn), ALU.mult)
    K.tt(stg.v(3, sn), stg.v(3, sn), stg.v(2, sn), ALU.mult)
    K.stt(sq['o'], stg.v(3, sn), B.neglam.v(), stg.v(1, sn), ALU.mult, ALU.add)
    K.tt(sq['s'], sq['o'], sq['o'], ALU.mult)


def attn_core2(B, n, sq, ms, dst):
    K = B.K
    sn = slice(0, n)
    K.mm(ms.v(sn), B.ones_f(), sq['s'])
    K.act(sq['s'], ms.v(sn), AF.Ln, scale=1.0 / 128, bias=B.eps_raw.v())
    K.act(sq['s'], sq['s'], AF.Exp, scale=-0.5)
    K.stt(dst, sq['o'], B.subg.v(), sq['s'], ALU.mult, ALU.mult)


def attn_p(B, ps_, okT, ov):
    K = B.K; nc = B.nc; X = ps_.X; H = ps_.H; NT = ps_.ntok
    wqk = B.ins['wqk']; wvv = B.ins['wv']; woc = B.ins['woc']
    units = []
    for j in range(8):
        units += [(DR(wqk[j], 'wqk'), 2048), (DR(wqk[8 + j], 'wqk'), 2048), (DR(wqk[16 + j], 'wqk'), 2048),
                  (DR(wqk[24 + j], 'wqk'), 2048), (DR(wvv[j], 'wv'), 4096)]
    units += [(DR(woc[o_], 'woc'), 2048) for o_ in range(16)]
    B.wplan(units)
    o = [BIG_OFF]
    def tk(name, shape, dt=F32):
        nb = 2 if dt == BF16 else 4
        for s in shape[1:]: nb *= s
        if nb >= 512: o[0] = (o[0] + 511) // 512 * 512
        t = T(nc, K.name(name), shape, dt, o[0]); o[0] += (nb + 31) // 32 * 32
        assert o[0] <= WR_OFF, (name, o[0] - BIG_OFF)
        return t
    ATT = tk('ATT', [128, 16, NT], BF16)
    QM = tk('QM', [128, 2, 2, NT], BF16)
    KB = tk('KB', [128, 2, NT], BF16)
    Vb = tk('Vb', [128, NT // 128, 256], BF16)
    PT = tk('PT', [128, 2, 2, 512], BF16)
    STG = tk('STG', [128, 4, 256])
    K.memset(QM.v(), 0.0)
    sqset = lambda i, n: {'o': B.scr[3 + 2 * i].v(slice(0, n)), 's': B.scr[4 + 2 * i].v(slice(0, n))}
    for j in range(8):
        for m in range(2):
            sl = B.wnext()
            for (c0, n) in ps_.tiles:
                cs = slice(c0, c0 + n); ps = B.nps()
                for kc in range(16):
                    K.mm(ps.v(slice(0, n)), wv(sl, kc), H.v(kc, cs), start=(kc == 0), stop=(kc == 15))
                K.act(QM.v(m, 0, cs, p=slice(0, 64)), ps.v(slice(0, n), p=slice(0, 64)), AF.Copy)
                K.copy(QM.v(m, 1, cs, p=slice(64, 128)), ps.v(slice(0, n), p=slice(64, 128)))
        if B.cut <= 1: continue
        for m in range(2):
            sl = B.wnext()
            for (c0, n) in ps_.tiles:
                cs = slice(c0, c0 + n); ps = B.nps()
                for kc in range(16):
                    K.mm(ps.v(slice(0, n)), wv(sl, kc), H.v(kc, cs), start=(kc == 0), stop=(kc == 15))
                K.act(KB.v(m, cs), ps.v(slice(0, n)), AF.Copy)
                kf = B.nscr()
                K.copy(kf.v(slice(0, n)), ps.v(slice(0, n)))
                K.dma(DR(okT[m, 2 * j:2 * j + 2].rearrange('h d t -> (h d) t')[:, c0:c0 + n], 'okT'), kf.v(slice(0, n)))
        if B.cut <= 2: continue
        sl = B.wnext()
        for tt_ in range(NT // 128):
            ps = B.nps()
            for kc in range(16):
                K.mm(ps.v(slice(0, 256)), H.v(kc, slice(128 * tt_, 128 * tt_ + 128)), wv(sl, kc, 256), start=(kc == 0), stop=(kc == 15))
            K.act(Vb.v(tt_), ps.v(slice(0, 256)), AF.Copy)
            vf = B.nscr()
            K.copy(vf.v(slice(0, 256)), ps.v(slice(0, 256)))
            K.dma(DR(ov[128 * tt_:128 * tt_ + 128, 2 * j:2 * j + 2, :].rearrange('t h d -> t (h d)'), 'ov'), vf.v(slice(0, 256)))
        units_l = [(sq_, h2) for sq_ in range(NT // 256) for h2 in range(2)]
        nu = len(units_l)

        def partA(u):
            sq_, h2 = units_l[u]
            qc = slice(256 * sq_, 256 * sq_ + 256)
            for m in range(2):
                pss = B.ps[2 * (u % 2) + m]
                for kt in range(2):
                    K.mm(pss.v(slice(256 * kt, 256 * kt + 256)), KB.v(m, slice(256 * sq_ + 128 * kt, 256 * sq_ + 128 * kt + 128)),
                         QM.v(m, h2, qc))
                K.act(PT.v(u % 2, m), pss.v(), AF.Exp, scale=0.125)

        def partB(u):
            sq_, h2 = units_l[u]
            psS = B.ps[4]; psO = B.ps[5]
            for m in range(2):
                for kt in range(2):
                    K.mm(psS.v(slice(256 * m, 256 * m + 256)), B.ones_b(), PT.v(u % 2, m, slice(256 * kt, 256 * kt + 256)),
                         start=(kt == 0), stop=(kt == 1))
                for kt in range(2):
                    K.mm(psO.v(slice(256 * m, 256 * m + 256)), Vb.v(2 * sq_ + kt, slice(128 * h2, 128 * h2 + 128)),
                         PT.v(u % 2, m, slice(256 * kt, 256 * kt + 256)), start=(kt == 0), stop=(kt == 1))
            attn_core1(B, 256, psO.v(slice(0, 256)), psO.v(slice(256, 512)), psS.v(slice(0, 256)), psS.v(slice(256, 512)), sqset(u % 2, 256), STG)

        def partC(u):
            sq_, h2 = units_l[u]
            qc = slice(256 * sq_, 256 * sq_ + 256)
            attn_core2(B, 256, sqset(u % 2, 256), B.ps[6], ATT.v(2 * j + h2, qc))
        partA(0)
        for u in range(nu):
            if u + 1 < nu: partA(u + 1)
            partB(u)
            if u >= 1: partC(u - 1)
        partC(nu - 1)
    B.dump_bf('att', ATT, 16)
    if B.cut <= 5: return
    out_proj(B, ps_, ATT, 1)


def out_proj(B, ps_, A, l):
    K = B.K; X = ps_.X
    for oc in range(16):
        sl = B.wnext()
        for (c0, n) in ps_.tiles:
            cs = slice(c0, c0 + n); ps = B.nps()
            for kc in range(16):
                K.mm(ps.v(slice(0, n)), wv(sl, kc), A.v(kc, cs), start=(kc == 0), stop=(kc == 15))
            K.stt(X.v(oc, cs), ps.v(slice(0, n)), mcol(B, l, 2, oc, ps_.vec), X.v(oc, cs), ALU.mult, ALU.add)


def host_l1(inp, H):
    f = np.float32
    lam = np.concatenate([inp['lam_q1'][0], inp['lam_k1'][0], inp['lam_q2'][0], inp['lam_k2'][0]])
    H['attp'] = np.ascontiguousarray(np.concatenate([inp['subln_g'][0][:, None], np.broadcast_to(lam, (128, 256))], 1)).astype(f)
    wq = inp['w_qkv'][0]
    H['wqk'] = units_fm(wq[:, :4096]); H['wv'] = units_fm(wq[:, 4096:], 256); H['woc'] = units_fm(inp['w_out_c'][0])


def rope_tables(B, tk, pos_dram, name, c0, n):
    K = B.K
    C = tk('ropeC', [128, n]); S = tk('ropeS', [128, n]); A = tk('ropeA', [128, n]); Tm = tk('ropeT', [128, n])
    K.dma(A.v(), DR(pos_dram[:, c0:c0 + n], name))
    K.ts(A.v(), A.v(), B.ropec.v(slice(0, 1)), ALU.mult)
    range_reduce(B, C.v(), A.v(), Tm.v(), shift=math.pi / 2)
    K.act(C.v(), C.v(), AF.Sin)
    range_reduce(B, S.v(), A.v(), Tm.v())
    K.act(S.v(), S.v(), AF.Sin)
    K.ts(S.v(), S.v(), B.ropec.v(slice(1, 2)), ALU.mult)
    return C, S


def rope_apply(B, ps, n, Cv, Sv, dst):
    K = B.K
    kf = B.nscr(); t1 = B.nscr()
    K.copy(kf.v(slice(0, n)), ps)
    pw = B.ps[7] if B.pin else B.nps()
    K.mm(pw.v(slice(0, n)), B.perm_f(), kf.v(slice(0, n)))
    K.tt(t1.v(slice(0, n)), kf.v(slice(0, n)), Cv, ALU.mult)
    K.tt(kf.v(slice(0, n)), pw.v(slice(0, n)), Sv, ALU.mult)
    K.tt(dst, t1.v(slice(0, n)), kf.v(slice(0, n)), ALU.add)


def phase_sconsts(B):
    K = B.K
    rc = B.din('ropec', [128, 2]); B.ropec = B.ct('ropec', [128, 2]); K.dma(B.ropec.v(), DR(rc.ap(), 'ropec'))
    ms = B.din('msel', [128, 4]); B.msel = B.ct('msel', [128, 4]); K.dma(B.msel.v(), DR(ms.ap(), 'msel'))
    si = B.din('sinit', [128, 2 * 2 * 64]); B.sinit = B.ct('sinit', [128, 2, 2, 64])
    load_small(B, B.sinit, si, 'sinit')
    B.hmid = B.ct('hmid', [128, 2, 2, 64])
    B.din('posk', [128, 2048]); B.din('posq', [128, 512])
    B.din('xs', [128, 16, 2048]); B.din('ckT', [2, 16, 64, 512]); B.din('cv', [16, 512, 128])
    B.kt_scr = B.dscr('kt_scr', [32, 128, 2048], BF16)
    B.v_scr = B.dscr('v_scr', [2048, 2048], BF16)
    B.x1_scr = B.dscr('x1_scr', [128, 16, 2048], F32)


class _Sub2:
    def __init__(self, t, idx): self.t = t.t[(slice(None),) + idx]; self._t = t
    def v(self): return self._t.v()


def s5_capture(B, d):
    def cb(j, dd, hu, husw):
        if dd != d: return
        K = B.K
        for w, src in ((0, hu), (1, husw)):
            dst = V(B.hmid.t[:, d, w, 8 * j:8 * j + 8].unsqueeze(2).unsqueeze(3), B.hmid.v().pg)
            K.copy(dst, src)
    return cb


def sample_l0_pass(B, half):
    K = B.K; nc = B.nc
    xs = B.ins['xs']
    ps_ = Pass(B, 'S%d' % half, 1024, 1)
    t0 = 1024 * half
    load_x(B, ps_, xs, 'xs', c0=t0)
    modulate(B, ps_, 0, 0)
    hx = T(nc, K.name('hx'), [128, 16, 16], F32, SCR_OFF + 2048 * 4)
    hh = T(nc, K.name('hh'), [128, 16, 16], BF16, SCR_OFF + 2048 * 5)
    hc0 = 1024 if half == 0 else 1009
    K.dma(hx.v(slice(None), slice(0, 15)), DR(xs[:, :, hc0:hc0 + 15], 'xs'))
    for kc in range(16):
        K.ts(hh.v(kc, slice(0, 15)), hx.v(kc, slice(0, 15)), mcol(B, 0, 1, kc, 1), ALU.mult, mcol(B, 0, 0, kc, 1), ALU.add)
    hv = lambda kc: hh.v(kc, slice(0, 15))
    halo = {'right': hv} if half == 0 else {'left': hv}
    ini_host = lambda d: (_Sub2(B.sinit, (0, d)), _Sub2(B.sinit, (1, d)))
    ini_mid = lambda d: (_Sub2(B.hmid, (d, 0)), _Sub2(B.hmid, (d, 1)))
    if half == 0:
        s5init = {0: ini_host(0), 1: ini_mid(1)}; s5out = s5_capture(B, 0)
    else:
        s5init = {0: ini_mid(0), 1: ini_host(1)}; s5out = None
    mixer0(B, ps_, [(0, 1024)], halo=halo, s5init=s5init, s5out=s5out)
    post_ln(B, ps_, 0, 0)
    modulate(B, ps_, 0, 1)
    ffn(B, ps_, 0)
    post_ln(B, ps_, 0, 1)
    for kc in range(16):
        K.dma(DR(B.x1_scr[:, kc, t0:t0 + 1024], 'x1_scr'), ps_.X.v(kc))
    modulate(B, ps_, 1, 0)
    wqk = B.ins['wqk']; wvv = B.ins['wv']
    units = []
    for j in range(8):
        units += [(DR(wqk[16 + j], 'wqk'), 2048), (DR(wqk[24 + j], 'wqk'), 2048), (DR(wvv[j], 'wv'), 4096)]
    B.wplan(units)
    o = [BIG_OFF]
    def tk(name, shape, dt=F32):
        nb = 2 if dt == BF16 else 4
        for s_ in shape[1:]: nb *= s_
        if nb >= 512: o[0] = (o[0] + 511) // 512 * 512
        t = T(nc, K.name(name), shape, dt, o[0]); o[0] += (nb + 31) // 32 * 32
        assert o[0] <= WR_OFF, (name, o[0] - BIG_OFF)
        return t
    C, S = rope_tables(B, tk, B.ins['posk'], 'posk', t0, 1024)
    KR = tk('KR', [128, 2, 512], BF16); VS = tk('VS', [128, 2, 256], BF16)
    H = ps_.H
    cnt = 0
    for j in range(8):
        for m in range(2):
            sl = B.wnext()
            for (c0, n) in ps_.tiles:
                cs = slice(c0, c0 + n); ps = B.nps()
                for kc in range(16):
                    K.mm(ps.v(slice(0, n)), wv(sl, kc), H.v(kc, cs), start=(kc == 0), stop=(kc == 15))
                kr = KR.v(cnt % 2); cnt += 1
                rope_apply(B, ps.v(slice(0, n)), n, C.v(cs), S.v(cs), kr)
                K.dma(DR(B.kt_scr[8 * m + j, :, t0 + c0:t0 + c0 + n], 'kt_scr'), kr)
        sl = B.wnext()
        for tt_ in range(8):
            ps = B.nps()
            for kc in range(16):
                K.mm(ps.v(slice(0, 256)), H.v(kc, slice(128 * tt_, 128 * tt_ + 128)), wv(sl, kc, 256), start=(kc == 0), stop=(kc == 15))
            vs = VS.v(tt_ % 2)
            K.act(vs, ps.v(slice(0, 256)), AF.Copy)
            K.dma(DR(B.v_scr[t0 + 128 * tt_:t0 + 128 * tt_ + 128, 256 * j:256 * j + 256], 'v_scr'), vs)


def s5pre_pass(B):
    K = B.K
    ps_ = Pass(B, 'S5PRE', 1024, 1)
    load_x(B, ps_, B.ins['xs'], 'xs', c0=1024)
    modulate(B, ps_, 0, 0)
    s5init = {0: None, 1: (_Sub2(B.sinit, (0, 1)), _Sub2(B.sinit, (1, 1)))}
    mixer0(B, ps_, [(0, 1024)], s5init=s5init, s5out=s5_capture(B, 1), dirs=(1,), pre=True)


def sq_pass(B, ys_out):
    K = B.K; nc = B.nc
    ps_ = Pass(B, 'SQ', 512, 1)
    X = ps_.X; H = ps_.H
    for kc in range(16):
        for blk in range(4):
            st = B.nscr()
            K.dma(st.v(), DR(B.x1_scr[:, kc, 512 * blk:512 * blk + 512], 'x1_scr'))
            if blk == 0:
                K.ts(X.v(kc), st.v(), B.msel.v(slice(0, 1)), ALU.mult)
            else:
                K.stt(X.v(kc), st.v(), B.msel.v(slice(blk, blk + 1)), X.v(kc), ALU.mult, ALU.add)
    modulate(B, ps_, 1, 0)
    wqk = B.ins['wqk']; woc = B.ins['woc']
    units = []
    for j in range(8):
        units += [(DR(wqk[j], 'wqk'), 2048), (DR(wqk[8 + j], 'wqk'), 2048)]
    units += [(DR(woc[o_], 'woc'), 2048) for o_ in range(16)]
    B.wplan(units)
    o = [BIG_OFF]
    def tk(name, shape, dt=F32):
        nb = 2 if dt == BF16 else 4
        for s_ in shape[1:]: nb *= s_
        if nb >= 512: o[0] = (o[0] + 511) // 512 * 512
        t = T(nc, K.name(name), shape, dt, o[0]); o[0] += (nb + 31) // 32 * 32
        assert o[0] <= WR_OFF, (name, o[0] - BIG_OFF)
        return t
    ATT = tk('ATTq', [128, 16, 512], BF16)
    QM = tk('QMq', [128, 2, 2, 512], BF16)
    KA = tk('KA', [128, 2, 2560], BF16)
    VA = tk('VA', [128, 20, 256], BF16)
    PT = tk('PTq', [128, 4, 512], BF16)
    STG = tk('STGq', [128, 4, 512])
    C, S = rope_tables(B, tk, B.ins['posq'], 'posq', 0, 512)
    ckT = B.ins['ckT']; cv = B.ins['cv']
    K.memset(QM.v(), 0.0)
    B.pin = True
    pending = [None]
    sqset = lambda i, n: {'o': B.scr[3 + 2 * i].v(slice(0, n)), 's': B.scr[4 + 2 * i].v(slice(0, n))}
    rot = [0]
    def rps():
        p = B.ps[4 + rot[0] % 2]; rot[0] += 1
        return p
    cs = slice(0, 512)
    for j in range(8):
        for m in range(2):
            K.dma(KA.v(m, slice(0, 2048)), DR(B.kt_scr[8 * m + j], 'kt_scr'))
            K.dma(KA.v(m, slice(2048, 2560)), DR(ckT[m, 2 * j:2 * j + 2].rearrange('h d t -> (h d) t'), 'ckT'), eng='pool')
        K.dma(V(VA.t[:, 0:16, :], VA.v().pg), DR(B.v_scr[:, 256 * j:256 * j + 256].rearrange('(k p) f -> p k f', p=128), 'v_scr'))
        for h2 in range(2):
            K.dma(V(VA.t[:, 16:20, 128 * h2:128 * h2 + 128], VA.v().pg),
                  DR(cv[2 * j + h2].rearrange('(k p) d -> p k d', p=128), 'cv'), eng='pool')
        for m in range(2):
            sl = B.wnext()
            ps = rps()
            for kc in range(16):
                K.mm(ps.v(), wv(sl, kc), H.v(kc, cs), start=(kc == 0), stop=(kc == 15))
            qr = PT.v(0)
            rope_apply(B, ps.v(), 512, C.v(), S.v(), qr)
            K.copy(QM.v(m, 0, p=slice(0, 64)), V(PT.t[0:64, 0, :], PT.v(0).pg))
            K.copy(QM.v(m, 1, p=slice(64, 128)), V(PT.t[64:128, 0, :], PT.v(0).pg))
        for h2 in range(2):
            steps = [(kt, m) for kt in range(20) for m in range(2)]
            def score(i):
                kt, m = steps[i]
                K.mm(B.ps[4 + i % 3].v(), KA.v(m, slice(128 * kt, 128 * kt + 128)), QM.v(m, h2))
            score(0); score(1)
            for i, (kt, m) in enumerate(steps):
                if i + 2 < len(steps): score(i + 2)
                if i == 26 and pending[0] is not None:
                    attn_core2(B, 512, *pending[0]); pending[0] = None
                pt = PT.v(i % 4)
                K.act(pt, B.ps[4 + i % 3].v(), AF.Exp, scale=0.125)
                K.mm(B.ps[m].v(), B.ones_b(), pt, start=(kt == 0), stop=(kt == 19))
                K.mm(B.ps[2 + m].v(), VA.v(kt, slice(128 * h2, 128 * h2 + 128)), pt, start=(kt == 0), stop=(kt == 19))
            hid = (2 * j + h2) % 2
            attn_core1(B, 512, B.ps[2].v(), B.ps[3].v(), B.ps[0].v(), B.ps[1].v(), sqset(hid, 512), STG)
            pending[0] = (sqset(hid, 512), B.ps[7], ATT.v(2 * j + h2))
    if pending[0] is not None:
        attn_core2(B, 512, *pending[0]); pending[0] = None
    B.pin = False
    out_proj(B, ps_, ATT, 1)
    post_ln(B, ps_, 1, 0)
    modulate(B, ps_, 1, 1)
    ffn(B, ps_, 1)
    post_ln(B, ps_, 1, 1)
    store_x(B, ps_, ys_out, 'ys')


def prompt_pass(B, yp_out, ost, okT, ov):
    K = B.K
    ps_ = Pass(B, 'P', 1024, 0)
    load_x(B, ps_, B.ins['xp'], 'xp')
    modulate(B, ps_, 0, 0)
    def s5out(j, d, hu, husw):
        s = B.nscr()
        K.copy(V(s.t[:, 0:32].rearrange('p (g s c) -> p g s c', g=8, c=1), s.v().pg), hu)
        K.dma(DR(ost[d, :, j, :], 'ost'), s.v(slice(0, 32)))
    mixer0(B, ps_, [(256 * i, 256) for i in range(4)], s5out=s5out)
    post_ln(B, ps_, 0, 0)
    modulate(B, ps_, 0, 1)
    ffn(B, ps_, 0)
    post_ln(B, ps_, 0, 1)
    modulate(B, ps_, 1, 0)
    attn_p(B, ps_, okT, ov)
    post_ln(B, ps_, 1, 0)
    modulate(B, ps_, 1, 1)
    ffn(B, ps_, 1)
    post_ln(B, ps_, 1, 1)
    store_x(B, ps_, yp_out, 'yp')


def build_full(B, parts=('p', 's')):
    phase_consts(B)
    g1 = phase_mod_gen(B); g2 = phase_s5prep_gen(B)
    d1 = d2 = False
    while not (d1 and d2):
        for _ in range(5):
            if not d1:
                try: next(g1)
                except StopIteration: d1 = True
        if not d2:
            try: next(g2)
            except StopIteration: d2 = True
    phase_l0consts(B)
    phase_l1consts(B)
    B.din('wff1', [2, 64, 128, 2048]); B.din('wff2', [2, 2, 16, 128, 4096])
    if 'p' in parts:
        B.din('xp', [128, 16, 1024])
        yp = B.dout('yp', [128, 16, 1024]); ost = B.dout('ost', [2, 128, 8, 32])
        okT = B.dout('okT', [2, 16, 64, 1024]); ov = B.dout('ov', [1024, 16, 128])
        prompt_pass(B, yp, ost, okT, ov)
    if 's' in parts:
        phase_sconsts(B)
        ys = B.dout('ys', [128, 16, 512])
        s5pre_pass(B)
        sample_l0_pass(B, 0)
        sample_l0_pass(B, 1)
        sq_pass(B, ys)


def host_sample(inp, core, C):
    f = np.float32
    b = core // 4; q = core % 4
    C['xs'] = fm_tokens(inp['x_sample'][b])
    sre = inp['state_s5_re'][b, 0]; sim = inp['state_s5_im'][b, 0]
    plain = np.concatenate([sre.transpose(2, 0, 1), sim.transpose(2, 0, 1)], 0)
    swp = np.concatenate([sim.transpose(2, 0, 1), sre.transpose(2, 0, 1)], 0)
    C['sinit'] = np.ascontiguousarray(np.stack([plain, swp], 1)).reshape(128, 256).astype(f)
    C['ckT'] = np.ascontiguousarray(inp['cache_k'][b, 0].transpose(0, 1, 3, 2))
    C['cv'] = np.ascontiguousarray(inp['cache_v'][b, 0])
    ms = np.zeros((128, 4), f); ms[:, q] = 1.0
    C['msel'] = ms
    t = np.arange(2048)
    r = np.arange(128); d = r % 64
    pos = np.where((d < 32)[:, None], (t // 64)[None, :], (t % 64)[None, :]).astype(f)
    C['posk'] = np.ascontiguousarray(pos)
    C['posq'] = np.ascontiguousarray(pos[:, 512 * q:512 * q + 512])
    freq = (10000.0 ** (-(d % 16) / 16.0)).astype(f)
    sign = np.where((d % 32) < 16, -1.0, 1.0).astype(f)
    C['ropec'] = np.ascontiguousarray(np.stack([freq, sign], 1))


_CACHE = {}


def _get_builder():
    if 'B' not in _CACHE:
        B = Builder()
        st = ExitStack()
        build_full(B)
        B.K.P.finalize(st)
        _CACHE['B'] = B; _CACHE['st'] = st
    return _CACHE['B']


def kernel(**inputs):
    inp = {k: np.asarray(v) for k, v in inputs.items()}
    B = _get_builder()
    Hc = host_all(inp)
    maps = []
    for c in range(8):
        m = dict(Hc); m.update(host_core(inp, c))
        maps.append({k: np.ascontiguousarray(v, dtype=np.float32) for k, v in m.items() if k in B.ins})
    res = run_bass_kernel_spmd(B.nc, maps, core_ids=list(range(8)))
    f = np.float32
    y_p = np.zeros((32, 256, 2048), f); y_s = np.zeros((2, 2048, 2048), f)
    st_re = np.zeros((32, 1, 2, 64, 64), f); st_im = np.zeros((32, 1, 2, 64, 64), f)
    nk = np.zeros((32, 1, 2, 16, 256, 64), f); nv = np.zeros((32, 1, 16, 256, 128), f)
    for c in range(8):
        r = res.results[c]
        b = c // 4; q = c % 4
        yp = np.asarray(r['yp'], f).transpose(2, 1, 0).reshape(1024, 2048)
        y_p[4 * c:4 * c + 4] = yp.reshape(4, 256, 2048)
        ys = np.asarray(r['ys'], f).transpose(2, 1, 0).reshape(512, 2048)
        y_s[b, 512 * q:512 * q + 512] = ys
        ost = np.asarray(r['ost'], f)
        st = ost.reshape(2, 2, 64, 8, 8, 4).transpose(1, 5, 0, 3, 4, 2).reshape(2, 4, 2, 64, 64)
        st_re[4 * c:4 * c + 4, 0] = st[0]; st_im[4 * c:4 * c + 4, 0] = st[1]
        okT = np.asarray(r['okT'], f)
        nk[4 * c:4 * c + 4, 0] = okT.reshape(2, 16, 64, 4, 256).transpose(3, 0, 1, 4, 2)
        ov = np.asarray(r['ov'], f)
        nv[4 * c:4 * c + 4, 0] = ov.reshape(4, 256, 16, 128).transpose(0, 2, 1, 3)
    return (y_p, y_s, st_re, st_im, nk, nv)
```

```python
import math
import numpy as np
import concourse.bass as bass
import concourse.mybir as mybir
from concourse.bass_utils import run_bass_kernel_spmd
from contextlib import ExitStack

F32 = mybir.dt.float32
BF16 = mybir.dt.bfloat16
AF = mybir.ActivationFunctionType
ALU = mybir.AluOpType
AX = mybir.AxisListType

D = 2048; KC = 16; DFF = 8192
ALPHA = 4 ** 0.25
LN_EPS = 1e-5
LAM_INIT = 0.8 - 0.6 * math.exp(-0.3 * 1)
ENGS = ('pe', 'act', 'dve', 'pool', 'sp')
NDS = 24
PAGE = 512
SB_BASE = 16640
SB_END = 229376


class Prog:
    def __init__(self, nc):
        self.nc = nc; self.ops = []; self.lastw = {}; self.readers = {}

    def op(self, eng, fn, reads=(), writes=(), dma=False):
        if eng != 'pe':
            pk = [k for k in reads if k[0] == 'ps']
            if pk: writes = list(writes) + pk
        i = len(self.ops); deps = set()
        lw = self.lastw; rd = self.readers
        for k in reads:
            j = lw.get(k)
            if j is not None: deps.add(j)
        for k in writes:
            j = lw.get(k)
            if j is not None: deps.add(j)
            r = rd.get(k)
            if r: deps.update(r)
        for k in reads:
            r = rd.get(k)
            if r is None: rd[k] = [i]
            else: r.append(i)
        for k in writes:
            lw[k] = i; rd[k] = []
        deps.discard(i)
        self.ops.append((eng, fn, deps, dma))
        return i

    def finalize(self, stack):
        nc = self.nc; ops = self.ops
        esem = {e: stack.enter_context(nc.semaphore('s_' + e)) for e in ENGS}
        dsem = [stack.enter_context(nc.semaphore('d%d' % i)) for i in range(NDS)]
        red = []
        needed = set()
        for (e, fn, deps, dma) in ops:
            best = {}; keep = []
            for d in deps:
                pe_, _, _, pdma = ops[d]
                if pdma:
                    keep.append(d); continue
                if pe_ == 'pe' and e == 'pe' and not dma: continue
                if best.get(pe_, -1) < d: best[pe_] = d
            keep += list(best.values())
            red.append(keep)
            needed.update(keep)
        ecount = {e: 0 for e in ENGS}; dcount = [0] * NDS
        dnx = {True: 0, False: 0}; half = NDS // 2
        ev = {}; dprev = {}
        for i, (e, fn, deps, dma) in enumerate(ops):
            if dma:
                sw = (e == 'pool')
                j = (0 if sw else half) + dnx[sw]; dnx[sw] = (dnx[sw] + 1) % half
                dprev[i] = (dsem[j], dcount[j]); dcount[j] += 16; ev[i] = (dsem[j], dcount[j])
            elif i in needed:
                ecount[e] += 1; ev[i] = (esem[e], ecount[e])
        self.ecount = ecount
        waited = {e: {} for e in ENGS}
        plan = {e: [] for e in ENGS}
        lastdma = {e: {} for e in ENGS}
        for i, (e, fn, deps, dma) in enumerate(ops):
            ws = {}
            if dma:
                s, v = dprev[i]
                if v > 0: ws[s.name] = (s, v)
            for d in red[i]:
                s, v = ev[d]
                if s.name not in ws or ws[s.name][1] < v: ws[s.name] = (s, v)
            wl = []
            for nm, (s, v) in ws.items():
                if waited[e].get(nm, 0) >= v: continue
                waited[e][nm] = v; wl.append((s, v))
            inc = ev.get(i)
            if dma: lastdma[e][inc[0].name] = inc
            plan[e].append((fn, wl, inc, 16 if dma else 1))
        with nc.Block() as block:
            def runner(e):
                def run(eng):
                    for fn, wl, inc, by in plan[e]:
                        for s, v in wl: eng.wait_ge(s, v)
                        ins = fn(eng)
                        if inc is not None: ins.then_inc(inc[0], by)
                    for nm, (s, v) in lastdma[e].items():
                        if waited[e].get(nm, 0) < v: eng.wait_ge(s, v)
                return run
            block.tensor(runner('pe')); block.scalar(runner('act')); block.vector(runner('dve'))
            block.gpsimd(runner('pool')); block.sync(runner('sp'))


class V:
    __slots__ = ('ap', 'pg')
    def __init__(self, ap, pg): self.ap = ap; self.pg = pg


class T:
    def __init__(self, nc, name, shape, dtype, off):
        self.es = 2 if dtype == BF16 else 4
        fsize = 1
        for s in shape[1:]: fsize *= s
        assert off % 4 == 0 and off >= SB_BASE and off + fsize * self.es <= SB_END, (name, off, fsize * self.es)
        self.t = nc.alloc_sbuf_tensor_at(name, list(shape), dtype, offset=off)
        self.off = off; self.shape = list(shape); self.fsize = fsize; self.nbytes = fsize * self.es
        st = []; acc = 1
        for s in reversed(shape[1:]):
            st.append(acc); acc *= s
        self.strides = list(reversed(st))

    def _pages(self, lo, hi):
        a = (self.off + lo * self.es) // PAGE; b = (self.off + hi * self.es - 1) // PAGE
        return [('sb', i) for i in range(a, b + 1)]

    def v(self, *idx, p=slice(None)):
        idx = tuple(idx) + (slice(None),) * (len(self.shape) - 1 - len(idx))
        lo = 0; hi = 0
        for i, s, n in zip(idx, self.strides, self.shape[1:]):
            if isinstance(i, int):
                lo += i * s; hi += i * s
            else:
                a, b, step = i.indices(n)
                if step > 0:
                    cnt = max(0, (b - a + step - 1) // step)
                    lo += a * s; hi += (a + (cnt - 1) * step) * s
                else:
                    cnt = max(0, (a - b - step - 1) // (-step))
                    hi += a * s; lo += (a + (cnt - 1) * step) * s
        return V(self.t[(p,) + idx], self._pages(lo, hi + 1))


class PSB:
    def __init__(self, t, i): self.t = t; self.i = i
    def v(self, *idx, p=slice(None)):
        return V(self.t[(p,) + tuple(idx)] if idx else self.t[p], [('ps', self.i)])


def DR(ap, name):
    return V(ap, [('dram', name)])


def _pgs(*vs):
    out = []
    for x in vs:
        if isinstance(x, V): out += x.pg
    return out


def _ap(x):
    return x.ap if isinstance(x, V) else x


class Kern:
    def __init__(self, nc):
        self.nc = nc; self.P = Prog(nc); self.uid = 0

    def name(self, s):
        self.uid += 1
        return '%s%d' % (s, self.uid)

    def mm(self, out, lhsT, rhs, start=True, stop=True, tp=None):
        o, l, r = out.ap, lhsT.ap, rhs.ap
        if tp is None:
            fn = lambda e: e.matmul(o, lhsT=l, rhs=r, start=start, stop=stop)
        else:
            fn = lambda e: e.matmul(o, lhsT=l, rhs=r, start=start, stop=stop, tile_position=tp)
        self.P.op('pe', fn, reads=lhsT.pg + rhs.pg, writes=out.pg)

    def act(self, out, in_, func, scale=1.0, bias=0.0, accum=None, eng='act'):
        o, i, s, b = out.ap, in_.ap, _ap(scale), _ap(bias)
        if accum is None:
            fn = lambda e: e.activation(out=o, in_=i, func=func, bias=b, scale=s)
        else:
            a = accum.ap
            fn = lambda e: e.activation(out=o, in_=i, func=func, bias=b, scale=s, accum_out=a)
        self.P.op('act', fn, reads=in_.pg + _pgs(scale, bias), writes=out.pg + _pgs(accum))

    def tt(self, out, in0, in1, op, eng='dve'):
        o, a, b = out.ap, in0.ap, in1.ap
        self.P.op(eng, lambda e: e.tensor_tensor(out=o, in0=a, in1=b, op=op), reads=in0.pg + in1.pg, writes=out.pg)

    def ts(self, out, in0, s1, op0, s2=None, op1=None, eng='dve'):
        o, a, x1, x2 = out.ap, in0.ap, _ap(s1), _ap(s2)
        if op1 is None:
            fn = lambda e: e.tensor_scalar(out=o, in0=a, scalar1=x1, scalar2=None, op0=op0)
        else:
            fn = lambda e: e.tensor_scalar(out=o, in0=a, scalar1=x1, scalar2=x2, op0=op0, op1=op1)
        self.P.op(eng, fn, reads=in0.pg + _pgs(s1, s2), writes=out.pg)

    def stt(self, out, in0, scalar, in1, op0, op1, eng='dve'):
        o, a, s, b = out.ap, in0.ap, _ap(scalar), in1.ap
        self.P.op(eng, lambda e: e.scalar_tensor_tensor(out=o, in0=a, scalar=s, in1=b, op0=op0, op1=op1),
                  reads=in0.pg + in1.pg + _pgs(scalar), writes=out.pg)

    def copy(self, out, in_, eng='dve'):
        o, i = out.ap, in_.ap
        self.P.op(eng, lambda e: e.tensor_copy(out=o, in_=i), reads=in_.pg, writes=out.pg)

    def memset(self, out, val, eng='dve'):
        o = out.ap
        self.P.op(eng, lambda e: e.memset(o, val), writes=out.pg)

    def recip(self, out, in_):
        o, i = out.ap, in_.ap
        self.P.op('dve', lambda e: e.reciprocal(out=o, in_=i), reads=in_.pg, writes=out.pg)

    def scan(self, out, d0, d1, init=0.0):
        o, a, b = out.ap, d0.ap, d1.ap
        self.P.op('dve', lambda e: e.tensor_tensor_scan(out=o, data0=a, data1=b, initial=init, op0=ALU.mult, op1=ALU.add),
                  reads=d0.pg + d1.pg, writes=out.pg)

    def reduce(self, out, in_, op=ALU.add, axis=AX.X):
        o, i = out.ap, in_.ap
        self.P.op('dve', lambda e: e.tensor_reduce(out=o, in_=i, axis=axis, op=op), reads=in_.pg, writes=out.pg)

    def dma(self, out, in_, eng='sp'):
        o, i = out.ap, in_.ap
        self.P.op(eng, lambda e: e.dma_start(out=o, in_=i), reads=in_.pg, writes=out.pg, dma=True)


CONST_OFF = SB_BASE
X_OFF = CONST_OFF + 9984
H_OFF = X_OFF + 65536
BIG_OFF = H_OFF + 32768
WR_OFF = BIG_OFF + 65536
RING_BYTES = 24064
SCR_OFF = WR_OFF + RING_BYTES
WSLOT = 8192
NWSLOT = 3


class Bump:
    def __init__(self, off, end): self.o = off; self.end = end
    def take(self, n):
        n = (n + 31) // 32 * 32
        o = self.o; self.o += n
        assert self.o <= self.end, ('bump overflow', self.o, self.end)
        return o


class Builder:
    def __init__(self, dbg=()):
        self.nc = nc = bass.Bass("TRN2", target_bir_lowering=False)
        self.K = Kern(nc)
        self.dbg = set(dbg)
        self.cut = 99
        self.ins = {}; self.outs = {}
        self.cb = Bump(CONST_OFF, X_OFF)
        self.ps = [PSB(nc.alloc_psum_tensor('ps%d' % i, [128, 512], F32), i) for i in range(8)]
        self.psn = 0; self.pin = False
        self.scr = [T(nc, 'scr%d' % i, [128, 512], F32, SCR_OFF + 2048 * i) for i in range(7)]
        self.scrn = 0
        self.scrb = [T(nc, 'scrb%d' % i, [128, 512], BF16, SCR_OFF + 2048 * i) for i in range(3)]
        self.wq = []
        self.wissued = 0; self.wused = 0; self.wpos = 0; self.wlive = []; self.wtiles = {}

    def din(self, name, shape, dtype=F32):
        t = self.nc.dram_tensor(name, list(shape), dtype, kind="ExternalInput")
        self.ins[name] = t
        return t

    def dout(self, name, shape, dtype=F32):
        t = self.nc.dram_tensor(name, list(shape), dtype, kind="ExternalOutput")
        self.outs[name] = t
        return t

    def dscr(self, name, shape, dtype):
        return self.nc.dram_tensor(name, list(shape), dtype, kind="Internal")

    def nps(self):
        p = self.ps[self.psn % 8]; self.psn += 1
        return p

    def nscr(self):
        s = self.scr[self.scrn % 3]; self.scrn += 1
        return s

    def nscrb(self):
        s = self.scrb[self.scrn % 3]; self.scrn += 1
        return s

    def ct(self, name, shape, dtype=F32):
        n = (2 if dtype == BF16 else 4)
        for s in shape[1:]: n *= s
        return T(self.nc, name, shape, dtype, self.cb.take(n))

    def dump(self, key, t, shape=None):
        if key not in self.dbg: return
        K = self.K
        fs = t.fsize
        o = self.dout('dbg_' + key, [t.shape[0], fs], F32)
        if t.es == 4:
            K.dma(DR(o.ap(), 'dbg_' + key), V(t.t[:].rearrange(_flat(len(t.shape))) if len(t.shape) > 2 else t.t[:], t._pages(0, fs)))
        else:
            raise NotImplementedError

    def dump_bf(self, key, t, nch):
        if key not in self.dbg: return
        K = self.K
        nt = t.shape[2]
        o = self.dout('dbg_' + key, [128, nch, nt], F32)
        for kc in range(nch):
            for c0 in range(0, nt, 512):
                s = self.nscr()
                K.copy(s.v(slice(0, 512)), t.v(kc, slice(c0, c0 + 512)))
                K.dma(DR(o[:, kc, c0:c0 + 512], 'dbg_' + key), s.v(slice(0, 512)))

    def wplan(self, units):
        self.wq += list(units)

    def wnext(self):
        K = self.K; nc = self.nc
        RING = RING_BYTES
        cur = self.wused
        while self.wissued < len(self.wq):
            dv, n = self.wq[self.wissued]
            nb = 2 * n
            pos = self.wpos
            if pos + nb > RING: pos = 0
            ok = True
            for (idx, a, b_) in self.wlive:
                if idx >= cur - 1 and not (pos + nb <= a or b_ <= pos):
                    ok = False; break
            if not ok and self.wissued > cur: break
            assert ok, 'weight ring too small'
            t = T(nc, K.name('wu'), [128, n], BF16, WR_OFF + pos)
            K.dma(t.v(), dv, eng='pool')
            self.wlive.append((self.wissued, pos, pos + nb)); self.wtiles[self.wissued] = t
            self.wlive = [x for x in self.wlive if x[0] >= cur - 1]
            self.wpos = pos + nb
            self.wissued += 1
        t = self.wtiles.pop(cur)
        self.wused += 1
        return t


def _flat(nd):
    names = 'abcdefg'[:nd - 1]
    return 'p ' + ' '.join(names) + ' -> p (' + ' '.join(names) + ')'


def phase_consts(B):
    K = B.K
    c = B.din('consts', [128, 3, 128])
    B.cst = B.ct('cst', [128, 3, 128])
    K.dma(B.cst.v(), DR(c.ap(), 'consts'))
    B.ones_f = lambda p=slice(None): B.cst.v(0, p=p)
    B.ident_f = lambda p=slice(None), n=128: B.cst.v(1, slice(0, n), p=p)
    B.perm_f = lambda: B.cst.v(2)
    B.cstb = B.ct('cstb', [128, 2, 128], BF16)
    K.copy(B.cstb.v(), B.cst.v(slice(0, 2)))
    B.ones_b = lambda p=slice(None): B.cstb.v(0, p=p)
    B.ident_b = lambda: B.cstb.v(1)
    lnp = B.din('lnp', [128, 2 * 2 * 2 * 16])
    B.lnp = B.ct('lnp', [128, 2, 2, 2, 16])
    K.dma(V(B.lnp.t[:].rearrange('p a b c d -> p (a b c d)'), B.lnp._pages(0, B.lnp.fsize)), DR(lnp.ap(), 'lnp'))
    B.eps_ln = B.ct('eps_ln', [128, 1]); K.memset(B.eps_ln.v(), LN_EPS / (ALPHA * ALPHA))
    B.eps_raw = B.ct('eps_raw', [128, 1]); K.memset(B.eps_raw.v(), LN_EPS)


def phase_mod(B):
    for _ in phase_mod_gen(B): pass


def phase_mod_gen(B):
    K = B.K
    cvec = B.din('cvec', [128, 32])
    wmod = B.din('wmod', [2, 48, 128, 4096])
    bcol = B.din('bcol', [128, 2 * 6 * 16])
    B.modt = B.ct('modt', [128, 2, 6, 16, 2])
    NS = 4
    slabs = [T(B.nc, 'slab%d' % i, [128, 16, 256], BF16, WR_OFF + 8192 * i) for i in range(NS)]
    so = WR_OFF + 8192 * NS
    cv = T(B.nc, 'cv', [128, 16, 2], F32, so)
    cs = T(B.nc, 'cs', [128, 16, 2], BF16, so + 512)
    bc = T(B.nc, 'bc', [128, 2, 6, 16], F32, so + 1024)
    rows = T(B.nc, 'rows', [128, 256], F32, so + 2048)
    K.dma(V(cv.t[:].rearrange('p a b -> p (a b)'), cv._pages(0, 32)), DR(cvec.ap(), 'cvec'))
    K.dma(V(bc.t[:].rearrange('p a b c -> p (a b c)'), bc._pages(0, 192)), DR(bcol.ap(), 'bcol'))
    K.act(cs.v(), cv.v(), AF.Silu)
    units = [(l, j) for l in range(2) for j in range(48)]
    issued = 0
    for ui, (l, j) in enumerate(units):
        while issued < len(units) and issued < ui + NS - 1:
            ll, jj = units[issued]
            sl = slabs[issued % NS]
            K.dma(V(sl.t[:].rearrange('p a b -> p (a b)'), sl._pages(0, sl.fsize)), DR(wmod[ll, jj], 'wmod'), eng='pool')
            issued += 1
        sl = slabs[ui % NS]
        ps = B.nps()
        for kc in range(16):
            K.mm(ps.v(slice(0, 256), p=slice(0, 2)), cs.v(kc), sl.v(kc), start=(kc == 0), stop=(kc == 15))
        K.copy(rows.v(p=slice(0, 2)), ps.v(slice(0, 256), p=slice(0, 2)))
        ps2 = B.nps()
        for i in range(2):
            K.mm(ps2.v(slice(2 * i, 2 * i + 2)), rows.v(slice(128 * i, 128 * i + 128), p=slice(0, 2)),
                 B.ident_f(p=slice(0, 2), n=2))
        six = j // 8; ch = (j % 8) * 2
        K.tt(B.modt.v(l, six, slice(ch, ch + 2)),
             V(ps2.t[:, 0:4].rearrange('p (a b) -> p a b', b=2), ps2.v().pg),
             V(bc.t[:, l, six, ch:ch + 2].unsqueeze(2).to_broadcast([128, 2, 2]), bc.v().pg), ALU.add)
        yield
    for l in range(2):
        for six in (1, 4):
            K.ts(B.modt.v(l, six), B.modt.v(l, six), 1.0, ALU.add)
        for six in (2, 5):
            K.ts(B.modt.v(l, six), B.modt.v(l, six), 1.0 / ALPHA, ALU.mult)
    B.dump('modt', B.modt)


def host_common(inp):
    f = np.float32
    H = {}
    ones = np.ones((128, 128), f); ident = np.eye(128, dtype=f)
    perm = np.zeros((128, 128), f)
    for m in range(128):
        j = m % 32
        k = m + 16 if j < 16 else m - 16
        perm[k, m] = 1.0
    H['consts'] = np.ascontiguousarray(np.stack([ones, ident, perm], 1))
    wm = inp['w_mod'].reshape(2, 16, 128, 48, 256).transpose(0, 3, 2, 1, 4)
    H['wmod'] = np.ascontiguousarray(wm).reshape(2, 48, 128, 4096)
    bm = inp['b_mod'].reshape(2, 6, 16, 128).transpose(3, 0, 1, 2)
    H['bcol'] = np.ascontiguousarray(bm).reshape(128, 192)
    lg = inp['ln_g'].reshape(2, 2, 16, 128).transpose(3, 0, 1, 2); lb = inp['ln_b'].reshape(2, 2, 16, 128).transpose(3, 0, 1, 2)
    H['lnp'] = np.ascontiguousarray(np.stack([lg, lb], 1)).reshape(128, 128)
    return H


def host_all(inp):
    H = host_common(inp)
    host_ff(inp, H)
    host_s5(inp, H)
    host_l0(inp, H)
    host_l1(inp, H)
    return H


def host_core(inp, core):
    f = np.float32
    b = core // 4
    C = {}
    C['xp'] = fm_tokens(inp['x_prompt'][4 * core:4 * core + 4].reshape(1024, 2048))
    host_sample(inp, core, C)
    cv = np.stack([inp['c_ctx'], inp['c'][b]], -1).reshape(16, 128, 2).transpose(1, 0, 2)
    C['cvec'] = np.ascontiguousarray(cv).reshape(128, 32)
    return C


class Pass:
    def __init__(self, B, name, ntok, vec):
        self.B = B; self.name = name; self.ntok = ntok; self.vec = vec
        self.X = T(B.nc, B.K.name('X'), [128, 16, ntok], F32, X_OFF)
        self.H = T(B.nc, B.K.name('H'), [128, 16, ntok], BF16, H_OFF)
        self.tiles = [(c, min(512, ntok - c)) for c in range(0, ntok, 512)]


def mcol(B, l, six, kc, vec):
    return B.modt.v(l, six, kc, slice(vec, vec + 1))


def modulate(B, ps_, l, which):
    K = B.K; X = ps_.X; H = ps_.H
    for kc in range(16):
        sc = mcol(B, l, 3 * which + 1, kc, ps_.vec); sh = mcol(B, l, 3 * which, kc, ps_.vec)
        if kc % 2 == 0:
            K.act(H.v(kc), X.v(kc), AF.Identity, scale=sc, bias=sh)
        else:
            K.ts(H.v(kc), X.v(kc), sc, ALU.mult, sh, ALU.add)


def wv(sl, kc, m=128):
    return sl.v(slice(kc * m, (kc + 1) * m))


def layer_norm(B, X, nch, tiles, gcol, bcol, eps_v, act_func=AF.Identity, out=None):
    K = B.K
    Dn = nch * 128
    for (c0, n) in tiles:
        cs = slice(c0, c0 + n)
        ps1 = B.nps(); ps2 = B.nps()
        for kc in range(nch):
            K.mm(ps1.v(slice(0, n)), B.ones_f(), X.v(kc, cs), start=(kc == 0), stop=(kc == nch - 1))
        for kc in range(nch):
            sq = B.nscrb()
            K.act(sq.v(slice(0, n)), X.v(kc, cs), AF.Square)
            K.mm(ps2.v(slice(0, n)), B.ones_b(), sq.v(slice(0, n)), start=(kc == 0), stop=(kc == nch - 1))
        mean, msq, var, rstd = B.scr[3], B.scr[4], B.scr[5], B.scr[6]
        sn = slice(0, n)
        K.ts(mean.v(sn), ps1.v(sn), 1.0 / Dn, ALU.mult)
        K.tt(msq.v(sn), mean.v(sn), mean.v(sn), ALU.mult)
        K.stt(var.v(sn), ps2.v(sn), 1.0 / Dn, msq.v(sn), ALU.mult, ALU.subtract)
        K.act(var.v(sn), var.v(sn), AF.Ln, bias=eps_v)
        K.act(rstd.v(sn), var.v(sn), AF.Exp, scale=-0.5)
        for kc in range(nch):
            t = B.nscr()
            K.tt(t.v(sn), X.v(kc, cs), mean.v(sn), ALU.subtract)
            K.tt(t.v(sn), t.v(sn), rstd.v(sn), ALU.mult)
            o = (out if out is not None else X).v(kc, cs)
            K.act(o, t.v(sn), act_func, scale=gcol(kc), bias=bcol(kc))


def post_ln(B, ps_, l, which):
    lnp = B.lnp
    layer_norm(B, ps_.X, 16, ps_.tiles,
               lambda kc: lnp.v(0, l, which, slice(kc, kc + 1)), lambda kc: lnp.v(1, l, which, slice(kc, kc + 1)),
               B.eps_ln.v())


def ffn(B, ps_, l):
    K = B.K; X = ps_.X; H = ps_.H
    w1 = B.ins['wff1']; w2 = B.ins['wff2']
    HID = T(B.nc, K.name('hid'), [128, 32, ps_.ntok], BF16, BIG_OFF)
    units = []
    for hh in range(2):
        units += [(DR(w1[l, hh * 32 + m], 'wff1'), 2048) for m in range(32)]
        units += [(DR(w2[l, hh, o], 'wff2'), 4096) for o in range(16)]
    B.wplan(units)
    for hh in range(2):
        for m in range(32):
            sl = B.wnext()
            for (c0, n) in ps_.tiles:
                cs = slice(c0, c0 + n); ps = B.nps()
                for kc in range(16):
                    K.mm(ps.v(slice(0, n)), wv(sl, kc), H.v(kc, cs), start=(kc == 0), stop=(kc == 15))
                r = B.nscr()
                K.act(r.v(slice(0, n)), ps.v(slice(0, n)), AF.Relu)
                K.tt(HID.v(m, cs), r.v(slice(0, n)), r.v(slice(0, n)), ALU.mult)
        for o in range(16):
            sl = B.wnext()
            for (c0, n) in ps_.tiles:
                cs = slice(c0, c0 + n); ps = B.nps()
                for kc in range(32):
                    K.mm(ps.v(slice(0, n)), wv(sl, kc), HID.v(kc, cs), start=(kc == 0), stop=(kc == 31))
                K.stt(X.v(o, cs), ps.v(slice(0, n)), mcol(B, l, 5, o, ps_.vec), X.v(o, cs), ALU.mult, ALU.add)


def units_fm(W, mcols=128):
    Kd, M = W.shape
    return np.ascontiguousarray(W.reshape(Kd // 128, 128, M // mcols, mcols).transpose(2, 1, 0, 3)).reshape(M // mcols, 128, (Kd // 128) * mcols)


def host_ff(inp, H):
    H['wff1'] = np.stack([units_fm(inp['w_ff1'][l]) for l in range(2)])
    H['wff2'] = np.stack([np.stack([units_fm(inp['w_ff2'][l][hh * 4096:(hh + 1) * 4096]) for hh in range(2)]) for l in range(2)])


def fm_tokens(x):
    return np.ascontiguousarray(x.reshape(x.shape[0], 16, 128).transpose(2, 1, 0))


def load_x(B, ps_, dram, name, c0=0):
    K = B.K
    for kc in range(16):
        K.dma(ps_.X.v(kc), DR(dram[:, kc, c0:c0 + ps_.ntok], name))


def store_x(B, ps_, dram, name):
    K = B.K
    for kc in range(16):
        K.dma(DR(dram[:, kc, :], name), ps_.X.v(kc))


def build_stage(B, stage):
    if stage.startswith('full'):
        build_full(B, parts=stage.split(':')[1] if ':' in stage else 'ps')
        return
    phase_consts(B)
    phase_mod(B)
    if stage == 'mod': return
    if stage.startswith('full'):
        pass
    if stage == 's5prep':
        phase_s5prep(B)
        return
    if stage.startswith('l0p'):
        if ':' in stage: B.cut = int(stage.split(':')[1])
        phase_s5prep(B); phase_l0consts(B)
        B.din('wff1', [2, 64, 128, 2048]); B.din('wff2', [2, 2, 16, 128, 4096])
        xp = B.din('xp', [128, 16, 1024]); yo = B.dout('yp', [128, 16, 1024])
        ost = B.dout('ost', [2, 128, 8, 32])
        ps_ = Pass(B, 'P', 1024, 0)
        load_x(B, ps_, xp, 'xp')
        modulate(B, ps_, 0, 0)
        def s5out(j, d, hu, husw):
            s = B.nscr()
            B.K.copy(V(s.t[:, 0:32].rearrange('p (g s c) -> p g s c', g=8, c=1), s.v().pg), hu)
            B.K.dma(DR(ost[d, :, j, :], 'ost'), s.v(slice(0, 32)))
        mixer0(B, ps_, [(256 * i, 256) for i in range(4)], s5out=s5out)
        post_ln(B, ps_, 0, 0)
        store_x(B, ps_, yo, 'yp')
        return
    if stage == 'l1c':
        phase_l1consts(B)
        B.dump('neglam', B.neglam); B.dump('subg', B.subg)
        return
    if stage.startswith('l1p'):
        if ':' in stage: B.cut = int(stage.split(':')[1])
        phase_l1consts(B)
        xp = B.din('xp1', [128, 16, 1024]); yo = B.dout('yp', [128, 16, 1024])
        okT = B.dout('okT', [2, 16, 64, 1024]); ov = B.dout('ov', [1024, 16, 128])
        ps_ = Pass(B, 'P', 1024, 0)
        load_x(B, ps_, xp, 'xp1')
        modulate(B, ps_, 1, 0)
        attn_p(B, ps_, okT, ov)
        post_ln(B, ps_, 1, 0)
        store_x(B, ps_, yo, 'yp')
        return
    if stage == 'ff':
        B.din('wff1', [2, 64, 128, 2048]); B.din('wff2', [2, 2, 16, 128, 4096])
        xp = B.din('xp', [128, 16, 1024]); yo = B.dout('yp', [128, 16, 1024])
        ps_ = Pass(B, 'P', 1024, 0)
        load_x(B, ps_, xp, 'xp')
        modulate(B, ps_, 0, 1)
        ffn(B, ps_, 0)
        post_ln(B, ps_, 0, 1)
        store_x(B, ps_, yo, 'yp')
        return


TWO_PI = 2.0 * math.pi
MAGIC = 12582912.0
CW1 = 6.28125
CW2 = TWO_PI - 6.28125


def range_reduce(B, out, x, tmp, shift=0.0):
    K = B.K
    if shift != 0.0:
        K.ts(out, x, shift, ALU.add)
        x = out
    K.ts(tmp, x, 1.0 / TWO_PI, ALU.mult, MAGIC, ALU.add)
    K.ts(tmp, tmp, MAGIC, ALU.subtract)
    K.stt(out, tmp, -CW1, x, ALU.mult, ALU.add)
    K.stt(out, tmp, -CW2, out, ALU.mult, ALU.add)


def phase_s5prep(B):
    for _ in phase_s5prep_gen(B): pass


def phase_s5prep_gen(B):
    K = B.K; nc = B.nc
    sA = B.din('s5A', [128, 3, 2, 64])
    sBq = B.din('s5Bq', [128, 2, 2, 64, 16])
    sC = B.din('s5C', [128, 2, 2, 64, 16])
    sK = B.din('s5K', [128, 16 + 128 + 64 + 256])
    B.s5w1 = B.dscr('s5w1', [2, 8, 128, 8 * 128], BF16)
    B.s5w3 = B.dscr('s5w3', [2, 8, 128, 8 * 128], BF16)
    B.s5w1s = B.dscr('s5w1s', [2, 8, 128, 8 * 128], BF16)
    B.s5w2 = B.dscr('s5w2', [8, 128, 8 * 128], BF16)
    B.s5rot = B.dscr('s5rot', [2, 2, 128, 64 * 128], F32)
    B.s5R = B.ct('s5R', [128, 2, 64])
    o = [X_OFF]
    def tk(name, shape, dt=F32):
        n = 2 if dt == BF16 else 4
        for s in shape[1:]: n *= s
        if n >= 512: o[0] = (o[0] + 511) // 512 * 512
        t = T(nc, K.name(name), shape, dt, o[0]); o[0] += (n + 31) // 32 * 32
        assert o[0] <= WR_OFF, ('s5prep overflow', name, o[0])
        return t
    A = tk('sA', [128, 3, 2, 64]); Bq = tk('sBq', [128, 2, 2, 64, 16]); Cc = tk('sC', [128, 2, 2, 64, 16])
    Kc = tk('sK', [128, 464])
    K.dma(V(A.t[:].rearrange('p a b c -> p (a b c)'), A.v().pg), DR(sA.ap().rearrange('p a b c -> p (a b c)'), 's5A'))
    K.dma(V(Bq.t[:].rearrange('p a b c d -> p (a b c d)'), Bq.v().pg), DR(sBq.ap().rearrange('p a b c d -> p (a b c d)'), 's5Bq'))
    K.dma(V(Cc.t[:].rearrange('p a b c d -> p (a b c d)'), Cc.v().pg), DR(sC.ap().rearrange('p a b c d -> p (a b c d)'), 's5C'))
    K.dma(Kc.v(), DR(sK.ap(), 's5K'))
    o_mark = o[0]
    sgn = tk('sgn', [128, 1]); K.memset(sgn.v(), 1.0); K.memset(sgn.v(p=slice(64, 128)), -1.0)
    nsg = tk('nsg', [128, 1]); K.memset(nsg.v(), -1.0); K.memset(nsg.v(p=slice(64, 128)), 1.0)

    def basics(src, ng, pre):
        dt = tk(pre + 'dt', [128, 2, ng]); a = tk(pre + 'a', [128, 2, ng]); th = tk(pre + 'th', [128, 2, ng])
        K.act(dt.v(), src.v(2), AF.Exp)
        K.tt(a.v(), src.v(0), dt.v(), ALU.mult)
        K.tt(th.v(), src.v(1), dt.v(), ALU.mult)
        return dt, a, th
    dtA, aA, thA = basics(A, 64, 'A')
    o_mark2 = o[0]
    NG = 2 * 64
    flatA = lambda t: V(t.t[:].rearrange('p a b -> p (a b)'), t.v().pg)
    amk = tk('amk', [128, 2, 64, 16]); ank = tk('ank', [128, 2, 64, 16]); tmpk = tk('tmpk', [128, 2, 64, 16])
    pwr = tk('pwr', [128, 2, 64, 16]); pwi = tk('pwi', [128, 2, 64, 16])
    kv = V(Kc.t[:, 0:16].unsqueeze(1).unsqueeze(1).to_broadcast([128, 2, 64, 16]), Kc.v().pg)
    bc16 = lambda t: V(t.t[:].unsqueeze(3).to_broadcast([128, 2, 64, 16]), t.v().pg)
    K.tt(amk.v(), bc16(aA), kv, ALU.mult)
    K.tt(ank.v(), bc16(thA), kv, ALU.mult)
    K.act(amk.v(), amk.v(), AF.Exp)
    f4 = lambda t: V(t.t[:].rearrange('p a b c -> p (a b c)'), t.v().pg)
    halfpi0 = tk('halfpi0', [128, 1]); K.memset(halfpi0.v(), math.pi / 2)
    range_reduce(B, f4(pwr), f4(ank), f4(tmpk))
    K.act(pwi.v(), pwr.v(), AF.Sin)
    K.act(pwr.v(), pwr.v(), AF.Abs)
    K.act(pwr.v(), pwr.v(), AF.Sin, scale=-1.0, bias=halfpi0.v())
    K.tt(pwr.v(), pwr.v(), amk.v(), ALU.mult)
    K.tt(pwi.v(), pwi.v(), amk.v(), ALU.mult)
    lbr = tk('lbr', [128, 2, 64]); lbi = tk('lbi', [128, 2, 64]); den = tk('den', [128, 2, 64]); t1 = tk('t1', [128, 2, 64])
    cre = tk('cre', [128, 2, 64]); cim = tk('cim', [128, 2, 64])
    K.ts(lbr.v(), pwr.v(slice(None), slice(None), 8), -1.0, ALU.add)
    K.copy(lbi.v(), pwi.v(slice(None), slice(None), 8))
    K.tt(den.v(), A.v(0), A.v(0), ALU.mult); K.tt(t1.v(), A.v(1), A.v(1), ALU.mult); K.tt(den.v(), den.v(), t1.v(), ALU.add)
    K.recip(den.v(), den.v())
    K.tt(cre.v(), lbr.v(), A.v(0), ALU.mult); K.tt(t1.v(), lbi.v(), A.v(1), ALU.mult); K.tt(cre.v(), cre.v(), t1.v(), ALU.add)
    K.tt(cre.v(), cre.v(), den.v(), ALU.mult)
    K.tt(cim.v(), lbi.v(), A.v(0), ALU.mult); K.tt(t1.v(), lbr.v(), A.v(1), ALU.mult); K.tt(cim.v(), cim.v(), t1.v(), ALU.subtract)
    K.tt(cim.v(), cim.v(), den.v(), ALU.mult)
    scim = tk('scim', [128, 2, 64]); K.ts(scim.v(), cim.v(), sgn.v(), ALU.mult)
    BbA = tk('BbA', [128, 2, 64, 16]); BbB = tk('BbB', [128, 2, 64, 16]); t16 = T(nc, K.name('t16'), [128, 2, 64, 16], F32, tmpk.off)
    bcp = lambda t: V(t.t[:].unsqueeze(3).to_broadcast([128, 2, 64, 16]), t.v().pg)
    K.tt(BbA.v(), Bq.v(0), bcp(cre), ALU.mult); K.tt(t16.v(), Bq.v(1), bcp(scim), ALU.mult); K.tt(BbA.v(), BbA.v(), t16.v(), ALU.subtract)
    K.tt(BbB.v(), Bq.v(1), bcp(cre), ALU.mult); K.tt(t16.v(), Bq.v(0), bcp(scim), ALU.mult); K.tt(BbB.v(), BbB.v(), t16.v(), ALU.add)
    nspwi = tk('nspwi', [128, 2, 64, 16]); spwr = tk('spwr', [128, 2, 64, 16]); npwi = tk('npwi', [128, 2, 64, 16])
    K.ts(nspwi.v(), pwi.v(), nsg.v(), ALU.mult)
    K.ts(spwr.v(), pwr.v(), sgn.v(), ALU.mult)
    K.ts(npwi.v(), pwi.v(), -1.0, ALU.mult)
    def ksl(k0, step):
        a0 = k0 + 7; a1 = a0 + 8 * step
        return slice(a0, a1 if a1 >= 0 else None, step)
    Z = [tk('Z%d' % i, [128, 8, 8, 16]) for i in range(4)]; m2 = tk('m2', [128, 8, 8, 16])
    Zb = tk('Zb', [128, 8, 8, 16], BF16)
    w1sb = tk('w1sb', [128, 8, 128], BF16); w1sw = tk('w1sw', [128, 8, 128], BF16); w2acc = tk('w2acc', [128, 8, 128]); w2sb = tk('w2sb', [128, 8, 128], BF16)

    def outer(dst, A1, A2, B1, B2, d, gs, ks):
        a1 = V(A1.t[:, d, gs, ks].unsqueeze(3).to_broadcast([128, 8, 8, 16]), A1.v().pg)
        a2 = V(A2.t[:, d, gs, ks].unsqueeze(3).to_broadcast([128, 8, 8, 16]), A2.v().pg)
        b1 = V(B1.t[:, d, gs, :].unsqueeze(2).to_broadcast([128, 8, 8, 16]), B1.v().pg)
        b2 = V(B2.t[:, d, gs, :].unsqueeze(2).to_broadcast([128, 8, 8, 16]), B2.v().pg)
        K.tt(dst.v(), a1, b1, ALU.mult); K.tt(m2.v(), a2, b2, ALU.mult); K.tt(dst.v(), dst.v(), m2.v(), ALU.add)

    C1 = lambda: None
    Cw = Cc
    class Sub:
        def __init__(s, t, i): s.t = t.t[:, i]; s._t = t
        def v(s): return s._t.v()
    for gch in range(8):
        gs = slice(8 * gch, 8 * gch + 8)
        for d in range(2):
            ks = ksl(7, -1) if d == 0 else ksl(0, 1)
            outer(Z[0], pwr, nspwi, BbA, BbB, d, gs, ks)
            ps = [B.nps(), B.nps()]
            for gl in range(8):
                K.mm(ps[gl // 4].v(slice(128 * (gl % 4), 128 * (gl % 4) + 128)),
                     V(Z[0].t[:, gl].rearrange('p a b -> p (a b)'), Z[0].v().pg), B.ident_f())
            for h in range(2):
                K.act(V(w1sb.t[:, 4 * h:4 * h + 4, :].rearrange('p a b -> p (a b)'), w1sb.v().pg), ps[h].v(), AF.Copy)
                pv = ps[h].t[:, :].rearrange('p (a b) -> p a b', b=128)
                K.copy(V(w1sw.t[:, 4 * h:4 * h + 4, 0:64], w1sw.v().pg), V(pv[:, :, 64:128], ps[h].v().pg))
                K.copy(V(w1sw.t[:, 4 * h:4 * h + 4, 64:128], w1sw.v().pg), V(pv[:, :, 0:64], ps[h].v().pg))
            K.dma(DR(B.s5w1[d, gch], 's5w1'), V(w1sb.t[:].rearrange('p a b -> p (a b)'), w1sb.v().pg))
            K.dma(DR(B.s5w1s[d, gch], 's5w1s'), V(w1sw.t[:].rearrange('p a b -> p (a b)'), w1sw.v().pg))
            ks = ksl(1, 1) if d == 0 else ksl(8, -1)
            outer(Z[1], spwr, npwi, Sub(Cw, 0), Sub(Cw, 1), d, gs, ks)
            K.act(Zb.v(), Z[1].v(), AF.Copy)
            K.dma(DR(B.s5w3[d, gch], 's5w3'), V(Zb.t[:].rearrange('p g a b -> p (g a b)'), Zb.v().pg))
            outer(Z[2], pwr, nspwi, BbA, BbB, d, gs, ksl(0, -1) if d == 0 else ksl(0, 1))
            outer(Z[3], spwr, npwi, Sub(Cw, 0), Sub(Cw, 1), d, gs, ksl(0, 1) if d == 0 else ksl(0, -1))
            ps = [B.nps(), B.nps()]
            for gl in range(8):
                K.mm(ps[gl // 4].v(slice(128 * (gl % 4), 128 * (gl % 4) + 128)),
                     V(Z[2].t[:, gl].rearrange('p a b -> p (a b)'), Z[2].v().pg),
                     V(Z[3].t[:, gl].rearrange('p a b -> p (a b)'), Z[3].v().pg))
            mk = V(Kc.t[:, 208 + 128 * d:208 + 128 * d + 128].unsqueeze(1).to_broadcast([128, 4, 128]), Kc.v().pg)
            for h in range(2):
                pv = V(ps[h].t[:, :].rearrange('p (a b) -> p a b', b=128), ps[h].v().pg)
                dv = V(w2acc.t[:, 4 * h:4 * h + 4, :], w2acc.v().pg)
                if d == 0:
                    K.tt(dv, pv, mk, ALU.mult)
                else:
                    t = V(m2.t[:, 0:4].rearrange('p a b c -> p a (b c)'), m2.v().pg)
                    K.tt(t, pv, mk, ALU.mult)
                    K.tt(dv, dv, t, ALU.add)
        for gl in range(8):
            g = 8 * gch + gl
            K.stt(w2sb.v(gl), B.ident_f(), Kc.v(slice(144 + g, 145 + g)), w2acc.v(gl), ALU.mult, ALU.add)
        K.dma(DR(B.s5w2[gch], 's5w2'), V(w2sb.t[:].rearrange('p a b -> p (a b)'), w2sb.v().pg))
        yield
    o[0] = o_mark2
    K.act(B.s5R.v(), aA.v(), AF.Exp, scale=8.0)
    ph = tk('ph', [128, 2, 64]); pt = tk('pt', [128, 2, 64])
    K.ts(ph.v(), thA.v(), 8.0, ALU.mult)
    range_reduce(B, flatA(ph), flatA(ph), flatA(pt))
    arg = tk('arg', [128, 64, 128]); red = tk('red', [128, 64, 128]); tmp = tk('rtmp', [128, 64, 128])
    f3 = lambda t: V(t.t[:].rearrange('p a b -> p (a b)'), t.v().pg)
    cvv = V(Kc.t[:, 16:144].unsqueeze(1).to_broadcast([128, 64, 128]), Kc.v().pg)
    halfpi = tk('halfpi', [128, 1]); K.memset(halfpi.v(), math.pi / 2)
    for d in range(2):
        K.tt(arg.v(), V(ph.t[:, d, :].unsqueeze(2).to_broadcast([128, 64, 128]), ph.v().pg), cvv, ALU.mult)
        range_reduce(B, f3(red), f3(arg), f3(tmp))
        K.act(tmp.v(), red.v(), AF.Sin, scale=sgn.v())
        K.dma(DR(B.s5rot[d, 1], 's5rot'), f3(tmp))
        K.act(red.v(), red.v(), AF.Abs)
        K.act(red.v(), red.v(), AF.Sin, scale=-1.0, bias=halfpi.v())
        K.dma(DR(B.s5rot[d, 0], 's5rot'), f3(red))
        yield
    if 's5prep' in B.dbg:
        for nm, src, shp, dt_ in (('w1', B.s5w1, [2 * 8 * 128, 1024], BF16), ('w3', B.s5w3, [2 * 8 * 128, 1024], BF16),
                                  ('w2', B.s5w2, [8 * 128, 1024], BF16), ('rot', B.s5rot, [4 * 128, 8192], F32)):
            o_ = B.dout('dbg_' + nm, shp, dt_)
            nd = len(src.shape)
            names = 'abcdefg'[:nd - 1]
            pat = ' '.join(names) + ' x -> (' + ' '.join(names) + ') x'
            K.dma(DR(o_.ap(), 'dbg_' + nm), DR(src.ap().rearrange(pat), {'w1': 's5w1', 'w3': 's5w3', 'w2': 's5w2', 'rot': 's5rot'}[nm]))
        B.dump('s5R', B.s5R)


def host_s5(inp, H):
    f = np.float32
    lre = inp['s5_lambda_re'][0]; lim = inp['s5_lambda_im'][0]; ldt = inp['s5_log_dt'][0]
    def layA(x):
        y = x.transpose(2, 0, 1)
        return np.concatenate([y, y], 0)
    ldtA = np.broadcast_to(ldt[None], (128, 2, 64))
    H['s5A'] = np.ascontiguousarray(np.stack([layA(lre), layA(lim), ldtA], 1)).astype(f)
    bre = inp['s5_b_re'][0].transpose(2, 0, 1, 3); bim = inp['s5_b_im'][0].transpose(2, 0, 1, 3)
    Q1 = np.concatenate([bre, bim], 0); Q2 = np.concatenate([bim, bre], 0)
    H['s5Bq'] = np.ascontiguousarray(np.stack([Q1, Q2], 1)).astype(f)
    cre = inp['s5_c_re'][0].transpose(3, 0, 1, 2); cim = inp['s5_c_im'][0].transpose(3, 0, 1, 2)
    C1 = np.concatenate([cre, cim], 0); C2 = np.concatenate([cim, cre], 0)
    H['s5C'] = np.ascontiguousarray(np.stack([C1, C2], 1)).astype(f)
    kv = np.arange(-7, 9, dtype=f); cv = np.arange(1, 129, dtype=f)
    dcol = np.tile(inp['s5_d'][0].reshape(64, 16).T, (8, 1))
    s_ = np.arange(128) // 16
    mF = (s_[:, None] <= s_[None, :]).astype(f); mB = (s_[:, None] >= s_[None, :]).astype(f)
    H['s5K'] = np.ascontiguousarray(np.concatenate([np.broadcast_to(kv, (128, 16)), np.broadcast_to(cv, (128, 128)), dcol, mF, mB], 1)).astype(f)


def load_small(B, t, dram, name):
    nd = len(t.shape)
    if nd > 2:
        pat = _flat(nd)
        B.K.dma(V(t.t[:].rearrange(pat), t.v().pg), DR(dram.ap(), name))
    else:
        B.K.dma(t.v(), DR(dram.ap(), name))


def phase_l0consts(B):
    cp = B.din('convp', [128, 8 * 31 + 8 * 4])
    B.convp = B.ct('convp', [128, 8 * 31 + 32])
    load_small(B, B.convp, cp, 'convp')
    B.wdw = lambda j: V(B.convp.t[:, 31 * j:31 * j + 31], B.convp.v().pg)
    B.cpar = lambda which, j: B.convp.v(slice(248 + 8 * which + j, 249 + 8 * which + j))
    B.din('wide', [128, 8 * 240])
    B.din('win', [24, 128, 2048]); B.din('wglu', [8, 128, 1024]); B.din('woab', [16, 128, 2048])


def mixer0(B, ps_, segs, halo=None, s5init=None, s5out=None, dirs=(0, 1), pre=False):
    K = B.K; nc = B.nc; X = ps_.X; H = ps_.H; NT = ps_.ntok
    win = B.ins['win']; wglu = B.ins['wglu']; woab = B.ins['woab']
    units = []
    if not pre:
        for j in range(8):
            units += [(DR(win[j], 'win'), 2048), (DR(win[8 + j], 'win'), 2048)]
    units += [(DR(win[16 + j], 'win'), 2048) for j in range(8)]
    if not pre:
        units += [(DR(wglu[m], 'wglu'), 1024) for m in range(8)]
        units += [(DR(woab[o], 'woab'), 2048) for o in range(16)]
    B.wplan(units)
    o = [BIG_OFF]
    def tk(name, shape, dt=F32):
        n = 2 if dt == BF16 else 4
        for s in shape[1:]: n *= s
        if n >= 512: o[0] = (o[0] + 511) // 512 * 512
        t = T(nc, K.name(name), shape, dt, o[0]); o[0] += (n + 31) // 32 * 32
        assert o[0] <= WR_OFF, (name, o[0] - BIG_OFF)
        return t
    US = tk('US', [128, 8, NT], BF16)
    o_s5 = o[0]
    segoff = []; tot = 0
    for (c0, L) in segs:
        segoff.append(tot); tot += L + 30
    UC = tk('UC', [128, 8, tot], BF16)
    YP = tk('YP', [128, 8, 512])
    Dg = tk('Dg', [128, 31, 128], BF16)
    Dgs = [Dg, T(nc, K.name('Dg2'), [128, 31, 128], BF16, H_OFF + 8 * NT * 2)]
    if not pre: K.memset(UC.v(), 0.0)

    def seg_parts(c0, n):
        out = []
        for si, (s0, L) in enumerate(segs):
            a = max(c0, s0); b = min(c0 + n, s0 + L)
            if a < b: out.append((si, a, b))
        return out
    for j in range(0 if pre else 8):
        slv = B.wnext()
        pvs = []
        for (c0, n) in ps_.tiles:
            cs = slice(c0, c0 + n); pv = B.nps(); pvs.append(pv)
            for kc in range(16):
                K.mm(pv.v(slice(0, n)), wv(slv, kc), H.v(kc, cs), start=(kc == 0), stop=(kc == 15))
        hps = []
        if halo is not None:
            for side in ('left', 'right'):
                hv = halo.get(side)
                if hv is None: continue
                pv = B.nps(); hps.append((side, hv, pv))
                for kc in range(16):
                    K.mm(pv.v(slice(0, 15)), wv(slv, kc), hv(kc), start=(kc == 0), stop=(kc == 15))
        slg = B.wnext()
        for pv, (c0, n) in zip(pvs, ps_.tiles):
            cs = slice(c0, c0 + n); pg = B.nps()
            for kc in range(16):
                K.mm(pg.v(slice(0, n)), wv(slg, kc), H.v(kc, cs), start=(kc == 0), stop=(kc == 15))
            sg = B.nscr()
            K.act(sg.v(slice(0, n)), pg.v(slice(0, n)), AF.Sigmoid)
            for (si, a, b) in seg_parts(c0, n):
                d0 = segoff[si] + 15 + (a - segs[si][0])
                K.tt(UC.v(j, slice(d0, d0 + (b - a))), pv.v(slice(a - c0, b - c0)), sg.v(slice(a - c0, b - c0)), ALU.mult)
        for (side, hv, pv) in hps:
            pg = B.nps()
            for kc in range(16):
                K.mm(pg.v(slice(0, 15)), wv(slg, kc), hv(kc), start=(kc == 0), stop=(kc == 15))
            sg = B.nscr()
            K.act(sg.v(slice(0, 15)), pg.v(slice(0, 15)), AF.Sigmoid)
            d0 = 0 if side == 'left' else tot - 15
            K.tt(UC.v(j, slice(d0, d0 + 15)), pv.v(slice(0, 15)), sg.v(slice(0, 15)), ALU.mult)
    if B.cut <= 1: return
    for j in range(8):
        sl = B.wnext()
        for ti, (c0, n) in enumerate(ps_.tiles):
            cs = slice(c0, c0 + n); ps = B.nps()
            for kc in range(16):
                K.mm(ps.v(slice(0, n)), wv(sl, kc), H.v(kc, cs), start=(kc == 0), stop=(kc == 15))
            if ti % 2 == 0: K.act(US.v(j, cs), ps.v(slice(0, n)), AF.Copy)
            else: K.copy(US.v(j, cs), ps.v(slice(0, n)))
    if B.cut <= 2: return
    for (c0, n) in ([] if pre else ps_.tiles):
        for j in range(8):
            Dg = Dgs[j % 2]
            if True:
                K.tt(Dg.v(), V(B.cstb.t[:, 1, :].unsqueeze(1).to_broadcast([128, 31, 128]), B.cstb.v().pg),
                     V(B.wdw(j).ap.unsqueeze(2).to_broadcast([128, 31, 128]), B.convp.v().pg), ALU.mult)
            for (si, a, b) in seg_parts(c0, n):
                ps = B.nps(); m = b - a
                base = segoff[si] + (a - segs[si][0])
                for tau in range(31):
                    K.mm(ps.v(slice(0, m)), Dg.v(tau), UC.v(j, slice(base + tau, base + tau + m)), start=(tau == 0), stop=(tau == 30))
                K.act(YP.v(j, slice(a - c0, b - c0)), ps.v(slice(0, m)), AF.Identity, bias=B.cpar(0, j))
        layer_norm(B, YP, 8, [(0, n)], lambda kc: B.cpar(1, kc), lambda kc: B.cpar(2, kc), B.eps_raw.v(),
                   act_func=AF.Silu, out=_ColShift(H, c0))
    B.dump_bf('yconv', H, 8)
    if B.cut <= 3: return
    o[0] = o_s5
    s5_core(B, ps_, US, segs, tk, s5init, s5out, dirs)
    B.dump_bf('ys', US, 8)
    if pre: return
    if B.cut <= 9: return
    for m in range(8):
        sl = B.wnext()
        for (c0, n) in ps_.tiles:
            cs = slice(c0, c0 + n); ps = B.nps()
            for kc in range(8):
                K.mm(ps.v(slice(0, n)), wv(sl, kc), US.v(kc, cs), start=(kc == 0), stop=(kc == 7))
            sg = B.nscr()
            K.act(sg.v(slice(0, n)), ps.v(slice(0, n)), AF.Sigmoid)
            K.tt(H.v(8 + m, cs), US.v(m, cs), sg.v(slice(0, n)), ALU.mult)
    for oc in range(16):
        sl = B.wnext()
        for (c0, n) in ps_.tiles:
            cs = slice(c0, c0 + n); ps = B.nps()
            for kc in range(16):
                K.mm(ps.v(slice(0, n)), wv(sl, kc), H.v(kc, cs), start=(kc == 0), stop=(kc == 15))
            K.stt(X.v(oc, cs), ps.v(slice(0, n)), mcol(B, 0, 2, oc, ps_.vec), X.v(oc, cs), ALU.mult, ALU.add)


class _ColShift:
    def __init__(self, t, c0): self.t = t; self.c0 = c0
    def v(self, kc, cs):
        return self.t.v(kc, slice(cs.start + self.c0, cs.stop + self.c0))


def s5_core(B, ps_, US, segs, tk, s5init, s5out, dirs):
    K = B.K; NT = ps_.ntok
    NS = len(segs); NCC = segs[0][1] // 8; NC = NT // 8
    assert NC == 128
    hoff = H_OFF + 8 * NT * 2
    ROTS = [T(B.nc, K.name('rot'), [128, 2, 8, 128], F32, hoff + 8192 * i) for i in range(2)]
    WIDE = tk('wide', [128, 8, 240], BF16)
    K.dma(V(WIDE.t[:].rearrange('p a b -> p (a b)'), WIDE.v().pg), DR(B.ins['wide'].ap(), 'wide'), eng='pool')
    W1S = [tk('W1s%d' % i, [128, 8, 128], BF16) for i in range(2)]; W1SS = [tk('W1ss%d' % i, [128, 8, 128], BF16) for i in range(2)]
    W3 = tk('W3s', [128, 2, 8, 128], BF16); W2 = tk('W2s', [128, 8, 128], BF16)
    Ush = tk('Ush', [128, 8, 128], BF16); Ysh = tk('Ysh', [128, 8, 128], BF16)
    Tt = tk('Tt', [128, 8, 128]); Ts = tk('Ts', [128, 8, 128]); Qt = tk('Qt', [128, 8, 128]); Qs = tk('Qs', [128, 8, 128])
    Rt = tk('Rt', [128, 8, 128]); Hbf = tk('Hbf', [128, 2, 8, 128], BF16)
    f3 = lambda t: V(t.t[:].rearrange('p a b -> p (a b)'), t.v().pg)
    def v4(t, gs=slice(None), rev=False, sl_c=slice(None)):
        a = t.t[:, gs].rearrange('p g (s c) -> p g s c', c=NCC)
        if rev: a = a[:, :, :, ::-1]
        return V(a[:, :, :, sl_c], t.v().pg)
    def p4(ps, rev=False):
        a = ps.t[:, :].rearrange('p (g s c) -> p g s c', g=4, c=NCC)
        if rev: a = a[:, :, :, ::-1]
        return V(a, ps.v().pg)
    def bc1(t, idx):
        a = t[idx]
        return a.unsqueeze(2).unsqueeze(3).to_broadcast([128, 8, NS, 1])
    its = [(j, d) for j in range(8) for d in dirs]
    def loads(k):
        j, d = its[k]
        K.dma(V(W1S[k % 2].t[:].rearrange('p a b -> p (a b)'), W1S[k % 2].v().pg), DR(B.s5w1[d, j], 's5w1'))
        K.dma(V(W1SS[k % 2].t[:].rearrange('p a b -> p (a b)'), W1SS[k % 2].v().pg), DR(B.s5w1s[d, j], 's5w1s'))
        K.dma(V(ROTS[k % 2].t[:].rearrange('p a b c -> p a (b c)'), ROTS[k % 2].v().pg),
              DR(B.s5rot[d, :, :, 1024 * j:1024 * j + 1024].rearrange('a p x -> p a x'), 's5rot'))
    loads(0)
    kk = -1
    for j in range(8):
        gsl = slice(8 * j, 8 * j + 8)
        K.dma(V(W2.t[:].rearrange('p a b -> p (a b)'), W2.v().pg), DR(B.s5w2[j], 's5w2'))
        if B.cut <= 4: return
        for half in range(2):
            ps = B.nps()
            for q in range(4):
                gl = 4 * half + q
                for s_ in range(8):
                    K.mm(ps.v(slice(128 * q, 128 * q + 128)), WIDE.v(gl, slice(112 - 16 * s_, 240 - 16 * s_)),
                         US.v(j, slice(s_, NT, 8)), start=(s_ == 0), stop=(s_ == 7))
            K.act(V(Ush.t[:, 4 * half:4 * half + 4, :].rearrange('p a b -> p (a b)'), Ush.v().pg), ps.v(), AF.Copy)
        if B.cut <= 5: return
        for d in dirs:
            kk += 1
            rev = (d == 1)
            W1 = W1S[kk % 2]; W1s = W1SS[kk % 2]; ROT = ROTS[kk % 2]
            def rot4(which, gs, sl_c=slice(0, NCC), ROT=ROT):
                a = ROT.t[:, which, gs, sl_c]
                return V(a.unsqueeze(2).to_broadcast([128, a.shape[1], NS, a.shape[-1]]), ROT.v().pg)
            K.dma(V(W3.t[:, d].rearrange('p a b -> p (a b)'), W3.v().pg), DR(B.s5w3[d, j], 's5w3'))
            pS = [B.nps(), B.nps()]; pW = [B.nps(), B.nps()]
            for gl in range(8):
                cs = slice(128 * (gl % 4), 128 * (gl % 4) + 128)
                K.mm(pS[gl // 4].v(cs), W1.v(gl), Ush.v(gl))
                K.mm(pW[gl // 4].v(cs), W1s.v(gl), Ush.v(gl))
            if kk + 1 < len(its): loads(kk + 1)
            if B.cut <= 6: return
            for h in range(2):
                gs = slice(4 * h, 4 * h + 4)
                cr = rot4(0, gs); ci = rot4(1, gs)
                K.tt(v4(Tt, gs), p4(pS[h], rev), cr, ALU.mult); K.tt(v4(Qt, gs), p4(pW[h], rev), ci, ALU.mult)
                K.tt(v4(Ts, gs), p4(pW[h], rev), cr, ALU.mult); K.tt(v4(Qs, gs), p4(pS[h], rev), ci, ALU.mult)
            K.tt(f3(Tt), f3(Tt), f3(Qt), ALU.add)
            K.tt(f3(Ts), f3(Ts), f3(Qs), ALU.subtract)
            Rv = V(B.s5R.t[:, d, gsl].unsqueeze(2).to_broadcast([128, 8, 128]), B.s5R.v().pg)
            K.copy(Rt.v(), Rv, eng='pool')
            K.memset(v4(Rt, sl_c=slice(0, 1)), 0.0, eng='pool')
            ini = s5init[d] if s5init is not None else None
            if ini is not None:
                for (tt_, it_) in ((Tt, ini[0]), (Ts, ini[1])):
                    iv = V(bc1(it_.t, (slice(None), gsl)), it_.v().pg)
                    rv = V(bc1(B.s5R.t, (slice(None), d, gsl)), B.s5R.v().pg)
                    K.tt(v4(Qt, sl_c=slice(0, 1)), iv, rv, ALU.mult)
                    K.tt(v4(tt_, sl_c=slice(0, 1)), v4(tt_, sl_c=slice(0, 1)), v4(Qt, sl_c=slice(0, 1)), ALU.add)
            K.scan(f3(Qt), f3(Rt), f3(Tt)); K.scan(f3(Qs), f3(Rt), f3(Ts))
            cr = rot4(0, slice(0, 8)); ci = rot4(1, slice(0, 8))
            K.tt(v4(Tt), v4(Qt), cr, ALU.mult); K.tt(v4(Ts), v4(Qs), ci, ALU.mult)
            K.tt(f3(Tt), f3(Tt), f3(Ts), ALU.subtract)
            if s5out is not None:
                lc = slice(NCC - 1, NCC)
                K.tt(v4(Ts, sl_c=lc), v4(Qs, sl_c=lc), rot4(0, slice(0, 8), lc), ALU.mult)
                K.tt(v4(Rt, sl_c=lc), v4(Qt, sl_c=lc), rot4(1, slice(0, 8), lc), ALU.mult)
                K.tt(v4(Ts, sl_c=lc), v4(Ts, sl_c=lc), v4(Rt, sl_c=lc), ALU.add)
                s5out(j, d, v4(Tt, sl_c=lc), v4(Ts, sl_c=lc))
            hb = Hbf.t[:, d].rearrange('p g (s c) -> p g s c', c=NCC)
            if rev: hb = hb[:, :, :, ::-1]
            K.act(V(hb[:, :, :, 1:NCC], Hbf.v().pg), v4(Tt, sl_c=slice(0, NCC - 1)), AF.Copy)
            if ini is not None:
                K.copy(V(hb[:, :, :, 0:1], Hbf.v().pg), V(bc1(ini[0].t, (slice(None), gsl)), ini[0].v().pg))
            else:
                K.memset(V(hb[:, :, :, 0:1], Hbf.v().pg), 0.0)
        if len(dirs) < 2:
            continue
        if B.cut <= 7: return
        for half in range(2):
            ps = B.nps()
            for q in range(4):
                gl = 4 * half + q
                cs = slice(128 * q, 128 * q + 128)
                K.mm(ps.v(cs), W3.v(0, gl), Hbf.v(0, gl), start=True, stop=False)
                K.mm(ps.v(cs), W3.v(1, gl), Hbf.v(1, gl), start=False, stop=False)
                K.mm(ps.v(cs), W2.v(gl), Ush.v(gl), start=False, stop=True)
            K.act(V(Ysh.t[:, 4 * half:4 * half + 4, :].rearrange('p a b -> p (a b)'), Ysh.v().pg), ps.v(), AF.Copy)
        if B.cut <= 8: return
        for hb_ in range(2):
            ps = B.nps()
            for q in range(4):
                s_ = 4 * hb_ + q
                for gl in range(8):
                    K.mm(ps.v(slice(128 * q, 128 * q + 128)), WIDE.v(s_, slice(112 - 16 * gl, 240 - 16 * gl)), Ysh.v(gl),
                         start=(gl == 0), stop=(gl == 7))
            dst = V(US.t[:, j, :].rearrange('p (c s) -> p c s', s=8)[:, :, 4 * hb_:4 * hb_ + 4], US.v(j).pg)
            src = V(ps.t[:, :].rearrange('p (s c) -> p c s', s=4), ps.v().pg)
            K.act(dst, src, AF.Gelu)


def host_l0(inp, H):
    f = np.float32
    wd = inp['w_dw'][0].reshape(31, 8, 128).transpose(2, 1, 0).reshape(128, 248)
    par = np.stack([inp['b_dw'][0], inp['conv_ln_g'][0], inp['conv_ln_b'][0], inp['s5_d'][0]], 0).reshape(4, 8, 128).transpose(2, 0, 1).reshape(128, 32)
    H['convp'] = np.ascontiguousarray(np.concatenate([wd, par], 1)).astype(f)
    wide = np.zeros((128, 8, 240), f)
    for a_ in range(8):
        for p in range(16):
            wide[16 * a_ + p, a_, 112 + p] = 1.0
    H['wide'] = np.ascontiguousarray(wide).reshape(128, 1920)
    H['win'] = units_fm(inp['w_in_ab'][0]); H['wglu'] = units_fm(inp['w_glu'][0]); H['woab'] = units_fm(inp['w_out_ab'][0])


def phase_l1consts(B):
    K = B.K
    ap_ = B.din('attp', [128, 257])
    B.attp = B.ct('attp', [128, 257])
    K.dma(B.attp.v(), DR(ap_.ap(), 'attp'))
    B.subg = B.ct('subg', [128, 1]); B.neglam = B.ct('neglam', [128, 1])
    K.ts(B.subg.v(), B.attp.v(slice(0, 1)), 1.0 - LAM_INIT, ALU.mult)
    pr = B.ct('lampr', [128, 2, 64]); sm = B.ct('lamsm', [128, 2])
    lv = B.attp.t[:, 1:257].rearrange('p (a b) -> p a b', b=64)
    K.tt(pr.v(), V(lv[:, 0::2, :], B.attp.v().pg), V(lv[:, 1::2, :], B.attp.v().pg), ALU.mult)
    K.reduce(sm.v(), pr.v())
    K.act(sm.v(), sm.v(), AF.Exp)
    K.tt(B.neglam.v(), sm.v(slice(1, 2)), sm.v(slice(0, 1)), ALU.subtract)
    K.ts(B.neglam.v(), B.neglam.v(), -LAM_INIT, ALU.add)
    B.din('wqk', [32, 128, 2048]); B.din('wv', [8, 128, 4096]); B.din('woc', [16, 128, 2048])


def attn_core1(B, n, O0, O1, S0, S1, sq, stg):
    K = B.K
    sn = slice(0, n)
    K.act(stg.v(0, sn), S0, AF.Ln)
    K.act(stg.v(2, sn), S1, AF.Ln)
    K.copy(stg.v(1, sn), O0); K.copy(stg.v(3, sn), O1)
    K.act(stg.v(0, sn), stg.v(0, sn), AF.Exp, scale=-1.0)
    K.act(stg.v(2, sn), stg.v(2, sn), AF.Exp, scale=-1.0)
    K.tt(stg.v(1, sn), stg.v(1, sn), stg.v(0, sn), ALU.mult)
    K.tt(stg.v(3, sn), stg.v(3, sn), stg.v(2, sn), ALU.mult)
    K.stt(sq['o'], stg.v(3, sn), B.neglam.v(), stg.v(1, sn), ALU.mult, ALU.add)
    K.tt(sq['s'], sq['o'], sq['o'], ALU.mult)


def attn_core2(B, n, sq, ms, dst):
    K = B.K
    sn = slice(0, n)
    K.mm(ms.v(sn), B.ones_f(), sq['s'])
    K.act(sq['s'], ms.v(sn), AF.Ln, scale=1.0 / 128, bias=B.eps_raw.v())
    K.act(sq['s'], sq['s'], AF.Exp, scale=-0.5)
    K.stt(dst, sq['o'], B.subg.v(), sq['s'], ALU.mult, ALU.mult)


def attn_p(B, ps_, okT, ov):
    K = B.K; nc = B.nc; X = ps_.X; H = ps_.H; NT = ps_.ntok
    wqk = B.ins['wqk']; wvv = B.ins['wv']; woc = B.ins['woc']
    units = []
    for j in range(8):
        units += [(DR(wqk[j], 'wqk'), 2048), (DR(wqk[8 + j], 'wqk'), 2048), (DR(wqk[16 + j], 'wqk'), 2048),
                  (DR(wqk[24 + j], 'wqk'), 2048), (DR(wvv[j], 'wv'), 4096)]
    units += [(DR(woc[o_], 'woc'), 2048) for o_ in range(16)]
    B.wplan(units)
    o = [BIG_OFF]
    def tk(name, shape, dt=F32):
        nb = 2 if dt == BF16 else 4
        for s in shape[1:]: nb *= s
        if nb >= 512: o[0] = (o[0] + 511) // 512 * 512
        t = T(nc, K.name(name), shape, dt, o[0]); o[0] += (nb + 31) // 32 * 32
        assert o[0] <= WR_OFF, (name, o[0] - BIG_OFF)
        return t
    ATT = tk('ATT', [128, 16, NT], BF16)
    QM = tk('QM', [128, 2, 2, NT], BF16)
    KB = tk('KB', [128, 2, NT], BF16)
    Vb = tk('Vb', [128, NT // 128, 256], BF16)
    PT = tk('PT', [128, 2, 2, 512], BF16)
    STG = tk('STG', [128, 4, 256])
    K.memset(QM.v(), 0.0)
    sqset = lambda i, n: {'o': B.scr[3 + 2 * i].v(slice(0, n)), 's': B.scr[4 + 2 * i].v(slice(0, n))}
    for j in range(8):
        for m in range(2):
            sl = B.wnext()
            for (c0, n) in ps_.tiles:
                cs = slice(c0, c0 + n); ps = B.nps()
                for kc in range(16):
                    K.mm(ps.v(slice(0, n)), wv(sl, kc), H.v(kc, cs), start=(kc == 0), stop=(kc == 15))
                K.act(QM.v(m, 0, cs, p=slice(0, 64)), ps.v(slice(0, n), p=slice(0, 64)), AF.Copy)
                K.copy(QM.v(m, 1, cs, p=slice(64, 128)), ps.v(slice(0, n), p=slice(64, 128)))
        if B.cut <= 1: continue
        for m in range(2):
            sl = B.wnext()
            for (c0, n) in ps_.tiles:
                cs = slice(c0, c0 + n); ps = B.nps()
                for kc in range(16):
                    K.mm(ps.v(slice(0, n)), wv(sl, kc), H.v(kc, cs), start=(kc == 0), stop=(kc == 15))
                K.act(KB.v(m, cs), ps.v(slice(0, n)), AF.Copy)
                kf = B.nscr()
                K.copy(kf.v(slice(0, n)), ps.v(slice(0, n)))
                K.dma(DR(okT[m, 2 * j:2 * j + 2].rearrange('h d t -> (h d) t')[:, c0:c0 + n], 'okT'), kf.v(slice(0, n)))
        if B.cut <= 2: continue
        sl = B.wnext()
        for tt_ in range(NT // 128):
            ps = B.nps()
            for kc in range(16):
                K.mm(ps.v(slice(0, 256)), H.v(kc, slice(128 * tt_, 128 * tt_ + 128)), wv(sl, kc, 256), start=(kc == 0), stop=(kc == 15))
            K.act(Vb.v(tt_), ps.v(slice(0, 256)), AF.Copy)
            vf = B.nscr()
            K.copy(vf.v(slice(0, 256)), ps.v(slice(0, 256)))
            K.dma(DR(ov[128 * tt_:128 * tt_ + 128, 2 * j:2 * j + 2, :].rearrange('t h d -> t (h d)'), 'ov'), vf.v(slice(0, 256)))
        units_l = [(sq_, h2) for sq_ in range(NT // 256) for h2 in range(2)]
        nu = len(units_l)

        def partA(u):
            sq_, h2 = units_l[u]
            qc = slice(256 * sq_, 256 * sq_ + 256)
            for m in range(2):
                pss = B.ps[2 * (u % 2) + m]
                for kt in range(2):
                    K.mm(pss.v(slice(256 * kt, 256 * kt + 256)), KB.v(m, slice(256 * sq_ + 128 * kt, 256 * sq_ + 128 * kt + 128)),
                         QM.v(m, h2, qc))
                K.act(PT.v(u % 2, m), pss.v(), AF.Exp, scale=0.125)

        def partB(u):
            sq_, h2 = units_l[u]
            psS = B.ps[4]; psO = B.ps[5]
            for m in range(2):
                for kt in range(2):
                    K.mm(psS.v(slice(256 * m, 256 * m + 256)), B.ones_b(), PT.v(u % 2, m, slice(256 * kt, 256 * kt + 256)),
                         start=(kt == 0), stop=(kt == 1))
                for kt in range(2):
                    K.mm(psO.v(slice(256 * m, 256 * m + 256)), Vb.v(2 * sq_ + kt, slice(128 * h2, 128 * h2 + 128)),
                         PT.v(u % 2, m, slice(256 * kt, 256 * kt + 256)), start=(kt == 0), stop=(kt == 1))
            attn_core1(B, 256, psO.v(slice(0, 256)), psO.v(slice(256, 512)), psS.v(slice(0, 256)), psS.v(slice(256, 512)), sqset(u % 2, 256), STG)

        def partC(u):
            sq_, h2 = units_l[u]
            qc = slice(256 * sq_, 256 * sq_ + 256)
            attn_core2(B, 256, sqset(u % 2, 256), B.ps[6], ATT.v(2 * j + h2, qc))
        partA(0)
        for u in range(nu):
            if u + 1 < nu: partA(u + 1)
            partB(u)
            if u >= 1: partC(u - 1)
        partC(nu - 1)
    B.dump_bf('att', ATT, 16)
    if B.cut <= 5: return
    out_proj(B, ps_, ATT, 1)


def out_proj(B, ps_, A, l):
    K = B.K; X = ps_.X
    for oc in range(16):
        sl = B.wnext()
        for (c0, n) in ps_.tiles:
            cs = slice(c0, c0 + n); ps = B.nps()
            for kc in range(16):
                K.mm(ps.v(slice(0, n)), wv(sl, kc), A.v(kc, cs), start=(kc == 0), stop=(kc == 15))
            K.stt(X.v(oc, cs), ps.v(slice(0, n)), mcol(B, l, 2, oc, ps_.vec), X.v(oc, cs), ALU.mult, ALU.add)


def host_l1(inp, H):
    f = np.float32
    lam = np.concatenate([inp['lam_q1'][0], inp['lam_k1'][0], inp['lam_q2'][0], inp['lam_k2'][0]])
    H['attp'] = np.ascontiguousarray(np.concatenate([inp['subln_g'][0][:, None], np.broadcast_to(lam, (128, 256))], 1)).astype(f)
    wq = inp['w_qkv'][0]
    H['wqk'] = units_fm(wq[:, :4096]); H['wv'] = units_fm(wq[:, 4096:], 256); H['woc'] = units_fm(inp['w_out_c'][0])


def rope_tables(B, tk, pos_dram, name, c0, n):
    K = B.K
    C = tk('ropeC', [128, n]); S = tk('ropeS', [128, n]); A = tk('ropeA', [128, n]); Tm = tk('ropeT', [128, n])
    K.dma(A.v(), DR(pos_dram[:, c0:c0 + n], name))
    K.ts(A.v(), A.v(), B.ropec.v(slice(0, 1)), ALU.mult)
    range_reduce(B, C.v(), A.v(), Tm.v(), shift=math.pi / 2)
    K.act(C.v(), C.v(), AF.Sin)
    range_reduce(B, S.v(), A.v(), Tm.v())
    K.act(S.v(), S.v(), AF.Sin)
    K.ts(S.v(), S.v(), B.ropec.v(slice(1, 2)), ALU.mult)
    return C, S


def rope_apply(B, ps, n, Cv, Sv, dst):
    K = B.K
    kf = B.nscr(); t1 = B.nscr()
    K.copy(kf.v(slice(0, n)), ps)
    pw = B.ps[7] if B.pin else B.nps()
    K.mm(pw.v(slice(0, n)), B.perm_f(), kf.v(slice(0, n)))
    K.tt(t1.v(slice(0, n)), kf.v(slice(0, n)), Cv, ALU.mult)
    K.tt(kf.v(slice(0, n)), pw.v(slice(0, n)), Sv, ALU.mult)
    K.tt(dst, t1.v(slice(0, n)), kf.v(slice(0, n)), ALU.add)


def phase_sconsts(B):
    K = B.K
    rc = B.din('ropec', [128, 2]); B.ropec = B.ct('ropec', [128, 2]); K.dma(B.ropec.v(), DR(rc.ap(), 'ropec'))
    ms = B.din('msel', [128, 4]); B.msel = B.ct('msel', [128, 4]); K.dma(B.msel.v(), DR(ms.ap(), 'msel'))
    si = B.din('sinit', [128, 2 * 2 * 64]); B.sinit = B.ct('sinit', [128, 2, 2, 64])
    load_small(B, B.sinit, si, 'sinit')
    B.hmid = B.ct('hmid', [128, 2, 2, 64])
    B.din('posk', [128, 2048]); B.din('posq', [128, 512])
    B.din('xs', [128, 16, 2048]); B.din('ckT', [2, 16, 64, 512]); B.din('cv', [16, 512, 128])
    B.kt_scr = B.dscr('kt_scr', [32, 128, 2048], BF16)
    B.v_scr = B.dscr('v_scr', [2048, 2048], BF16)
    B.x1_scr = B.dscr('x1_scr', [128, 16, 2048], F32)


class _Sub2:
    def __init__(self, t, idx): self.t = t.t[(slice(None),) + idx]; self._t = t
    def v(self): return self._t.v()


def s5_capture(B, d):
    def cb(j, dd, hu, husw):
        if dd != d: return
        K = B.K
        for w, src in ((0, hu), (1, husw)):
            dst = V(B.hmid.t[:, d, w, 8 * j:8 * j + 8].unsqueeze(2).unsqueeze(3), B.hmid.v().pg)
            K.copy(dst, src)
    return cb


def sample_l0_pass(B, half):
    K = B.K; nc = B.nc
    xs = B.ins['xs']
    ps_ = Pass(B, 'S%d' % half, 1024, 1)
    t0 = 1024 * half
    load_x(B, ps_, xs, 'xs', c0=t0)
    modulate(B, ps_, 0, 0)
    hx = T(nc, K.name('hx'), [128, 16, 16], F32, SCR_OFF + 2048 * 4)
    hh = T(nc, K.name('hh'), [128, 16, 16], BF16, SCR_OFF + 2048 * 5)
    hc0 = 1024 if half == 0 else 1009
    K.dma(hx.v(slice(None), slice(0, 15)), DR(xs[:, :, hc0:hc0 + 15], 'xs'))
    for kc in range(16):
        K.ts(hh.v(kc, slice(0, 15)), hx.v(kc, slice(0, 15)), mcol(B, 0, 1, kc, 1), ALU.mult, mcol(B, 0, 0, kc, 1), ALU.add)
    hv = lambda kc: hh.v(kc, slice(0, 15))
    halo = {'right': hv} if half == 0 else {'left': hv}
    ini_host = lambda d: (_Sub2(B.sinit, (0, d)), _Sub2(B.sinit, (1, d)))
    ini_mid = lambda d: (_Sub2(B.hmid, (d, 0)), _Sub2(B.hmid, (d, 1)))
    if half == 0:
        s5init = {0: ini_host(0), 1: ini_mid(1)}; s5out = s5_capture(B, 0)
    else:
        s5init = {0: ini_mid(0), 1: ini_host(1)}; s5out = None
    mixer0(B, ps_, [(0, 1024)], halo=halo, s5init=s5init, s5out=s5out)
    post_ln(B, ps_, 0, 0)
    modulate(B, ps_, 0, 1)
    ffn(B, ps_, 0)
    post_ln(B, ps_, 0, 1)
    for kc in range(16):
        K.dma(DR(B.x1_scr[:, kc, t0:t0 + 1024], 'x1_scr'), ps_.X.v(kc))
    modulate(B, ps_, 1, 0)
    wqk = B.ins['wqk']; wvv = B.ins['wv']
    units = []
    for j in range(8):
        units += [(DR(wqk[16 + j], 'wqk'), 2048), (DR(wqk[24 + j], 'wqk'), 2048), (DR(wvv[j], 'wv'), 4096)]
    B.wplan(units)
    o = [BIG_OFF]
    def tk(name, shape, dt=F32):
        nb = 2 if dt == BF16 else 4
        for s_ in shape[1:]: nb *= s_
        if nb >= 512: o[0] = (o[0] + 511) // 512 * 512
        t = T(nc, K.name(name), shape, dt, o[0]); o[0] += (nb + 31) // 32 * 32
        assert o[0] <= WR_OFF, (name, o[0] - BIG_OFF)
        return t
    C, S = rope_tables(B, tk, B.ins['posk'], 'posk', t0, 1024)
    KR = tk('KR', [128, 2, 512], BF16); VS = tk('VS', [128, 2, 256], BF16)
    H = ps_.H
    cnt = 0
    for j in range(8):
        for m in range(2):
            sl = B.wnext()
            for (c0, n) in ps_.tiles:
                cs = slice(c0, c0 + n); ps = B.nps()
                for kc in range(16):
                    K.mm(ps.v(slice(0, n)), wv(sl, kc), H.v(kc, cs), start=(kc == 0), stop=(kc == 15))
                kr = KR.v(cnt % 2); cnt += 1
                rope_apply(B, ps.v(slice(0, n)), n, C.v(cs), S.v(cs), kr)
                K.dma(DR(B.kt_scr[8 * m + j, :, t0 + c0:t0 + c0 + n], 'kt_scr'), kr)
        sl = B.wnext()
        for tt_ in range(8):
            ps = B.nps()
            for kc in range(16):
                K.mm(ps.v(slice(0, 256)), H.v(kc, slice(128 * tt_, 128 * tt_ + 128)), wv(sl, kc, 256), start=(kc == 0), stop=(kc == 15))
            vs = VS.v(tt_ % 2)
            K.act(vs, ps.v(slice(0, 256)), AF.Copy)
            K.dma(DR(B.v_scr[t0 + 128 * tt_:t0 + 128 * tt_ + 128, 256 * j:256 * j + 256], 'v_scr'), vs)


def s5pre_pass(B):
    K = B.K
    ps_ = Pass(B, 'S5PRE', 1024, 1)
    load_x(B, ps_, B.ins['xs'], 'xs', c0=1024)
    modulate(B, ps_, 0, 0)
    s5init = {0: None, 1: (_Sub2(B.sinit, (0, 1)), _Sub2(B.sinit, (1, 1)))}
    mixer0(B, ps_, [(0, 1024)], s5init=s5init, s5out=s5_capture(B, 1), dirs=(1,), pre=True)


def sq_pass(B, ys_out):
    K = B.K; nc = B.nc
    ps_ = Pass(B, 'SQ', 512, 1)
    X = ps_.X; H = ps_.H
    for kc in range(16):
        for blk in range(4):
            st = B.nscr()
            K.dma(st.v(), DR(B.x1_scr[:, kc, 512 * blk:512 * blk + 512], 'x1_scr'))
            if blk == 0:
                K.ts(X.v(kc), st.v(), B.msel.v(slice(0, 1)), ALU.mult)
            else:
                K.stt(X.v(kc), st.v(), B.msel.v(slice(blk, blk + 1)), X.v(kc), ALU.mult, ALU.add)
    modulate(B, ps_, 1, 0)
    wqk = B.ins['wqk']; woc = B.ins['woc']
    units = []
    for j in range(8):
        units += [(DR(wqk[j], 'wqk'), 2048), (DR(wqk[8 + j], 'wqk'), 2048)]
    units += [(DR(woc[o_], 'woc'), 2048) for o_ in range(16)]
    B.wplan(units)
    o = [BIG_OFF]
    def tk(name, shape, dt=F32):
        nb = 2 if dt == BF16 else 4
        for s_ in shape[1:]: nb *= s_
        if nb >= 512: o[0] = (o[0] + 511) // 512 * 512
        t = T(nc, K.name(name), shape, dt, o[0]); o[0] += (nb + 31) // 32 * 32
        assert o[0] <= WR_OFF, (name, o[0] - BIG_OFF)
        return t
    ATT = tk('ATTq', [128, 16, 512], BF16)
    QM = tk('QMq', [128, 2, 2, 512], BF16)
    KA = tk('KA', [128, 2, 2560], BF16)
    VA = tk('VA', [128, 20, 256], BF16)
    PT = tk('PTq', [128, 4, 512], BF16)
    STG = tk('STGq', [128, 4, 512])
    C, S = rope_tables(B, tk, B.ins['posq'], 'posq', 0, 512)
    ckT = B.ins['ckT']; cv = B.ins['cv']
    K.memset(QM.v(), 0.0)
    B.pin = True
    pending = [None]
    sqset = lambda i, n: {'o': B.scr[3 + 2 * i].v(slice(0, n)), 's': B.scr[4 + 2 * i].v(slice(0, n))}
    rot = [0]
    def rps():
        p = B.ps[4 + rot[0] % 2]; rot[0] += 1
        return p
    cs = slice(0, 512)
    for j in range(8):
        for m in range(2):
            K.dma(KA.v(m, slice(0, 2048)), DR(B.kt_scr[8 * m + j], 'kt_scr'))
            K.dma(KA.v(m, slice(2048, 2560)), DR(ckT[m, 2 * j:2 * j + 2].rearrange('h d t -> (h d) t'), 'ckT'), eng='pool')
        K.dma(V(VA.t[:, 0:16, :], VA.v().pg), DR(B.v_scr[:, 256 * j:256 * j + 256].rearrange('(k p) f -> p k f', p=128), 'v_scr'))
        for h2 in range(2):
            K.dma(V(VA.t[:, 16:20, 128 * h2:128 * h2 + 128], VA.v().pg),
                  DR(cv[2 * j + h2].rearrange('(k p) d -> p k d', p=128), 'cv'), eng='pool')
        for m in range(2):
            sl = B.wnext()
            ps = rps()
            for kc in range(16):
                K.mm(ps.v(), wv(sl, kc), H.v(kc, cs), start=(kc == 0), stop=(kc == 15))
            qr = PT.v(0)
            rope_apply(B, ps.v(), 512, C.v(), S.v(), qr)
            K.copy(QM.v(m, 0, p=slice(0, 64)), V(PT.t[0:64, 0, :], PT.v(0).pg))
            K.copy(QM.v(m, 1, p=slice(64, 128)), V(PT.t[64:128, 0, :], PT.v(0).pg))
        for h2 in range(2):
            steps = [(kt, m) for kt in range(20) for m in range(2)]
            def score(i):
                kt, m = steps[i]
                K.mm(B.ps[4 + i % 3].v(), KA.v(m, slice(128 * kt, 128 * kt + 128)), QM.v(m, h2))
            score(0); score(1)
            for i, (kt, m) in enumerate(steps):
                if i + 2 < len(steps): score(i + 2)
                if i == 26 and pending[0] is not None:
                    attn_core2(B, 512, *pending[0]); pending[0] = None
                pt = PT.v(i % 4)
                K.act(pt, B.ps[4 + i % 3].v(), AF.Exp, scale=0.125)
                K.mm(B.ps[m].v(), B.ones_b(), pt, start=(kt == 0), stop=(kt == 19))
                K.mm(B.ps[2 + m].v(), VA.v(kt, slice(128 * h2, 128 * h2 + 128)), pt, start=(kt == 0), stop=(kt == 19))
            hid = (2 * j + h2) % 2
            attn_core1(B, 512, B.ps[2].v(), B.ps[3].v(), B.ps[0].v(), B.ps[1].v(), sqset(hid, 512), STG)
            pending[0] = (sqset(hid, 512), B.ps[7], ATT.v(2 * j + h2))
    if pending[0] is not None:
        attn_core2(B, 512, *pending[0]); pending[0] = None
    B.pin = False
    out_proj(B, ps_, ATT, 1)
    post_ln(B, ps_, 1, 0)
    modulate(B, ps_, 1, 1)
    ffn(B, ps_, 1)
    post_ln(B, ps_, 1, 1)
    store_x(B, ps_, ys_out, 'ys')


def prompt_pass(B, yp_out, ost, okT, ov):
    K = B.K
    ps_ = Pass(B, 'P', 1024, 0)
    load_x(B, ps_, B.ins['xp'], 'xp')
    modulate(B, ps_, 0, 0)
    def s5out(j, d, hu, husw):
        s = B.nscr()
        K.copy(V(s.t[:, 0:32].rearrange('p (g s c) -> p g s c', g=8, c=1), s.v().pg), hu)
        K.dma(DR(ost[d, :, j, :], 'ost'), s.v(slice(0, 32)))
    mixer0(B, ps_, [(256 * i, 256) for i in range(4)], s5out=s5out)
    post_ln(B, ps_, 0, 0)
    modulate(B, ps_, 0, 1)
    ffn(B, ps_, 0)
    post_ln(B, ps_, 0, 1)
    modulate(B, ps_, 1, 0)
    attn_p(B, ps_, okT, ov)
    post_ln(B, ps_, 1, 0)
    modulate(B, ps_, 1, 1)
    ffn(B, ps_, 1)
    post_ln(B, ps_, 1, 1)
    store_x(B, ps_, yp_out, 'yp')


def build_full(B, parts=('p', 's')):
    phase_consts(B)
    g1 = phase_mod_gen(B); g2 = phase_s5prep_gen(B)
    d1 = d2 = False
    while not (d1 and d2):
        for _ in range(5):
            if not d1:
                try: next(g1)
                except StopIteration: d1 = True
        if not d2:
            try: next(g2)
            except StopIteration: d2 = True
    phase_l0consts(B)
    phase_l1consts(B)
    B.din('wff1', [2, 64, 128, 2048]); B.din('wff2', [2, 2, 16, 128, 4096])
    if 'p' in parts:
        B.din('xp', [128, 16, 1024])
        yp = B.dout('yp', [128, 16, 1024]); ost = B.dout('ost', [2, 128, 8, 32])
        okT = B.dout('okT', [2, 16, 64, 1024]); ov = B.dout('ov', [1024, 16, 128])
        prompt_pass(B, yp, ost, okT, ov)
    if 's' in parts:
        phase_sconsts(B)
        ys = B.dout('ys', [128, 16, 512])
        s5pre_pass(B)
        sample_l0_pass(B, 0)
        sample_l0_pass(B, 1)
        sq_pass(B, ys)


def host_sample(inp, core, C):
    f = np.float32
    b = core // 4; q = core % 4
    C['xs'] = fm_tokens(inp['x_sample'][b])
    sre = inp['state_s5_re'][b, 0]; sim = inp['state_s5_im'][b, 0]
    plain = np.concatenate([sre.transpose(2, 0, 1), sim.transpose(2, 0, 1)], 0)
    swp = np.concatenate([sim.transpose(2, 0, 1), sre.transpose(2, 0, 1)], 0)
    C['sinit'] = np.ascontiguousarray(np.stack([plain, swp], 1)).reshape(128, 256).astype(f)
    C['ckT'] = np.ascontiguousarray(inp['cache_k'][b, 0].transpose(0, 1, 3, 2))
    C['cv'] = np.ascontiguousarray(inp['cache_v'][b, 0])
    ms = np.zeros((128, 4), f); ms[:, q] = 1.0
    C['msel'] = ms
    t = np.arange(2048)
    r = np.arange(128); d = r % 64
    pos = np.where((d < 32)[:, None], (t // 64)[None, :], (t % 64)[None, :]).astype(f)
    C['posk'] = np.ascontiguousarray(pos)
    C['posq'] = np.ascontiguousarray(pos[:, 512 * q:512 * q + 512])
    freq = (10000.0 ** (-(d % 16) / 16.0)).astype(f)
    sign = np.where((d % 32) < 16, -1.0, 1.0).astype(f)
    C['ropec'] = np.ascontiguousarray(np.stack([freq, sign], 1))


_CACHE = {}


def _get_builder():
    if 'B' not in _CACHE:
        B = Builder()
        st = ExitStack()
        build_full(B)
        B.K.P.finalize(st)
        _CACHE['B'] = B; _CACHE['st'] = st
    return _CACHE['B']


def kernel(**inputs):
    inp = {k: np.asarray(v) for k, v in inputs.items()}
    B = _get_builder()
    Hc = host_all(inp)
    maps = []
    for c in range(8):
        m = dict(Hc); m.update(host_core(inp, c))
        maps.append({k: np.ascontiguousarray(v, dtype=np.float32) for k, v in m.items() if k in B.ins})
    res = run_bass_kernel_spmd(B.nc, maps, core_ids=list(range(8)))
    f = np.float32
    y_p = np.zeros((32, 256, 2048), f); y_s = np.zeros((2, 2048, 2048), f)
    st_re = np.zeros((32, 1, 2, 64, 64), f); st_im = np.zeros((32, 1, 2, 64, 64), f)
    nk = np.zeros((32, 1, 2, 16, 256, 64), f); nv = np.zeros((32, 1, 16, 256, 128), f)
    for c in range(8):
        r = res.results[c]
        b = c // 4; q = c % 4
        yp = np.asarray(r['yp'], f).transpose(2, 1, 0).reshape(1024, 2048)
        y_p[4 * c:4 * c + 4] = yp.reshape(4, 256, 2048)
        ys = np.asarray(r['ys'], f).transpose(2, 1, 0).reshape(512, 2048)
        y_s[b, 512 * q:512 * q + 512] = ys
        ost = np.asarray(r['ost'], f)
        st = ost.reshape(2, 2, 64, 8, 8, 4).transpose(1, 5, 0, 3, 4, 2).reshape(2, 4, 2, 64, 64)
        st_re[4 * c:4 * c + 4, 0] = st[0]; st_im[4 * c:4 * c + 4, 0] = st[1]
        okT = np.asarray(r['okT'], f)
        nk[4 * c:4 * c + 4, 0] = okT.reshape(2, 16, 64, 4, 256).transpose(3, 0, 1, 4, 2)
        ov = np.asarray(r['ov'], f)
        nv[4 * c:4 * c + 4, 0] = ov.reshape(4, 256, 16, 128).transpose(0, 2, 1, 3)
    return (y_p, y_s, st_re, st_im, nk, nv)
```
